# Optimizing a Trainium2 kernel written in Bass

```python
import math
import jax
import jax.numpy as jnp
from jax import lax
import numpy as np

D_MODEL = 2048
BATCH = 8
SEQ = 4096
DEPTH = 4

CHUNK = 64
N_MIXERS = 4
D_MIX = D_MODEL
D_GROUP = D_MIX // N_MIXERS

HG_DIM = 128
HG_HEADS = D_GROUP // HG_DIM

S5_CH = 16
S5_GROUPS = D_GROUP // S5_CH
S5_STATE = 64
S5_DT_MIN = 1e-3
S5_DT_MAX = 1e-1

SB_DIM = 128
SB_HEADS = D_GROUP // SB_DIM
Q_BLOCK = 128

RW_DIM = 64
RW_HEADS = D_GROUP // RW_DIM
RW_W_RANK = 64
RW_A_RANK = 64
RW_G_RANK = 128
RW_GN_EPS = 64e-5
RW_SIZES = (D_GROUP, D_GROUP, D_GROUP, RW_W_RANK, RW_A_RANK, RW_G_RANK)
RW_SPLITS = tuple(int(s) for s in np.cumsum(RW_SIZES)[:-1])
RW_WIDTH = sum(RW_SIZES)
RW_OFFSET = 8 * D_GROUP
N_IN = RW_OFFSET + RW_WIDTH

D_FF = 256 * math.ceil(8 * D_MODEL / 3 / 256)
DEEPNORM_ALPHA = (2 * DEPTH) ** 0.25
DEEPNORM_BETA = (8 * DEPTH) ** -0.25
LN_EPS = 1e-5
RMS_EPS = 1e-6

kernel_name = 'hybrid_stream_encoder_hgrn2_s5_sb_rwkv7'


def _layer_norm(x, g, b):
    xf = x.astype(jnp.float32)
    mu = jnp.mean(xf, -1, keepdims=True)
    var = jnp.mean(jnp.square(xf - mu), -1, keepdims=True)
    y = (xf - mu) * lax.rsqrt(var + LN_EPS)
    return (y * g.astype(jnp.float32) + b.astype(jnp.float32)).astype(x.dtype)


def _rms_norm(xf):
    return xf * lax.rsqrt(jnp.mean(xf * xf, -1, keepdims=True) + RMS_EPS)


def _token_shift(x, mu):
    prev = jnp.pad(x, ((0, 0), (1, 0), (0, 0)))[:, :-1]
    return x + mu * (prev - x)


def _hgrn2(q, f_logit, i, g, lb, norm_g):
    B, S, _ = q.shape
    f32 = jnp.float32
    n_chunks = S // CHUNK
    lbf = lb.astype(f32)
    log_f = jnp.logaddexp(jnp.log(jnp.maximum(lbf, 0.0)),
                          jnp.log1p(-lbf) + jax.nn.log_sigmoid(f_logit.astype(f32)))
    k = -jnp.expm1(log_f)
    qf = jax.nn.silu(q.astype(f32))
    v = i.astype(f32)

    def to_chunks(t):
        return t.reshape(B, n_chunks, CHUNK, HG_HEADS, HG_DIM).transpose(1, 0, 3, 2, 4)

    causal = jnp.tril(jnp.ones((CHUNK, CHUNK), bool))[:, :, None]

    def step(state, inp):
        qb, kb, vb, lfb = inp
        bcum = jnp.cumsum(lfb, axis=-2)
        diff = bcum[..., :, None, :] - bcum[..., None, :, :]
        decay = jnp.exp(jnp.where(causal, diff, -jnp.inf))
        scores = jnp.einsum('bhtk,bhsk,bhtsk->bhts', qb, kb, decay)
        o = (jnp.einsum('bhts,bhsv->bhtv', scores, vb)
             + jnp.einsum('bhtk,bhkv->bhtv', qb * jnp.exp(bcum), state))
        b_last = bcum[..., -1:, :]
        state = (jnp.exp(b_last[..., 0, :])[..., None] * state
                 + jnp.einsum('bhsk,bhsv->bhkv', kb * jnp.exp(b_last - bcum), vb))
        return state, o

    s0 = jnp.zeros((B, HG_HEADS, HG_DIM, HG_DIM), f32)
    _, o = lax.scan(step, s0, (to_chunks(qf), to_chunks(k), to_chunks(v), to_chunks(log_f)))
    o = o.transpose(1, 0, 3, 2, 4).reshape(B, S, HG_HEADS, HG_DIM)
    o = _rms_norm(o).reshape(B, S, D_GROUP) * norm_g.astype(f32)
    return (o * jax.nn.silu(g.astype(f32))).astype(q.dtype)


def _s5(u, a_re, a_im, log_dt, b_re, b_im, c_re, c_im, d_skip, glu_w, glu_b):
    B, S, _ = u.shape
    f32 = jnp.float32
    uf = u.astype(f32)
    ug = uf.reshape(B, S, S5_GROUPS, S5_CH)
    dt = jnp.exp(log_dt.astype(f32))[:, None]
    ar, ai = a_re.astype(f32), a_im.astype(f32)
    mag = jnp.exp(ar * dt)
    lr, li = mag * jnp.cos(ai * dt), mag * jnp.sin(ai * dt)
    den = ar * ar + ai * ai
    zr = ((lr - 1.0) * ar + li * ai) / den
    zi = (li * ar - (lr - 1.0) * ai) / den
    br, bi = b_re.astype(f32), b_im.astype(f32)
    bbar_r = zr[..., None] * br - zi[..., None] * bi
    bbar_i = zr[..., None] * bi + zi[..., None] * br
    xr = jnp.einsum('bsgc,gpc->bsgp', ug, bbar_r)
    xi = jnp.einsum('bsgc,gpc->bsgp', ug, bbar_i)
    lam_r = jnp.broadcast_to(lr[None, None], (1, S, S5_GROUPS, S5_STATE))
    lam_i = jnp.broadcast_to(li[None, None], (1, S, S5_GROUPS, S5_STATE))

    def combine(e1, e2):
        a1r, a1i, b1r, b1i = e1
        a2r, a2i, b2r, b2i = e2
        return (a2r * a1r - a2i * a1i, a2r * a1i + a2i * a1r,
                a2r * b1r - a2i * b1i + b2r, a2r * b1i + a2i * b1r + b2i)

    _, _, hr, hi = lax.associative_scan(combine, (lam_r, lam_i, xr, xi), axis=1)
    y = (jnp.einsum('bsgp,gcp->bsgc', hr, c_re.astype(f32))
         - jnp.einsum('bsgp,gcp->bsgc', hi, c_im.astype(f32)))
    y = y.reshape(B, S, D_GROUP) + d_skip.astype(f32) * uf
    y = jax.nn.gelu(y)
    y = y * jax.nn.sigmoid(y @ glu_w.astype(f32) + glu_b.astype(f32))
    return y.astype(u.dtype)


def _stick_breaking(q, k, v):
    B, S, _ = q.shape
    f32 = jnp.float32

    def heads(t):
        return t.astype(f32).reshape(B, S, SB_HEADS, SB_DIM).transpose(0, 2, 1, 3)

    qh, kh, vh = heads(q) * (SB_DIM ** -0.5), heads(k), heads(v)
    n_blk = S // Q_BLOCK
    q_blocks = qh.reshape(B, SB_HEADS, n_blk, Q_BLOCK, SB_DIM).transpose(2, 0, 1, 3, 4)
    key_pos = jnp.arange(S)

    def block(args):
        qb, start = args
        z = jnp.einsum('bhtd,bhsd->bhts', qb, kh)
        q_pos = start + jnp.arange(Q_BLOCK)
        mask = key_pos[None, :] < q_pos[:, None]
        log_rest = jnp.where(mask, jax.nn.log_sigmoid(-z), 0.0)
        after = lax.cumsum(log_rest, axis=3, reverse=True) - log_rest
        w = jnp.where(mask, jnp.exp(jax.nn.log_sigmoid(z) + after), 0.0)
        return jnp.einsum('bhts,bhsd->bhtd', w, vh)

    o = lax.map(block, (q_blocks, jnp.arange(n_blk) * Q_BLOCK))
    return o.transpose(1, 0, 3, 2, 4).reshape(B, S, D_GROUP).astype(q.dtype)


def _rwkv7(r, k, v, xw, xa, xg, w0, w2, a0, a2, g2, k_k, k_a, r_k, gn_g, gn_b):
    B, S, _ = r.shape
    f32 = jnp.float32
    w_log = -jax.nn.softplus(-(w0 + jnp.tanh(xw) @ w2).astype(f32)) - 0.5
    decay = jnp.exp(-jnp.exp(w_log))
    a = jax.nn.sigmoid((a0 + xa @ a2).astype(f32))
    g = (jax.nn.sigmoid(xg) @ g2).astype(f32)

    def heads(t):
        return t.astype(f32).reshape(B, S, RW_HEADS, RW_DIM)

    rh, kh, vh, dh, ah = heads(r), heads(k), heads(v), heads(decay), heads(a)
    kk = kh * k_k.astype(f32).reshape(RW_HEADS, RW_DIM)
    kk = kk * lax.rsqrt(jnp.maximum(jnp.sum(kk * kk, -1, keepdims=True), 1e-24))
    kh = kh * (1.0 + (ah - 1.0) * k_a.astype(f32).reshape(RW_HEADS, RW_DIM))

    def step(state, inp):
        rt, wt, kt, vt, kkt, at = inp
        removed = jnp.einsum('bhvk,bhk->bhv', state, kkt)
        state = (state * wt[:, :, None, :]
                 - removed[..., None] * (kkt * at)[:, :, None, :]
                 + vt[..., None] * kt[:, :, None, :])
        return state, jnp.einsum('bhvk,bhk->bhv', state, rt)

    s0 = jnp.zeros((B, RW_HEADS, RW_DIM, RW_DIM), f32)
    xs = tuple(jnp.moveaxis(t, 1, 0) for t in (rh, dh, kh, vh, kk, ah))
    _, out = lax.scan(step, s0, xs)
    out = jnp.moveaxis(out, 0, 1)
    mu = jnp.mean(out, -1, keepdims=True)
    var = jnp.mean(jnp.square(out - mu), -1, keepdims=True)
    out = ((out - mu) * lax.rsqrt(var + RW_GN_EPS)).reshape(B, S, D_GROUP)
    out = out * gn_g.astype(f32) + gn_b.astype(f32)
    bonus = jnp.sum(rh * kh * r_k.astype(f32), -1, keepdims=True) * vh
    out = (out + bonus.reshape(B, S, D_GROUP)) * g
    return out.astype(r.dtype)


def setup_inputs(seed: int = 0) -> dict:
    key = jax.random.key(seed)
    ks = iter(jax.random.split(key, 48))
    f32 = jnp.float32
    L = DEPTH

    def nrm(shape, scale):
        return scale * jax.random.normal(next(ks), shape, f32)

    n_idx = jnp.arange(S5_STATE, dtype=f32)
    return {
        'x': nrm((BATCH, SEQ, D_MODEL), 1.0),
        'c': nrm((BATCH, D_MODEL), 1.0),
        'ada_w': nrm((L, D_MODEL, 6 * D_MODEL), 0.5 * D_MODEL ** -0.5),
        'ada_b': nrm((L, 6 * D_MODEL), 0.01),
        'w_in': nrm((L, D_MODEL, N_IN), D_MODEL ** -0.5),
        'w_out': nrm((L, D_MIX, D_MODEL), DEEPNORM_BETA * D_MIX ** -0.5),
        'hg_lb_logits': nrm((L, D_GROUP), 0.5),
        'hg_norm_g': 1.0 + nrm((L, D_GROUP), 0.02),
        's5_a_re': -0.5 + nrm((L, S5_GROUPS, S5_STATE), 0.01),
        's5_a_im': math.pi * n_idx + nrm((L, S5_GROUPS, S5_STATE), 0.01),
        's5_log_dt': jax.random.uniform(next(ks), (L, S5_GROUPS), f32,
                                        math.log(S5_DT_MIN), math.log(S5_DT_MAX)),
        's5_b_re': nrm((L, S5_GROUPS, S5_STATE, S5_CH), (2 * S5_CH) ** -0.5),
        's5_b_im': nrm((L, S5_GROUPS, S5_STATE, S5_CH), (2 * S5_CH) ** -0.5),
        's5_c_re': nrm((L, S5_GROUPS, S5_CH, S5_STATE), S5_STATE ** -0.5),
        's5_c_im': nrm((L, S5_GROUPS, S5_CH, S5_STATE), S5_STATE ** -0.5),
        's5_d': nrm((L, D_GROUP), 1.0),
        's5_glu_w': nrm((L, D_GROUP, D_GROUP), D_GROUP ** -0.5),
        's5_glu_b': nrm((L, D_GROUP), 0.01),
        'rw_mu': jax.random.uniform(next(ks), (L, RW_WIDTH), f32),
        'rw_w0': nrm((L, D_GROUP), 1.0),
        'rw_w2': nrm((L, RW_W_RANK, D_GROUP), 0.1 * RW_W_RANK ** -0.5),
        'rw_a0': nrm((L, D_GROUP), 0.5),
        'rw_a2': nrm((L, RW_A_RANK, D_GROUP), 0.1 * RW_A_RANK ** -0.5),
        'rw_g2': nrm((L, RW_G_RANK, D_GROUP), RW_G_RANK ** -0.5),
        'rw_k_k': 0.85 + nrm((L, D_GROUP), 0.02),
        'rw_k_a': 1.0 + nrm((L, D_GROUP), 0.02),
        'rw_r_k': nrm((L, RW_HEADS, RW_DIM), 0.1),
        'rw_gn_g': 1.0 + nrm((L, D_GROUP), 0.02),
        'rw_gn_b': nrm((L, D_GROUP), 0.01),
        'ln1_g': 1.0 + nrm((L, D_MODEL), 0.02),
        'ln1_b': nrm((L, D_MODEL), 0.01),
        'ffn_w1': nrm((L, D_MODEL, D_FF), D_MODEL ** -0.5),
        'ffn_w3': nrm((L, D_MODEL, D_FF), D_MODEL ** -0.5),
        'ffn_w2': nrm((L, D_FF, D_MODEL), DEEPNORM_BETA * D_FF ** -0.5),
        'ln2_g': 1.0 + nrm((L, D_MODEL), 0.02),
        'ln2_b': nrm((L, D_MODEL), 0.01),
    }


def reference(x, c, ada_w, ada_b, w_in, w_out, hg_lb_logits, hg_norm_g,
              s5_a_re, s5_a_im, s5_log_dt, s5_b_re, s5_b_im, s5_c_re, s5_c_im,
              s5_d, s5_glu_w, s5_glu_b,
              rw_mu, rw_w0, rw_w2, rw_a0, rw_a2, rw_g2, rw_k_k, rw_k_a, rw_r_k,
              rw_gn_g, rw_gn_b,
              ln1_g, ln1_b, ffn_w1, ffn_w3, ffn_w2, ln2_g, ln2_b):
    lb_all = jnp.cumsum(jax.nn.softmax(hg_lb_logits.astype(jnp.float32), axis=0), axis=0)
    lb_all = lb_all - lb_all[:1]
    c_act = jax.nn.silu(c)
    for l in range(DEPTH):
        mod = c_act @ ada_w[l] + ada_b[l]
        shift1, scale1, gate1, shift2, scale2, gate2 = [m[:, None, :] for m in jnp.split(mod, 6, axis=-1)]

        h = x * (1.0 + scale1) + shift1
        proj = h @ w_in[l]
        hg_q, hg_f, hg_i, hg_g, s5_u, sb_q, sb_k, sb_v = jnp.split(proj[..., :RW_OFFSET], 8, axis=-1)
        rw = _token_shift(proj[..., RW_OFFSET:], rw_mu[l])
        rw_r, rw_k, rw_v, rw_xw, rw_xa, rw_xg = jnp.split(rw, RW_SPLITS, axis=-1)
        o_a = _hgrn2(hg_q, hg_f, hg_i, hg_g, lb_all[l], hg_norm_g[l])
        o_b = _s5(s5_u, s5_a_re[l], s5_a_im[l], s5_log_dt[l], s5_b_re[l], s5_b_im[l],
                  s5_c_re[l], s5_c_im[l], s5_d[l], s5_glu_w[l], s5_glu_b[l])
        o_c = _stick_breaking(sb_q, sb_k, sb_v)
        o_d = _rwkv7(rw_r, rw_k, rw_v, rw_xw, rw_xa, rw_xg, rw_w0[l], rw_w2[l], rw_a0[l],
                     rw_a2[l], rw_g2[l], rw_k_k[l], rw_k_a[l], rw_r_k[l], rw_gn_g[l], rw_gn_b[l])
        mix = jnp.concatenate([o_a, o_b, o_c, o_d], axis=-1) @ w_out[l]
        x = _layer_norm(DEEPNORM_ALPHA * x + gate1 * mix, ln1_g[l], ln1_b[l])

        h = x * (1.0 + scale2) + shift2
        ffn = (jax.nn.silu(h @ ffn_w1[l]) * (h @ ffn_w3[l])) @ ffn_w2[l]
        x = _layer_norm(DEEPNORM_ALPHA * x + gate2 * ffn, ln2_g[l], ln2_b[l])
    return x
```

```python
import math
import numpy as np
import concourse.bass as bass
import concourse.mybir as mybir
from concourse.bass_utils import run_bass_kernel_spmd

F32 = mybir.dt.float32
F32R = mybir.dt.float32r
AF = mybir.ActivationFunctionType
ALU = mybir.AluOpType

D = 2048
NIN = 5888
DFF = 5632
DEPTH = 4
ALPHA = (2 * DEPTH) ** 0.25
LN_EPS = 1e-5
TB = 512
ENGS = ("pe", "act", "dve", "pool", "sp")
N_DMA_SEMS = 48
ARENA_R = 39424
ARENA_F = 12352


class Op:
    __slots__ = ("eng", "fn", "reads", "writes", "is_dma", "waits", "sem", "val", "vc", "idx", "bar")


class Prog:
    def __init__(self):
        self.nc = bass.Bass("TRN2", target_bir_lowering=False)
        self.ops = []
        self._stack = []

    def enter(self, cm):
        v = cm.__enter__()
        self._stack.append(cm)
        return v

    def sb(self, name, shape, dt=F32):
        return self.enter(self.nc.sbuf_tensor(name, list(shape), dt))

    def ps(self, name, shape, dt=F32):
        return self.enter(self.nc.psum_tensor(name, list(shape), dt))

    def op(self, eng, fn, reads=(), writes=(), dma=False):
        o = Op()
        o.eng, o.fn, o.reads, o.writes, o.is_dma, o.bar = eng, fn, tuple(reads), tuple(writes), dma, False
        self.ops.append(o)
        return o

    def dma(self, out, in_, reads=(), writes=(), q="sp", **kw):
        return self.op(q, lambda e: e.dma_start(out=out, in_=in_, **kw), reads, writes, dma=True)

    def barrier(self):
        o = Op()
        o.bar = True
        self.ops.append(o)

    def finish(self):
        nc = self.nc
        sems = {(e, k): self.enter(nc.semaphore("s_%s%d" % (e, k))) for e in ENGS for k in range(3)}
        dsems = [self.enter(nc.semaphore("d%d" % i)) for i in range(N_DMA_SEMS)]
        ep = 0
        cnt = {e: 1 for e in ENGS}
        dcnt = [0] * N_DMA_SEMS
        dlast = [None] * N_DMA_SEMS
        ndma = 0
        nsw = 0
        last_w, readers = {}, {}
        known = {e: {} for e in ENGS}
        per_eng = {e: [("mark", 0)] for e in ENGS}
        for i, o in enumerate(self.ops):
            if o.bar:
                tot = {("e", e, ep): cnt[e] for e in ENGS}
                for k in range(N_DMA_SEMS):
                    if dcnt[k]:
                        tot[("d", k)] = dcnt[k]
                for e in ENGS:
                    w = [(s, v) for s, v in tot.items() if known[e].get(s, 0) < v]
                    per_eng[e].append(("bar", w, ep))
                    known[e] = {s: v for s, v in tot.items() if s[0] == "d"}
                ep += 1
                cnt = {e: 1 for e in ENGS}
                last_w, readers = {}, {}
                continue
            o.idx = i
            deps = []
            if o.is_dma:
                half_ = N_DMA_SEMS // 2
                if o.eng == "pool":
                    k = half_ + nsw % half_
                    nsw += 1
                else:
                    k = ndma % half_
                    ndma += 1
                dcnt[k] += 16
                o.sem, o.val = ("d", k), dcnt[k]
                if dlast[k] is not None:
                    deps.append(dlast[k])
                dlast[k] = o
            else:
                cnt[o.eng] += 1
                o.sem, o.val = ("e", o.eng, ep), cnt[o.eng]
            for r in o.reads:
                w = last_w.get(r)
                if w is not None:
                    deps.append(w)
            for r in o.writes:
                w = last_w.get(r)
                if w is not None:
                    deps.append(w)
                deps.extend(readers.get(r, ()))
            kn = known[o.eng]
            need = {}
            for d in deps:
                if d is o:
                    continue
                if (not d.is_dma) and (not o.is_dma) and d.eng == o.eng == "pe":
                    continue
                if kn.get(d.sem, 0) >= d.val:
                    continue
                if need.get(d.sem, (0, None))[0] < d.val:
                    need[d.sem] = (d.val, d)
            waits = []
            for s_, (v, d) in sorted(need.items(), key=lambda kv: -kv[1][1].idx):
                if kn.get(s_, 0) >= v:
                    continue
                waits.append((s_, v))
                for s2, v2 in d.vc.items():
                    if kn.get(s2, 0) < v2:
                        kn[s2] = v2
                kn[s_] = max(kn.get(s_, 0), v)
            o.waits = waits
            vc = dict(kn)
            vc[o.sem] = o.val
            o.vc = vc
            for r in o.reads:
                readers.setdefault(r, []).append(o)
            for r in o.writes:
                last_w[r] = o
                readers[r] = []
            per_eng[o.eng].append(o)
        self.n_ops = sum(1 for o in self.ops if not o.bar)

        def semof(s_):
            return sems[(s_[1], s_[2] % 3)] if s_[0] == "e" else dsems[s_[1]]

        finals = [(("e", e, ep), cnt[e]) for e in ENGS]
        finals += [(("d", k), dcnt[k]) for k in range(N_DMA_SEMS) if dcnt[k]]
        blk = self.enter(nc.Block())

        def body(ename, with_final=False):
            def f(eng):
                for o in per_eng[ename]:
                    if isinstance(o, tuple):
                        if o[0] == "mark":
                            eng.nop().then_inc(sems[(ename, 0)], 1)
                            continue
                        for s_, v in o[1]:
                            eng.wait_ge(semof(s_), v)
                        e_old = o[2]
                        eng.sem_clear(sems[(ename, (e_old + 2) % 3)])
                        eng.nop().then_inc(sems[(ename, (e_old + 1) % 3)], 1)
                        continue
                    for s_, v in o.waits:
                        eng.wait_ge(semof(s_), v)
                    o.fn(eng).then_inc(semof(o.sem), 16 if o.is_dma else 1)
                if with_final:
                    for s_, v in finals:
                        eng.wait_ge(semof(s_), v)
            return f

        blk.tensor(body("pe"))
        blk.scalar(body("act"))
        blk.vector(body("dve"))
        blk.gpsimd(body("pool"))
        blk.sync(body("sp", with_final=True))
        while self._stack:
            self._stack.pop().__exit__(None, None, None)
        return nc


def wtile(w):
    K, N = w.shape
    return np.ascontiguousarray(w.reshape(K // 128, 128, N // 128, 128).transpose(2, 1, 0, 3))


def vecT(v):
    sh = v.shape
    return np.ascontiguousarray(np.swapaxes(v.reshape(sh[:-1] + (sh[-1] // 128, 128)), -1, -2))


class Builder:
    def __init__(self, S, L, stages=("ada", "proj", "mix", "wout", "ffn"), mix="hgrn+s5+sb+rwkv", dbg=False):
        self.S, self.L, self.stages, self.mixmode, self.dbg = S, L, stages, mix, dbg
        self.P = Prog()
        self.nc = self.P.nc
        self.rr = 0

    def din(self, name, shape):
        return self.nc.dram_tensor(name, list(shape), F32, kind="ExternalInput").ap()

    def dscratch(self, name, shape):
        return self.nc.dram_tensor(name, list(shape), F32, kind="Internal").ap()

    def A(self, off, n):
        return self.arenaR[:, off:off + n]

    def AF(self, off, n):
        return self.arenaF[:, off:off + n]

    def evac_eng(self):
        self.rr += 1
        return "act" if self.rr % 2 else "dve"

    def copy(self, eng, out, in_, reads, writes):
        if eng == "act":
            self.P.op("act", lambda e: e.copy(out, in_), reads, writes)
        else:
            self.P.op(eng, lambda e: e.tensor_copy(out, in_), reads, writes)

    def build(self):
        P, nc, S, L = self.P, self.nc, self.S, self.L
        NB = S // TB
        self.x_in = self.din("x", [S, D])
        self.cT = self.din("cT", [128, 16])
        self.ada_w = self.din("ada_w", [L, 96, 128, 16 * 128])
        self.ada_bT = self.din("ada_bT", [L, 128, 96])
        self.w_in = self.din("w_in", [L, 46, 128, 16 * 128])
        self.w_out = self.din("w_out", [L, 16, 128, 16 * 128])
        self.w1 = self.din("w1", [L, 44, 128, 16 * 128])
        self.w3 = self.din("w3", [L, 44, 128, 16 * 128])
        self.w2 = self.din("w2", [L, 16, 128, 44 * 128])
        self.ln1_g = self.din("ln1_g", [L, D])
        self.ln1_b = self.din("ln1_b", [L, D])
        self.ln2_g = self.din("ln2_g", [L, D])
        self.ln2_b = self.din("ln2_b", [L, D])
        self.ident_d = self.din("ident", [128, 128])
        self.sbmask_d = self.din("sbmask", [4, 128, 512])
        self.tri_d = self.din("tri", [128, 128])
        self.ones_d = self.din("ones", [128, 128])
        self.m32_d = self.din("m32", [128, 128])
        self.rst_d = self.din("rst", [128, 4096])
        self.hg_lgT = self.din("hg_lgT", [128, 16])
        self.hg_ngT = self.din("hg_ngT", [128, 16])
        self.s5_dT_d = self.din("s5_dT", [128, 16])
        self.hm_d = self.din("hm", [128, 2])
        self.blk64_d = self.din("blk64", [128, 128])
        self.rst64_d = self.din("rst64", [128, 2048])
        self.rwmask_d = self.din("rwmask", [3, 128, 256])
        self.rw_muT = self.din("rw_muT", [L, 128, 14])
        self.rw_w0T = self.din("rw_w0T", [L, 128, 4])
        self.rw_a0T = self.din("rw_a0T", [L, 128, 4])
        self.rw_kkT = self.din("rw_kkT", [L, 128, 4])
        self.rw_kaT = self.din("rw_kaT", [L, 128, 4])
        self.rw_rkT = self.din("rw_rkT", [L, 128, 4])
        self.rw_gngT = self.din("rw_gngT", [L, 128, 4])
        self.rw_gnbT = self.din("rw_gnbT", [L, 128, 4])
        self.rw_w2p = self.din("rw_w2p", [L, 128, 512])
        self.rw_a2p = self.din("rw_a2p", [L, 128, 512])
        self.rw_g2 = self.din("rw_g2", [L, 128, 512])
        self.s5_gbT_d = self.din("s5_gbT", [128, 16])
        self.s5_arT = self.din("s5_arT", [L, 128, 16])
        self.s5_aiT = self.din("s5_aiT", [L, 128, 16])
        self.s5_ldtT = self.din("s5_ldtT", [L, 128, 16])
        self.s5_wb = self.din("s5_wb", [L, 128, 2 * 16 * 128])
        self.s5_wcre = self.din("s5_wcre", [L, 128, 16 * 128])
        self.s5_wcim = self.din("s5_wcim", [L, 128, 16 * 128])
        self.s5_glu = self.din("s5_glu", [L, 4, 128, 4 * 128])
        self.out = self.nc.dram_tensor("out", [S, D], F32, kind="ExternalOutput").ap()
        self.xA = self.dscratch("xA", [S, D])
        self.xB = self.dscratch("xB", [S, D])
        if not self.dbg:
            self.projT = self.dscratch("projT", [NIN, S])
        if self.dbg:
            self.catT = self.nc.dram_tensor("catT", [D, S], F32, kind="ExternalOutput").ap()
            self.projT = self.nc.dram_tensor("projT", [NIN, S], F32, kind="ExternalOutput").ap()
        else:
            self.catT = self.dscratch("catT", [D, S])

        self.arenaR = P.sb("arenaR", [128, ARENA_R])
        self.arenaF = P.sb("arenaF", [128, ARENA_F])
        self.ident = P.sb("ident_sb", [128, 128])
        self.modT = P.sb("modT", [128, 96])
        self.cact = P.sb("cact", [128, 16])
        self.epsT = P.sb("epsT", [128, 1])
        self.oneT = P.sb("oneT", [128, 1])
        self.rmsT = P.sb("rmsT", [128, 1])
        self.ones_sb = P.sb("ones_sb", [128, 128])
        self.hg_small = P.sb("hg_small", [128, 3 * (S // 32)])
        self.hg_m32 = P.sb("hg_m32", [128, 128])
        self.hg_lg = P.sb("hg_lg", [128, 4 * 4])
        self.hg_lb = P.sb("hg_lb", [128, 4 * 4])
        self.hg_oml = P.sb("hg_oml", [128, 4 * 4])
        self.hg_ng = P.sb("hg_ng", [128, 4 * 4])
        self.hg_sum = P.sb("hg_sum", [128, 4])
        self.s5p = P.sb("s5p", [128, 20, 16])
        self.rwp = P.sb("rwp", [128, 48])
        self.rw_hm = P.sb("rw_hm", [128, 2])
        self.gnepsT = P.sb("gnepsT", [128, 1])
        self.s5_dT = P.sb("s5_dT_sb", [128, 16])
        self.s5_gbT = P.sb("s5_gbT_sb", [128, 16])
        self.psb = [P.ps("psb%d" % i, [128, 512]) for i in range(8)]
        P.dma(self.ident[:].bitcast(F32R), self.ident_d.bitcast(F32R), writes=["ident"], q="pool")
        P.dma(self.cact[:], self.cT, writes=["cact"])
        P.op("act", lambda e: e.activation(out=self.cact[:], in_=self.cact[:], func=AF.Silu), ["cact"], ["cact"])
        P.op("dve", lambda e: e.memset(self.epsT[:], LN_EPS), [], ["epsT"])
        P.op("dve", lambda e: e.memset(self.oneT[:], 1.0), [], ["oneT"])
        P.op("dve", lambda e: e.memset(self.rmsT[:], 1e-6), [], ["rmsT"])
        P.dma(self.ones_sb[:], self.ones_d, writes=["ones_sb"])
        P.dma(self.hg_lg[:], self.hg_lgT, writes=["hgp"])
        P.dma(self.hg_ng[:], self.hg_ngT, writes=["hgp"])
        P.dma(self.s5_dT[:], self.s5_dT_d, writes=["s5d"])
        P.dma(self.rw_hm[:], self.hm_d, writes=["hm"])
        P.op("dve", lambda e: e.memset(self.gnepsT[:], 64e-5), [], ["gneps"])
        P.dma(self.s5_gbT[:], self.s5_gbT_d, writes=["s5d"])
        lg3 = self.hg_lg[:].rearrange("p (l h) -> p l h", h=4)
        lb3 = self.hg_lb[:].rearrange("p (l h) -> p l h", h=4)
        P.op("act", lambda e: e.activation(out=self.hg_lg[:], in_=self.hg_lg[:], func=AF.Exp), ["hgp"], ["hgp"])
        P.op("dve", lambda e: e.tensor_tensor(self.hg_sum[:], lg3[:, 0, :], lg3[:, 1, :], ALU.add), ["hgp"], ["hgsum"])
        P.op("dve", lambda e: e.tensor_tensor(self.hg_sum[:], self.hg_sum[:], lg3[:, 2, :], ALU.add), ["hgp", "hgsum"], ["hgsum"])
        P.op("dve", lambda e: e.tensor_tensor(self.hg_sum[:], self.hg_sum[:], lg3[:, 3, :], ALU.add), ["hgp", "hgsum"], ["hgsum"])
        P.op("dve", lambda e: e.reciprocal(self.hg_sum[:], self.hg_sum[:]), ["hgsum"], ["hgsum"])
        P.op("dve", lambda e: e.memset(lb3[:, 0, :], 0.0), [], ["hglb"])
        for ll in range(1, 4):
            P.op("dve", lambda e, ll=ll: e.tensor_tensor(lg3[:, ll, :], lg3[:, ll, :], self.hg_sum[:], ALU.mult), ["hgp", "hgsum"], ["hgp"])
            P.op("dve", lambda e, ll=ll: e.tensor_tensor(lb3[:, ll, :], lb3[:, ll - 1, :], lg3[:, ll, :], ALU.add), ["hgp", "hglb"], ["hglb"])
        P.op("dve", lambda e: e.tensor_scalar(self.hg_oml[:], self.hg_lb[:], -1.0, 1.0, ALU.mult, ALU.add), ["hglb"], ["hglb"])
        P.barrier()
        for l in range(L):
            xin = self.x_in if l == 0 else self.xA
            xout = self.out if l == L - 1 else self.xA
            if "ada" in self.stages:
                self.stage_ada(l)
                P.barrier()
            if "proj" in self.stages:
                self.stage_proj(l, xin)
                P.barrier()
            if "mix" in self.stages:
                self.stage_mix(l)
                P.barrier()
            if "wout" in self.stages:
                self.stage_wout(l, xin)
                P.barrier()
            if "ffn" in self.stages:
                self.stage_ffn(l, xout)
                P.barrier()
        return P.finish()

    def stage_ada(self, l):
        P = self.P
        ps = self.psb[0]
        for j in range(96):
            wt = self.AF((j % 3) * 2048, 2048)
            rw = "adaw%d" % (j % 3)
            P.dma(wt, self.ada_w[l, j], writes=[rw], q=("sp" if j % 2 else "act"))
            for a in range(16):
                P.op("pe", lambda e, wt=wt, a=a, j=j: e.matmul(ps[:, j:j + 1], wt[:, a * 128:(a + 1) * 128],
                                                              self.cact[:, a:a + 1], start=(a == 0), stop=(a == 15)),
                     [rw, "cact"], ["ps_ada"])
        bt = self.AF(3 * 2048, 96)
        P.dma(bt, self.ada_bT[l], writes=["adab"])
        P.op("dve", lambda e: e.tensor_tensor(self.modT[:], ps[:, 0:96], bt, ALU.add), ["ps_ada", "adab"], ["modT"])
        for g in (1, 4):
            P.op("dve", lambda e, g=g: e.tensor_scalar_add(self.modT[:, g * 16:(g + 1) * 16], self.modT[:, g * 16:(g + 1) * 16], 1.0),
                 ["modT"], ["modT"])

    def load_xT(self, src, t0, hT, shift_g, scale_g, tag):
        P = self.P
        xt = [self.A(i * 2048, 2048) for i in range(4)]
        for i in range(4):
            P.dma(xt[i].bitcast(F32R), src[t0 + i * 128:t0 + (i + 1) * 128, :].bitcast(F32R), writes=["R%d" % i], q="pool")
        for k in range(16):
            ps = self.psb[k % 2]
            for i in range(4):
                P.op("pe", lambda e, ps=ps, i=i, k=k: e.transpose(ps[:, i * 128:(i + 1) * 128], xt[i][:, k * 128:(k + 1) * 128], self.ident[:]),
                     ["R%d" % i, "ident"], ["pst%d" % (k % 2)])
            eng = "dve" if k % 2 else "pool"
            eng = "dve"
            P.op(eng, lambda e, ps=ps, k=k: e.tensor_scalar(hT[:, k, :].bitcast(F32R), ps[:], self.modT[:, scale_g * 16 + k:scale_g * 16 + k + 1],
                                                            self.modT[:, shift_g * 16 + k:shift_g * 16 + k + 1], ALU.mult, ALU.add),
                 ["pst%d" % (k % 2), "modT"], ["%s%d" % (tag, k)])
        return xt

    def stage_proj(self, l, xin):
        P, S = self.P, self.S
        off_h = 8192
        off_w = off_h + 2 * 8192
        off_o = 0
        for tb in range(S // TB):
            t0 = tb * TB
            hT = self.A(off_h + (tb % 2) * 8192, 8192).rearrange("p (a b) -> p a b", b=512)
            tag = "hT%d_" % (tb % 2)
            self.load_xT(xin, t0, hT, 0, 1, tag)
            for j in range(46):
                wi = (tb * 46 + j) % 3
                wt = self.A(off_w + wi * 2048, 2048)
                P.dma(wt.bitcast(F32R), self.w_in[l, j].bitcast(F32R), writes=["w%d" % wi], q="pool")
                ps = self.psb[2 + j % 4]
                for a in range(16):
                    P.op("pe", lambda e, ps=ps, wt=wt, a=a, hT=hT: e.matmul(ps[:], wt[:, a * 128:(a + 1) * 128].bitcast(F32R), hT[:, a, :].bitcast(F32R),
                                                                        start=(a == 0), stop=(a == 15)),
                         ["w%d" % wi, tag + str(a)], ["psm%d" % (j % 4)])
                oi = j % 4
                ot = self.AF(off_o + oi * 512, 512)
                self.copy(self.evac_eng(), ot, ps[:], ["psm%d" % (j % 4)], ["ot%d" % oi])
                P.dma(self.projT[j * 128:(j + 1) * 128, t0:t0 + TB], ot, reads=["ot%d" % oi], q=("sp" if j % 2 else "act"))

    def stage_mix(self, l):
        if self.mixmode == "stub":
            P, S = self.P, self.S
            for j in range(16):
                for h in range(S // 2048 if S >= 2048 else 1):
                    n = min(S, 2048)
                    t = self.AF((j % 2) * 2048, n)
                    P.dma(t, self.projT[j * 128:(j + 1) * 128, h * n:(h + 1) * n], writes=["mx%d" % (j % 2)])
                    P.dma(self.catT[j * 128:(j + 1) * 128, h * n:(h + 1) * n], t, reads=["mx%d" % (j % 2)])
        else:
            P = self.P
            for nm in self.mixmode.split("+"):
                getattr(self, "mix_" + nm)(l)
                P.barrier()


    def out_rows(self, src_tile_ap, row0, t0, n, tag, q="sp"):
        self.P.dma(self.catT[row0:row0 + 128, t0:t0 + n], src_tile_ap, reads=[tag], q=q)

    def mix_sb(self, l):
        P, S = self.P, self.S
        NBK = S // 128
        NG = S // TB
        isq = 1.0 / math.sqrt(128.0)
        qT = self.A(0, S)
        kT = self.A(S, S)
        vT = self.A(2 * S, S)
        vtok = self.A(3 * S, S).rearrange("p (a b) -> p a b", b=128)
        wTb = [self.A(4 * S + i * 512, 512) for i in range(2)]
        sph = [self.A(4 * S + 1024 + i * 512, 512) for i in range(2)]
        spl = [self.A(4 * S + 2048 + i * 512, 512) for i in range(2)]
        tri = self.A(4 * S + 3072, 128)
        ones = self.A(4 * S + 3200, 128)
        eb = [self.AF(i * 512, 512) for i in range(2)]
        spb = [self.AF(1024 + i * 512, 512) for i in range(2)]
        Cb = self.AF(2048, 512)
        ob = self.AF(2560, 512)
        msk = self.AF(3072, 2048).rearrange("p (a b) -> p a b", b=512)
        btb = [self.AF(5120 + i * 512, 512) for i in range(2)]
        P.dma(msk, self.sbmask_d.rearrange("a p t -> p a t"), writes=["msk"])
        P.dma(tri.bitcast(F32R), self.tri_d.bitcast(F32R), writes=["tri"], q="pool")
        P.dma(ones.bitcast(F32R), self.ones_d.bitcast(F32R), writes=["ones"], q="pool")
        pa, pbk, pc, po = self.psb[0:2], self.psb[2:4], self.psb[4:6], self.psb[6:8]
        it = 0
        for h in range(4):
            P.dma(qT.bitcast(F32R), self.projT[2560 + h * 128:2560 + (h + 1) * 128, :].bitcast(F32R), writes=["qT"], q="pool")
            P.dma(kT.bitcast(F32R), self.projT[3072 + h * 128:3072 + (h + 1) * 128, :].bitcast(F32R), writes=["kT"], q="pool")
            P.dma(vT.bitcast(F32R), self.projT[3584 + h * 128:3584 + (h + 1) * 128, :].bitcast(F32R), writes=["vT"], q="pool")
            P.op("act", lambda e: e.activation(out=kT.bitcast(F32R), in_=kT, func=AF.Copy, scale=-isq), ["kT"], ["kT"])
            for b in range(NBK):
                ps = self.psb[b % 2]
                P.op("pe", lambda e, ps=ps, b=b: e.transpose(ps[:, 0:128], vT[:, b * 128:(b + 1) * 128], self.ident[:]), ["vT", "ident"], ["ps%d" % (b % 2)])
                self.copy("dve" if b % 2 else "act", vtok[:, b, :].bitcast(F32R), ps[:, 0:128], ["ps%d" % (b % 2)], ["vtok%d" % b])
            for g in range(NG):
                qg = qT[:, g * TB:(g + 1) * TB].bitcast(F32R)
                first = True
                psO = po[g % 2]
                otag = "ps%d" % (6 + g % 2)
                for kb in range(4 * g + 3, -1, -1):
                    i2 = it % 2
                    it += 1
                    kblk = kT[:, kb * 128:(kb + 1) * 128].bitcast(F32R)
                    A_, B_, C_ = pa[i2], pbk[i2], pc[i2]
                    e_, sp_, w_, sh_, sl_, bt_ = eb[i2], spb[i2], wTb[i2], sph[i2], spl[i2], btb[i2]
                    tA, tB, tC = "ps%d" % i2, "ps%d" % (2 + i2), "ps%d" % (4 + i2)
                    P.op("pe", lambda e, A_=A_, kblk=kblk, qg=qg: e.matmul(A_[:], kblk, qg, start=True, stop=True), ["kT", "qT"], [tA])
                    P.op("act", lambda e, A_=A_, e_=e_: e.activation(out=e_, in_=A_[:], func=AF.Exp, scale=-1.0), [tA], ["e%d" % i2])
                    P.op("act", lambda e, e_=e_, sp_=sp_: e.activation(out=sp_, in_=e_, func=AF.Ln, bias=self.oneT[:], scale=1.0), ["e%d" % i2, "oneT"], ["sp%d" % i2])
                    m = kb - 4 * g
                    if m >= 0:
                        P.op("pool", lambda e, sp_=sp_, m=m: e.tensor_tensor(sp_, sp_, msk[:, m, :], ALU.mult), ["sp%d" % i2, "msk"], ["sp%d" % i2])
                    P.op("pool", lambda e, sp_=sp_, sh_=sh_: e.tensor_copy(sh_.bitcast(F32R), sp_), ["sp%d" % i2], ["sh%d" % i2])
                    P.op("pool", lambda e, sp_=sp_, sh_=sh_, sl_=sl_: e.tensor_tensor(sl_.bitcast(F32R), sp_, sh_, ALU.subtract), ["sp%d" % i2, "sh%d" % i2], ["sl%d" % i2])
                    P.op("pe", lambda e, B_=B_, kblk=kblk, qg=qg: e.matmul(B_[:], kblk, qg, start=True, stop=False), ["kT", "qT"], [tB])
                    P.op("pe", lambda e, B_=B_, sh_=sh_: e.matmul(B_[:], tri.bitcast(F32R), sh_.bitcast(F32R), start=False, stop=False), ["tri", "sh%d" % i2], [tB])
                    P.op("pe", lambda e, B_=B_, sl_=sl_: e.matmul(B_[:], tri.bitcast(F32R), sl_.bitcast(F32R), start=False, stop=True), ["tri", "sl%d" % i2], [tB])
                    if first:
                        P.op("act", lambda e, B_=B_, w_=w_: e.activation(out=w_.bitcast(F32R), in_=B_[:], func=AF.Exp, scale=-1.0), [tB], ["w%d" % i2])
                    else:
                        P.op("dve", lambda e, B_=B_, bt_=bt_: e.tensor_tensor(bt_, B_[:], Cb, ALU.add), [tB, "Cb"], ["bt%d" % i2])
                        P.op("act", lambda e, bt_=bt_, w_=w_: e.activation(out=w_.bitcast(F32R), in_=bt_, func=AF.Exp, scale=-1.0), ["bt%d" % i2], ["w%d" % i2])
                    if m >= 0:
                        P.op("pool", lambda e, w_=w_, m=m: e.tensor_tensor(w_.bitcast(F32R), w_, msk[:, m, :], ALU.mult), ["w%d" % i2, "msk"], ["w%d" % i2])
                    if kb != 0:
                        P.op("pe", lambda e, C_=C_, sh_=sh_: e.matmul(C_[:], ones.bitcast(F32R), sh_.bitcast(F32R), start=True, stop=False), ["ones", "sh%d" % i2], [tC])
                        P.op("pe", lambda e, C_=C_, sl_=sl_: e.matmul(C_[:], ones.bitcast(F32R), sl_.bitcast(F32R), start=False, stop=True), ["ones", "sl%d" % i2], [tC])
                        if first:
                            P.op("dve", lambda e, C_=C_: e.tensor_copy(Cb, C_[:]), [tC], ["Cb"])
                        else:
                            P.op("dve", lambda e, C_=C_: e.tensor_tensor(Cb, Cb, C_[:], ALU.add), [tC, "Cb"], ["Cb"])
                    P.op("pe", lambda e, psO=psO, kb=kb, w_=w_, first=first: e.matmul(psO[:], vtok[:, kb, :].bitcast(F32R), w_.bitcast(F32R), start=first, stop=(kb == 0)),
                         ["vtok%d" % kb, "w%d" % i2], [otag])
                    first = False
                self.copy("dve", ob, psO[:], [otag], ["ob"])
                self.out_rows(ob, 1024 + h * 128, g * TB, TB, "ob")

    def mix_hgrn(self, l):
        P, S = self.P, self.S
        NC = S // 32
        NB = S // 128
        Qh = self.A(0, S)
        Qt = self.A(S, S)
        Kh = self.A(2 * S, S)
        Kt = self.A(3 * S, S)
        vT = self.A(4 * S, S)
        sm = 5 * S
        ktok = [self.A(sm + i * 128, 128) for i in range(2)]
        vtok = [self.A(sm + 256 + i * 128, 128) for i in range(2)]
        scm = [self.A(sm + 512 + i * 128, 128) for i in range(2)]
        St = [self.A(sm + 768 + i * 128, 128) for i in range(2)]
        T0, T1, T2 = self.AF(0, S), self.AF(S, S), self.AF(2 * S, S)
        fo = 3 * S if 3 * S + 1200 <= ARENA_F else None
        assert fo is None or True
        small = self.hg_small
        eref, erefi, elast = small[:, 0:NC], small[:, NC:2 * NC], small[:, 2 * NC:3 * NC]
        m32 = self.hg_m32[:]
        ob = [T0[:, i * 512:(i + 1) * 512] for i in range(3)]
        rst = self.A(sm + 1024, S)
        P.dma(rst.bitcast(F32R), self.rst_d[:, 0:S].bitcast(F32R), writes=["rst"], q="pool")
        P.dma(m32, self.m32_d, writes=["m32"])
        c3 = lambda t: t.rearrange("p (c t) -> p c t", t=32)
        for h in range(4):
            lbc = self.hg_lb[:, 4 * l + h:4 * l + h + 1]
            oml = self.hg_oml[:, 4 * l + h:4 * l + h + 1]
            P.dma(T0, self.projT[512 + h * 128:512 + (h + 1) * 128, :], writes=["T0"])
            P.dma(T1, self.projT[h * 128:(h + 1) * 128, :], writes=["T1"], q="act")
            P.dma(vT.bitcast(F32R), self.projT[1024 + h * 128:1024 + (h + 1) * 128, :].bitcast(F32R), writes=["vT"], q="pool")
            P.op("act", lambda e: e.activation(out=T2, in_=T0, func=AF.Sigmoid), ["T0"], ["T2"])
            P.op("dve", lambda e, oml=oml, lbc=lbc: e.tensor_scalar(T2, T2, oml, lbc, ALU.mult, ALU.add), ["T2", "hgp"], ["T2"])
            P.op("act", lambda e: e.activation(out=T2, in_=T2, func=AF.Ln), ["T2"], ["T2"])
            P.op("act", lambda e: e.activation(out=T0, in_=T0, func=AF.Sigmoid, scale=-1.0), ["T0"], ["T0"])
            P.op("dve", lambda e, oml=oml: e.tensor_scalar(T0, T0, oml, None, ALU.mult), ["T0", "hgp"], ["T0"])
            P.op("dve", lambda e: e.tensor_tensor_scan(T2, rst, T2, 0.0, ALU.mult, ALU.add), ["T2", "rst"], ["T2"])
            P.op("act", lambda e: e.activation(out=T1, in_=T1, func=AF.Silu), ["T1"], ["T1"])
            P.op("act", lambda e: e.activation(out=eref, in_=c3(T2)[:, :, 15], func=AF.Exp, scale=-1.0), ["T2"], ["hgs"])
            P.op("act", lambda e: e.activation(out=erefi, in_=c3(T2)[:, :, 15], func=AF.Exp), ["T2"], ["hgs"])
            P.op("act", lambda e: e.activation(out=elast, in_=c3(T2)[:, :, 31], func=AF.Exp), ["T2"], ["hgs"])
            P.op("act", lambda e: e.activation(out=Qh.bitcast(F32R), in_=T2, func=AF.Exp), ["T2"], ["Qh"])
            P.op("dve", lambda e: e.tensor_tensor(Qh.bitcast(F32R), Qh, T1, ALU.mult), ["Qh", "T1"], ["Qh"])
            P.op("pool", lambda e: e.tensor_tensor(c3(Qt).bitcast(F32R), c3(Qh), eref.unsqueeze(2).to_broadcast([128, NC, 32]), ALU.mult), ["Qh", "hgs"], ["Qt"])
            P.op("act", lambda e: e.activation(out=Kt.bitcast(F32R), in_=T2, func=AF.Exp, scale=-1.0), ["T2"], ["Kt"])
            P.op("dve", lambda e: e.tensor_tensor(Kt.bitcast(F32R), Kt, T0, ALU.mult), ["Kt", "T0"], ["Kt"])
            P.op("pool", lambda e: e.tensor_tensor(c3(Kh).bitcast(F32R), c3(Kt), elast.unsqueeze(2).to_broadcast([128, NC, 32]), ALU.mult), ["Kt", "hgs"], ["Kh"])
            P.op("dve", lambda e: e.tensor_tensor(c3(Kt).bitcast(F32R), c3(Kt), erefi.unsqueeze(2).to_broadcast([128, NC, 32]), ALU.mult), ["Kt", "hgs", "Kh"], ["Kt"])
            P.op("dve", lambda e: e.tensor_scalar(St[0].bitcast(F32R), m32, 0.0, None, ALU.mult), ["m32"], ["St0"])
            for b in range(NB):
                i2 = b % 2
                tsl = slice(b * 128, (b + 1) * 128)
                pk, pv, psc, po_, pkv = self.psb[0 + i2], self.psb[2 + i2], self.psb[4 + i2], self.psb[6], self.psb[7]
                P.op("pe", lambda e, pk=pk, tsl=tsl: e.transpose(pk[:, 0:128], Kh[:, tsl], self.ident[:]), ["Kh", "ident"], ["ps%d" % i2])
                P.op("pe", lambda e, pv=pv, tsl=tsl: e.transpose(pv[:, 0:128], vT[:, tsl], self.ident[:]), ["vT", "ident"], ["ps%d" % (2 + i2)])
                self.copy("act", ktok[i2].bitcast(F32R), pk[:, 0:128], ["ps%d" % i2], ["ktok%d" % i2])
                self.copy("act", vtok[i2].bitcast(F32R), pv[:, 0:128], ["ps%d" % (2 + i2)], ["vtok%d" % i2])
                P.op("pe", lambda e, psc=psc, tsl=tsl: e.matmul(psc[:, 0:128], Kt[:, tsl].bitcast(F32R), Qt[:, tsl].bitcast(F32R), start=True, stop=True), ["Kt", "Qt"], ["ps%d" % (4 + i2)])
                P.op("dve", lambda e, psc=psc, i2=i2: e.tensor_tensor(scm[i2].bitcast(F32R), psc[:, 0:128], m32, ALU.mult), ["ps%d" % (4 + i2), "m32"], ["scm%d" % i2])
                bank = po_[:, i2 * 128:(i2 + 1) * 128]
                otag = "po%d" % i2
                P.op("pe", lambda e, bank=bank, i2=i2: e.matmul(bank, vtok[i2].bitcast(F32R), scm[i2].bitcast(F32R), start=True, stop=False, skip_group_check=True),
                     ["vtok%d" % i2, "scm%d" % i2], [otag])
                for c in range(4):
                    cg = b * 4 + c
                    cur, nxt = St[cg % 2], St[(cg + 1) % 2]
                    csl = slice(b * 128 + c * 32, b * 128 + (c + 1) * 32)
                    P.op("pe", lambda e, bank=bank, c=c, cur=cur, csl=csl: e.matmul(bank[:, c * 32:(c + 1) * 32], cur.bitcast(F32R), Qh[:, csl].bitcast(F32R), start=False, stop=(c == 3), skip_group_check=True),
                         ["St%d" % (cg % 2), "Qh"], [otag])
                    kvp = pkv[:, (cg % 4) * 128:(cg % 4 + 1) * 128]
                    kvt = "pkv%d" % (cg % 4)
                    tp = (32 * c, 0)
                    P.op("pe", lambda e, kvp=kvp, c=c, i2=i2, tp=tp: e.matmul(kvp, ktok[i2][32 * c:32 * c + 32, :].bitcast(F32R), vtok[i2][32 * c:32 * c + 32, :].bitcast(F32R),
                                                                         start=True, stop=True, tile_position=tp, skip_group_check=True),
                         ["ktok%d" % i2, "vtok%d" % i2], [kvt])
                    P.op("dve", lambda e, nxt=nxt, cur=cur, kvp=kvp, cg=cg: e.scalar_tensor_tensor(nxt.bitcast(F32R), cur, elast[:, cg:cg + 1], kvp, ALU.mult, ALU.add),
                         ["St%d" % (cg % 2), kvt, "hgs"], ["St%d" % ((cg + 1) % 2)])
                if b % 4 == 3:
                    pass
                self.copy("act", T2[:, tsl], bank, [otag], ["T2"])
            P.dma(T1, self.projT[1536 + h * 128:1536 + (h + 1) * 128, :], writes=["T1"])
            P.op("act", lambda e: e.activation(out=T1, in_=T1, func=AF.Silu), ["T1"], ["T1"])
            P.op("dve", lambda e, h=h: e.tensor_scalar(T1, T1, self.hg_ng[:, 4 * l + h:4 * l + h + 1], None, ALU.mult), ["T1", "hgp"], ["T1"])
            for g in range(S // TB):
                gs = slice(g * TB, (g + 1) * TB)
                sq = ob[0]
                P.op("pool", lambda e, gs=gs: e.tensor_tensor(sq, T2[:, gs], T2[:, gs], ALU.mult), ["T2", "T0"], ["T0"])
                pr = self.psb[g % 2]
                P.op("pe", lambda e, pr=pr: e.matmul(pr[:], self.ones_sb[:], sq, start=True, stop=True), ["T0", "ones_sb"], ["ps%d" % (g % 2)])
                rt = ob[1]
                P.op("act", lambda e, pr=pr: e.activation(out=rt, in_=pr[:], func=AF.Sqrt, bias=self.rmsT[:], scale=1.0 / 128.0), ["ps%d" % (g % 2), "rmsT", "T0"], ["T0"])
                P.op("dve", lambda e: e.reciprocal(rt, rt), ["T0"], ["T0"])
                P.op("dve", lambda e, gs=gs: e.tensor_tensor(rt, rt, T2[:, gs], ALU.mult), ["T0", "T2"], ["T0"])
                P.op("pool", lambda e, gs=gs: e.tensor_tensor(ob[2], rt, T1[:, gs], ALU.mult), ["T0", "T1"], ["T0"])
                self.out_rows(ob[2], h * 128, g * TB, TB, "T0")


    def mix_s5(self, l):
        P, S = self.P, self.S
        NBLK = S // 128
        PI_ = math.pi
        sp = self.s5p
        def sv(i):
            return sp[:, i, :]
        AR, AI, DT, MAG, TH, CS, SN, LR, LI, DEN, ZR, ZI, K, TMP, TMP2, GPR, GPI, HLR, HLI, PIH = range(20)
        P.dma(sv(AR), self.s5_arT[l], writes=["s5p"])
        P.dma(sv(AI), self.s5_aiT[l], writes=["s5p"])
        P.dma(sv(DT), self.s5_ldtT[l], writes=["s5p"])
        o = lambda eng, fn: P.op(eng, fn, ["s5p"], ["s5p"])
        o("act", lambda e: e.activation(out=sv(DT), in_=sv(DT), func=AF.Exp))
        o("dve", lambda e: e.tensor_tensor(sv(MAG), sv(AR), sv(DT), ALU.mult))
        o("act", lambda e: e.activation(out=sv(MAG), in_=sv(MAG), func=AF.Exp))
        o("dve", lambda e: e.tensor_tensor(sv(TH), sv(AI), sv(DT), ALU.mult))
        o("dve", lambda e: e.memset(sv(K), 0.0))
        for mth in range(1, 8):
            o("dve", lambda e, mth=mth: e.tensor_single_scalar(sv(TMP), sv(TH), (2 * mth - 1) * PI_, ALU.is_gt))
            o("dve", lambda e: e.tensor_tensor(sv(K), sv(K), sv(TMP), ALU.add))
        C1 = 6.28125
        C2 = 2 * PI_ - C1
        o("dve", lambda e: e.scalar_tensor_tensor(sv(TH), sv(K), -C1, sv(TH), ALU.mult, ALU.add))
        o("dve", lambda e: e.scalar_tensor_tensor(sv(TH), sv(K), -C2, sv(TH), ALU.mult, ALU.add))
        o("dve", lambda e: e.memset(sv(PIH), PI_ / 2))
        o("dve", lambda e: e.tensor_scalar_min(sv(TH), sv(TH), PI_))
        o("dve", lambda e: e.tensor_scalar_max(sv(TH), sv(TH), -PI_))
        o("act", lambda e: e.activation(out=sv(SN), in_=sv(TH), func=AF.Sin))
        o("dve", lambda e: e.tensor_scalar(sv(TMP), sv(TH), -1.0, None, ALU.mult))
        o("dve", lambda e: e.tensor_tensor(sv(TMP), sv(TMP), sv(TH), ALU.max))
        o("dve", lambda e: e.tensor_scalar(sv(TMP), sv(TMP), -1.0, PI_ / 2, ALU.mult, ALU.add))
        o("act", lambda e: e.activation(out=sv(CS), in_=sv(TMP), func=AF.Sin))
        o("dve", lambda e: e.tensor_tensor(sv(LR), sv(MAG), sv(CS), ALU.mult))
        o("dve", lambda e: e.tensor_tensor(sv(LI), sv(MAG), sv(SN), ALU.mult))
        o("dve", lambda e: e.tensor_tensor(sv(DEN), sv(AR), sv(AR), ALU.mult))
        o("dve", lambda e: e.tensor_tensor(sv(TMP), sv(AI), sv(AI), ALU.mult))
        o("dve", lambda e: e.tensor_tensor(sv(DEN), sv(DEN), sv(TMP), ALU.add))
        o("dve", lambda e: e.reciprocal(sv(DEN), sv(DEN)))
        o("dve", lambda e: e.tensor_scalar_add(sv(TMP), sv(LR), -1.0))
        o("dve", lambda e: e.tensor_tensor(sv(ZR), sv(TMP), sv(AR), ALU.mult))
        o("dve", lambda e: e.tensor_tensor(sv(TMP2), sv(LI), sv(AI), ALU.mult))
        o("dve", lambda e: e.tensor_tensor(sv(ZR), sv(ZR), sv(TMP2), ALU.add))
        o("dve", lambda e: e.tensor_tensor(sv(ZR), sv(ZR), sv(DEN), ALU.mult))
        o("dve", lambda e: e.tensor_tensor(sv(ZI), sv(TMP), sv(AI), ALU.mult))
        o("dve", lambda e: e.tensor_tensor(sv(TMP2), sv(LI), sv(AR), ALU.mult))
        o("dve", lambda e: e.tensor_tensor(sv(ZI), sv(TMP2), sv(ZI), ALU.subtract))
        o("dve", lambda e: e.tensor_tensor(sv(ZI), sv(ZI), sv(DEN), ALU.mult))
        Er = self.AF(0, 2048).rearrange("p (a j) -> p a j", j=128)
        Ei = self.AF(2048, 2048).rearrange("p (a j) -> p a j", j=128)
        D0 = self.AF(4096, 2048).rearrange("p (a j) -> p a j", j=128)
        t1 = self.AF(6144, 1024)
        t2 = self.AF(7168, 1024)
        xr = self.AF(8192, 1024)
        xi = self.AF(9216, 1024)
        pw = self.AF(10240, 64).rearrange("p (c a) -> p c a", a=16)
        bc = lambda v, n: v.unsqueeze(2).to_broadcast([128, 16, n])
        ot = lambda eng, fn: P.op(eng, fn, ["s5p", "s5t"], ["s5t", "t10", "t11"])
        ot("dve", lambda e: e.memset(Er[:, :, 0:1], 1.0))
        ot("dve", lambda e: e.memset(Ei[:, :, 0:1], 0.0))
        ot("dve", lambda e: e.tensor_copy(pw[:, 0, :], sv(CS)))
        ot("dve", lambda e: e.tensor_scalar(pw[:, 1, :], sv(SN), -1.0, None, ALU.mult))
        n = 1
        while n < 128:
            ot("dve", lambda e, n=n: e.tensor_tensor(Er[:, :, n:2 * n], Er[:, :, 0:n], bc(pw[:, 0, :], n), ALU.mult))
            ot("dve", lambda e, n=n: e.tensor_tensor(t1.rearrange("p (a j) -> p a j", a=16)[:, :, 0:n], Ei[:, :, 0:n], bc(pw[:, 1, :], n), ALU.mult))
            ot("dve", lambda e, n=n: e.tensor_tensor(Er[:, :, n:2 * n], Er[:, :, n:2 * n], t1.rearrange("p (a j) -> p a j", a=16)[:, :, 0:n], ALU.subtract))
            ot("dve", lambda e, n=n: e.tensor_tensor(Ei[:, :, n:2 * n], Er[:, :, 0:n], bc(pw[:, 1, :], n), ALU.mult))
            ot("dve", lambda e, n=n: e.tensor_tensor(t1.rearrange("p (a j) -> p a j", a=16)[:, :, 0:n], Ei[:, :, 0:n], bc(pw[:, 0, :], n), ALU.mult))
            ot("dve", lambda e, n=n: e.tensor_tensor(Ei[:, :, n:2 * n], Ei[:, :, n:2 * n], t1.rearrange("p (a j) -> p a j", a=16)[:, :, 0:n], ALU.add))
            ot("dve", lambda e: e.tensor_tensor(pw[:, 2, :], pw[:, 0, :], pw[:, 0, :], ALU.mult))
            ot("dve", lambda e: e.tensor_tensor(pw[:, 3, :], pw[:, 1, :], pw[:, 1, :], ALU.mult))
            ot("dve", lambda e: e.tensor_tensor(pw[:, 1, :], pw[:, 0, :], pw[:, 1, :], ALU.mult))
            ot("dve", lambda e: e.tensor_scalar(pw[:, 1, :], pw[:, 1, :], 2.0, None, ALU.mult))
            ot("dve", lambda e: e.tensor_tensor(pw[:, 0, :], pw[:, 2, :], pw[:, 3, :], ALU.subtract))
            n *= 2
        ot("dve", lambda e: e.tensor_copy(D0, bc(sv(MAG), 128)))
        ot("dve", lambda e: e.memset(D0[:, :, 0:1], 0.0))
        uT = self.A(0, 4 * S).rearrange("p (q t) -> p q t", t=S)
        wb = self.A(4 * S + 15360, 4096).rearrange("p (c a n) -> p c a n", c=2, a=16)
        wc = self.A(4 * S + 1024, 4096).rearrange("p (c a n) -> p c a n", c=2, a=16)
        wg = self.A(4 * S + 5120, 2048).rearrange("p (n a k) -> p n a k", n=4, a=4)
        hrb = self.A(4 * S + 7168, 1024).rearrange("p (a j) -> p a j", j=128)
        hib = self.A(4 * S + 8192, 1024).rearrange("p (a j) -> p a j", j=128)
        yg = self.A(4 * S + 9216, 2048).rearrange("p (q t) -> p q t", t=512)
        assert 4 * S + 15360 + 4096 <= ARENA_R
        wcf = self.AF(10304, 2048).rearrange("p (a n) -> p a n", a=16)
        wcf2 = self.AF(12352 - 0, 0) if False else None
        P.dma(uT.bitcast(F32R), self.projT[2048:2560, :].rearrange("(q p) t -> p q t", p=128).bitcast(F32R), writes=["uT"], q="pool")
        P.dma(wb.bitcast(F32R), self.s5_wb[l].bitcast(F32R), writes=["wb"], q="pool")
        P.dma(wg.bitcast(F32R), self.s5_glu[l].rearrange("n p (a k) -> p n a k", a=4).bitcast(F32R), writes=["wg"], q="pool")
        cre = self.A(4 * S + 11264, 2048).rearrange("p (a n) -> p a n", a=16)
        cim = self.A(4 * S + 13312, 2048).rearrange("p (a n) -> p a n", a=16)
        P.dma(cre.bitcast(F32R), self.s5_wcre[l].bitcast(F32R), writes=["cre"], q="pool")
        P.dma(cim.bitcast(F32R), self.s5_wcim[l].bitcast(F32R), writes=["cim"], q="pool")
        ow = lambda eng, fn, w: P.op(eng, fn, ["s5p", "cre", "cim", "wcf", "wc"], [w])
        ow("dve", lambda e: e.tensor_tensor(wcf, cim, bc(sv(ZI), 128), ALU.mult), "wcf")
        ow("dve", lambda e: e.tensor_tensor(wc[:, 0].bitcast(F32R), cre, bc(sv(ZR), 128), ALU.mult), "wc")
        ow("dve", lambda e: e.tensor_tensor(wc[:, 0].bitcast(F32R), wc[:, 0], wcf, ALU.subtract), "wc")
        ow("dve", lambda e: e.tensor_tensor(wcf, cim, bc(sv(ZR), 128), ALU.mult), "wcf")
        ow("dve", lambda e: e.tensor_tensor(wc[:, 1].bitcast(F32R), cre, bc(sv(ZI), 128), ALU.mult), "wc")
        ow("dve", lambda e: e.tensor_tensor(wc[:, 1].bitcast(F32R), wc[:, 1], wcf, ALU.add), "wc")
        ow("dve", lambda e: e.tensor_scalar(wc[:, 1].bitcast(F32R), wc[:, 1], -1.0, None, ALU.mult), "wc")
        o("dve", lambda e: e.memset(sv(GPR), 0.0))
        o("dve", lambda e: e.memset(sv(GPI), 0.0))
        PR, PIm, PY, PG = self.psb[0:2], self.psb[2:4], self.psb[4], self.psb[5:7]
        import os as _os
        stop = int(_os.environ.get("S5_STOP", 99))
        for b in range(int(_os.environ.get("S5_NBLK", NBLK))):
            tsl = slice(b * 128, (b + 1) * 128)
            for hf in range(2):
                asl = slice(hf * 8, hf * 8 + 8)
                for a8 in range(8):
                    a = hf * 8 + a8
                    for c, PP, tg in ((0, PR, "pr"), (1, PIm, "pi")):
                        dst = PP[a8 // 4][:, (a8 % 4) * 128:(a8 % 4 + 1) * 128]
                        P.op("pe", lambda e, dst=dst, c=c, a=a, tsl=tsl: e.matmul(dst, wb[:, c, a, :].bitcast(F32R), uT[:, a // 4, tsl].bitcast(F32R),
                                                                          start=True, stop=True, skip_group_check=True),
                             ["wb", "uT"], [tg + str(a8 // 4)])
                for k2 in range(2 if stop >= 2 else 0):
                    sl = slice(k2 * 512, (k2 + 1) * 512)
                    er = Er[:, hf * 8 + k2 * 4: hf * 8 + k2 * 4 + 4, :]
                    ei = Ei[:, hf * 8 + k2 * 4: hf * 8 + k2 * 4 + 4, :]
                    v4 = lambda t: t.rearrange("p (a j) -> p a j", j=128)
                    prr, pii = v4(PR[k2][:]), v4(PIm[k2][:])
                    P.op("dve", lambda e, sl=sl, er=er, prr=prr: e.tensor_tensor(v4(xr[:, sl]), prr, er, ALU.mult), ["pr%d" % k2, "s5t"], ["xr%d" % k2])
                    P.op("act", lambda e, sl=sl, ei=ei, pii=pii: e.copy(v4(t1[:, sl]), pii), ["pi%d" % k2], ["t1%d" % k2])
                    P.op("pool", lambda e, sl=sl, ei=ei: e.tensor_tensor(v4(t2[:, sl]), v4(t1[:, sl]), ei, ALU.mult), ["t1%d" % k2, "s5t"], ["t2%d" % k2])
                    P.op("dve", lambda e, sl=sl: e.tensor_tensor(xr[:, sl], xr[:, sl], t2[:, sl], ALU.subtract), ["xr%d" % k2, "t2%d" % k2], ["xr%d" % k2])
                    P.op("dve", lambda e, sl=sl, ei=ei, prr=prr: e.tensor_tensor(v4(xi[:, sl]), prr, ei, ALU.mult), ["pr%d" % k2, "s5t"], ["xi%d" % k2])
                    P.op("pool", lambda e, sl=sl, er=er: e.tensor_tensor(v4(t2[:, sl]), v4(t1[:, sl]), er, ALU.mult), ["t1%d" % k2, "s5t", "xr%d" % k2], ["t2%d" % k2])
                    P.op("dve", lambda e, sl=sl: e.tensor_tensor(xi[:, sl], xi[:, sl], t2[:, sl], ALU.add), ["xi%d" % k2, "t2%d" % k2], ["xi%d" % k2])
                if stop < 3:
                    continue
                x3r = xr.rearrange("p (a j) -> p a j", j=128)
                x3i = xi.rearrange("p (a j) -> p a j", j=128)
                tagx = ["xr0", "xr1", "xi0", "xi1"]
                P.op("dve", lambda e, asl=asl: e.tensor_tensor(sv(TMP)[:, asl], sv(MAG)[:, asl], sv(GPR)[:, asl], ALU.mult), ["s5p"], ["s5p"])
                P.op("dve", lambda e, asl=asl: e.tensor_tensor(x3r[:, :, 0], x3r[:, :, 0], sv(TMP)[:, asl], ALU.add), ["s5p"] + tagx, ["xr0", "xr1"])
                P.op("dve", lambda e, asl=asl: e.tensor_tensor(sv(TMP)[:, asl], sv(MAG)[:, asl], sv(GPI)[:, asl], ALU.mult), ["s5p"], ["s5p"])
                P.op("dve", lambda e, asl=asl: e.tensor_tensor(x3i[:, :, 0], x3i[:, :, 0], sv(TMP)[:, asl], ALU.add), ["s5p"] + tagx, ["xi0", "xi1"])
                d0 = D0[:, asl, :].rearrange("p a j -> p (a j)")
                P.op("dve", lambda e, d0=d0: e.tensor_tensor_scan(xr, d0, xr, 0.0, ALU.mult, ALU.add), ["xr0", "xr1", "s5t"], ["xr0", "xr1"])
                P.op("dve", lambda e, d0=d0: e.tensor_tensor_scan(xi, d0, xi, 0.0, ALU.mult, ALU.add), ["xi0", "xi1", "s5t"], ["xi0", "xi1"])
                if stop < 4:
                    continue
                erh, eih = Er[:, asl, :], Ei[:, asl, :]
                P.op("pool", lambda e, erh=erh: e.tensor_tensor(t1.rearrange("p (a j) -> p a j", j=128), x3r, erh, ALU.mult), ["xr0", "xr1", "s5t", "t10", "t11"], ["t10", "t11"])
                P.op("pool", lambda e, eih=eih: e.tensor_tensor(t2.rearrange("p (a j) -> p a j", j=128), x3i, eih, ALU.mult), ["xi0", "xi1", "s5t", "t20", "t21"], ["t20", "t21"])
                P.op("dve", lambda e: e.tensor_tensor(hrb.bitcast(F32R), t1.rearrange("p (a j) -> p a j", j=128), t2.rearrange("p (a j) -> p a j", j=128), ALU.add),
                     ["t10", "t11", "t20", "t21"], ["hr"])
                P.op("pool", lambda e, erh=erh: e.tensor_tensor(t1.rearrange("p (a j) -> p a j", j=128), x3i, erh, ALU.mult), ["xi0", "xi1", "s5t", "t10", "t11", "hr"], ["t10", "t11"])
                P.op("pool", lambda e, eih=eih: e.tensor_tensor(t2.rearrange("p (a j) -> p a j", j=128), x3r, eih, ALU.mult), ["xr0", "xr1", "s5t", "t20", "t21", "hr"], ["t20", "t21"])
                P.op("dve", lambda e: e.tensor_tensor(hib.bitcast(F32R), t1.rearrange("p (a j) -> p a j", j=128), t2.rearrange("p (a j) -> p a j", j=128), ALU.subtract),
                     ["t10", "t11", "t20", "t21"], ["hi"])
                if stop < 5:
                    continue
                P.op("dve", lambda e, asl=asl: e.tensor_copy(sv(HLR)[:, asl], hrb[:, :, 127]), ["hr", "s5p"], ["s5p"])
                P.op("dve", lambda e, asl=asl: e.tensor_copy(sv(HLI)[:, asl], hib[:, :, 127]), ["hi", "s5p"], ["s5p"])
                for (dst, x1, c1, x2, c2, op_) in ((GPR, HLR, CS, HLI, SN, ALU.subtract), (GPI, HLR, SN, HLI, CS, ALU.add)):
                    P.op("dve", lambda e, asl=asl, x1=x1, c1=c1: e.tensor_tensor(sv(TMP)[:, asl], sv(x1)[:, asl], sv(c1)[:, asl], ALU.mult), ["s5p"], ["s5p"])
                    P.op("dve", lambda e, asl=asl, x2=x2, c2=c2: e.tensor_tensor(sv(TMP2)[:, asl], sv(x2)[:, asl], sv(c2)[:, asl], ALU.mult), ["s5p"], ["s5p"])
                    P.op("dve", lambda e, asl=asl, dst=dst, op_=op_: e.tensor_tensor(sv(dst)[:, asl], sv(TMP)[:, asl], sv(TMP2)[:, asl], op_), ["s5p"], ["s5p"])
                if stop < 6:
                    continue
                for ql in range(2):
                    q = 2 * hf + ql
                    dst = PY[:, q * 128:(q + 1) * 128]
                    n_ = 0
                    for a4 in range(4):
                        a8 = ql * 4 + a4
                        a = hf * 8 + a8
                        for c, hb, tg in ((0, hrb, "hr"), (1, hib, "hi")):
                            P.op("pe", lambda e, dst=dst, c=c, a=a, a8=a8, hb=hb, n_=n_: e.matmul(dst, wc[:, c, a, :].bitcast(F32R), hb[:, a8, :].bitcast(F32R),
                                                                                          start=(n_ == 0), stop=(n_ == 7), skip_group_check=True),
                                 ["wc", tg], ["py"])
                            n_ += 1
            if stop < 7:
                continue
            bs = (b % 4) * 128
            for q in range(4):
                P.op("dve", lambda e, q=q, tsl=tsl: e.scalar_tensor_tensor(t1[:, q * 128:(q + 1) * 128], uT[:, q, tsl], self.s5_dT[:, 4 * l + q:4 * l + q + 1], PY[:, q * 128:(q + 1) * 128], ALU.mult, ALU.add),
                     ["uT", "py", "s5d", "t10", "t11"], ["t10"])
                P.op("act", lambda e, q=q, bs=bs: e.activation(out=yg[:, q, bs:bs + 128].bitcast(F32R), in_=t1[:, q * 128:(q + 1) * 128], func=AF.Gelu), ["t10"], ["yg%d" % q])
            if b % 4 == 3:
                t0 = (b // 4) * TB
                for n4 in range(4):
                    pg = PG[n4 % 2]
                    for q in range(4):
                        P.op("pe", lambda e, pg=pg, n4=n4, q=q: e.matmul(pg[:], wg[:, n4, q, :].bitcast(F32R), yg[:, q, :].bitcast(F32R), start=(q == 0), stop=(q == 3)),
                             ["wg"] + ["yg%d" % qq for qq in range(4)], ["pg%d" % (n4 % 2)])
                    sg = t2[:, (n4 % 2) * 512:(n4 % 2 + 1) * 512]
                    P.op("act", lambda e, pg=pg, sg=sg, n4=n4: e.activation(out=sg, in_=pg[:], func=AF.Sigmoid, bias=self.s5_gbT[:, 4 * l + n4:4 * l + n4 + 1], scale=1.0),
                         ["pg%d" % (n4 % 2), "s5d", "t20", "t21"], ["t2%d" % (n4 % 2)])
                    P.op("dve", lambda e, sg=sg, n4=n4: e.tensor_tensor(sg, sg, yg[:, n4, :], ALU.mult), ["t2%d" % (n4 % 2), "yg%d" % n4], ["t2%d" % (n4 % 2)])
                    self.out_rows(sg, 512 + n4 * 128, t0, TB, "t2%d" % (n4 % 2), q=("sp" if n4 % 2 else "act"))


    def mix_rwkv(self, l):
        P, S = self.P, self.S
        HS = min(S, 1024)
        NH = S // HS
        NSB = HS // 128
        NCH = HS // 64
        E05 = math.exp(-0.5)
        RA, FA = self.A, self.AF
        KT, BT, KK, RT, VT, GT, BV = [RA(i * HS, HS) for i in range(7)]
        ro = 7 * HS
        W2p, A2p, G2w = RA(ro, 512), RA(ro + 512, 512), RA(ro + 1024, 512)
        XWA = RA(ro + 1536, HS)
        XG = RA(ro + 1536 + HS, HS)
        mo = ro + 1536 + 2 * HS
        BTm = [RA(mo + h * 128, 128) for h in range(2)]
        KKm = [RA(mo + 256 + h * 128, 128) for h in range(2)]
        KTm = [RA(mo + 512 + h * 128, 128) for h in range(2)]
        RTm = [RA(mo + 768 + h * 128, 128) for h in range(2)]
        blk = RA(mo + 1024, 128)
        rst = RA(mo + 1152, HS)
        tqr = RA(mo + 1152 + HS, 512)
        assert mo + 1152 + HS + 512 <= ARENA_R
        T = [FA(i * HS, HS) for i in range(4)]
        fo = 4 * HS
        def ft(n):
            nonlocal fo
            r = FA(fo, n)
            fo += n
            return r
        Am, ATm, AkTm, BrbTm, BrkTm = [ft(256) for _ in range(5)]
        Pk = [ft(256) for _ in range(2)]
        PTk = [ft(256) for _ in range(2)]
        MTk = [ft(256) for _ in range(2)]
        tok = [ft(128) for _ in range(3)]
        Vc = [ft(128) for _ in range(2)]
        Vpad = [ft(128) for _ in range(2)]
        Upad = [[ft(128) for _ in range(2)] for _ in range(2)]
        Wsb = ft(128)
        WkV = ft(128)
        Pbd = ft(128)
        Ptmp = ft(128)
        blkm = ft(128)
        gam = ft(NCH)
        msk = ft(3 * 256).rearrange("p (a b) -> p a b", b=256)
        idn2 = ft(256)
        OTs = ft(512)
        tq = [ft(512) for _ in range(3)]
        assert fo <= ARENA_F, fo
        rp = self.rwp
        MU, W0, A0, KKs, KA, OMKA, RK, GNG, GNB = 0, 14, 18, 22, 26, 30, 34, 38, 42
        col = lambda base, i: rp[:, base + i:base + i + 1]
        hm = self.rw_hm
        P.dma(rp[:, 0:14], self.rw_muT[l], writes=["rwp"])
        for base, src in ((W0, self.rw_w0T), (A0, self.rw_a0T), (KKs, self.rw_kkT), (KA, self.rw_kaT), (RK, self.rw_rkT), (GNG, self.rw_gngT), (GNB, self.rw_gnbT)):
            P.dma(rp[:, base:base + 4], src[l], writes=["rwp"])
        P.op("dve", lambda e: e.tensor_scalar(rp[:, OMKA:OMKA + 4], rp[:, KA:KA + 4], -1.0, 1.0, ALU.mult, ALU.add), ["rwp"], ["rwp"])
        P.dma(W2p.bitcast(F32R), self.rw_w2p[l].bitcast(F32R), writes=["rww"], q="pool")
        P.dma(A2p.bitcast(F32R), self.rw_a2p[l].bitcast(F32R), writes=["rww"], q="pool")
        P.dma(G2w.bitcast(F32R), self.rw_g2[l].bitcast(F32R), writes=["rww"], q="pool")
        P.dma(blk.bitcast(F32R), self.blk64_d.bitcast(F32R), writes=["blk"], q="pool")
        P.dma(rst.bitcast(F32R), self.rst64_d[:, 0:HS].bitcast(F32R), writes=["rst"], q="pool")
        P.dma(msk, self.rwmask_d.rearrange("a p t -> p a t"), writes=["msk"])
        P.dma(idn2[:, 0:128], self.ident_d, writes=["idn2"])
        P.dma(idn2[:, 128:256], self.ident_d, writes=["idn2"])
        P.dma(blkm, self.blk64_d, writes=["blkm"])
        for t_ in Vc + Vpad + Upad[0] + Upad[1] + [Wsb]:
            P.op("dve", lambda e, t_=t_: e.memset(t_, 0.0), [], ["small"])

        def shifted(dst, dstp, tagX, tagP, row0, t0):
            P.dma(dst.bitcast(F32R), self.projT[row0:row0 + 128, t0:t0 + HS].bitcast(F32R), writes=[tagX], q="pool")
            if t0 == 0:
                P.op("dve", lambda e: e.memset(dstp[:, 0:1], 0.0), [], [tagP])
                P.dma(dstp[:, 1:HS], self.projT[row0:row0 + 128, 0:HS - 1], writes=[tagP])
            else:
                P.dma(dstp, self.projT[row0:row0 + 128, t0 - 1:t0 + HS - 1], writes=[tagP])

        def tshift(X, Xp, mucol, tagX, tagP):
            P.op("pool", lambda e: e.tensor_tensor(Xp, Xp, X, ALU.subtract), [tagX, tagP], [tagP])
            P.op("dve", lambda e: e.scalar_tensor_tensor(X.bitcast(F32R), Xp, mucol, X, ALU.mult, ALU.add), [tagX, tagP, "rwp"], [tagX])

        RW0 = 4096
        import os as _os
        rstop = int(_os.environ.get("RW_STOP", 99))
        for hp in range(4 if rstop > 0 else 0):
            P.op("dve", lambda e: e.memset(Pbd, 0.0), [], ["Pbd"])
            for half in range(NH):
                t0 = half * HS
                shifted(XWA, T[0], "XWA", "T0", RW0 + 1536, t0)
                tshift(XWA, T[0], col(MU, 12), "XWA", "T0")
                P.op("act", lambda e: e.activation(out=XWA[0:64].bitcast(F32R), in_=XWA[0:64], func=AF.Tanh), ["XWA"], ["XWA"])
                shifted(XG, T[1], "XG", "T1", RW0 + 1664, t0)
                tshift(XG, T[1], col(MU, 13), "XG", "T1")
                P.op("act", lambda e: e.activation(out=XG.bitcast(F32R), in_=XG, func=AF.Sigmoid), ["XG"], ["XG"])
                csl = slice(hp * 128, (hp + 1) * 128)
                for g4 in range(HS // 512):
                    gs = slice(g4 * 512, (g4 + 1) * 512)
                    pw_, pa_, pg_ = self.psb[0], self.psb[1], self.psb[2]
                    P.op("pe", lambda e, gs=gs, csl=csl, pw_=pw_: e.matmul(pw_[:], W2p[:, csl].bitcast(F32R), XWA[:, gs].bitcast(F32R), start=True, stop=True), ["rww", "XWA"], ["ps0"])
                    P.op("pe", lambda e, gs=gs, csl=csl, pa_=pa_: e.matmul(pa_[:], A2p[:, csl].bitcast(F32R), XWA[:, gs].bitcast(F32R), start=True, stop=True), ["rww", "XWA"], ["ps1"])
                    P.op("pe", lambda e, gs=gs, csl=csl, pg_=pg_: e.matmul(pg_[:], G2w[:, csl].bitcast(F32R), XG[:, gs].bitcast(F32R), start=True, stop=True), ["rww", "XG"], ["ps2"])
                    P.op("act", lambda e, gs=gs, pw_=pw_, hp=hp: e.activation(out=T[2][:, gs], in_=pw_[:], func=AF.Sigmoid, bias=col(W0, hp), scale=1.0), ["ps0", "rwp"], ["T2"])
                    P.op("act", lambda e, gs=gs, pa_=pa_, hp=hp: e.activation(out=T[3][:, gs], in_=pa_[:], func=AF.Sigmoid, bias=col(A0, hp), scale=1.0), ["ps1", "rwp"], ["T3"])
                    P.op("act", lambda e, gs=gs, pg_=pg_: e.copy(GT[:, gs].bitcast(F32R), pg_[:]), ["ps2"], ["GT"])
                P.op("dve", lambda e: e.tensor_scalar(T[2], T[2], -E05, None, ALU.mult), ["T2"], ["T2"])
                shifted(KK, T[1], "KK", "T1", RW0 + 512 + hp * 128, t0)
                tshift(KK, T[1], col(MU, 4 + hp), "KK", "T1")
                P.op("dve", lambda e, hp=hp: e.tensor_scalar(T[0], KK, col(KKs, hp), None, ALU.mult), ["KK", "rwp"], ["T0"])
                for g4 in range(HS // 512):
                    gs = slice(g4 * 512, (g4 + 1) * 512)
                    pq = self.psb[g4 % 2]
                    P.op("pool", lambda e, gs=gs: e.tensor_tensor(tqr.bitcast(F32R), T[0][:, gs], T[0][:, gs], ALU.mult), ["T0"], ["tqr"])
                    P.op("pe", lambda e, pq=pq: e.matmul(pq[:], blk.bitcast(F32R), tqr.bitcast(F32R), start=True, stop=True), ["blk", "tqr"], ["ps%d" % (g4 % 2)])
                    P.op("dve", lambda e, pq=pq: e.tensor_scalar_max(tq[1], pq[:], 1e-24), ["ps%d" % (g4 % 2)], ["tq1"])
                    P.op("act", lambda e: e.activation(out=tq[1], in_=tq[1], func=AF.Sqrt), ["tq1"], ["tq1"])
                    P.op("dve", lambda e: e.reciprocal(tq[1], tq[1]), ["tq1"], ["tq1"])
                    P.op("dve", lambda e, gs=gs: e.tensor_tensor(T[0][:, gs], T[0][:, gs], tq[1], ALU.mult), ["T0", "tq1"], ["T0"])
                P.op("dve", lambda e: e.tensor_tensor_scan(T[1], rst, T[2], 0.0, ALU.mult, ALU.add), ["T2", "rst"], ["T1"])
                c3 = lambda t_: t_.rearrange("p (c t) -> p c t", t=64)
                P.op("act", lambda e: e.activation(out=gam, in_=c3(T[1])[:, :, 63], func=AF.Exp), ["T1"], ["gam"])
                P.op("pool", lambda e: e.tensor_tensor(T[2], T[1], T[2], ALU.subtract), ["T1", "T2"], ["T2"])
                P.op("act", lambda e: e.activation(out=T[2], in_=T[2], func=AF.Exp), ["T2"], ["T2"])
                P.op("dve", lambda e: e.tensor_tensor(KT.bitcast(F32R), T[0], T[2], ALU.mult), ["T0", "T2"], ["KT"])
                P.op("act", lambda e: e.activation(out=T[2], in_=T[1], func=AF.Exp, scale=-1.0), ["T1"], ["T2"])
                P.op("pool", lambda e: e.tensor_tensor(T[0], T[0], T[3], ALU.mult), ["T0", "T3"], ["T0"])
                P.op("dve", lambda e: e.scalar_tensor_tensor(BT.bitcast(F32R), T[0], -1.0, T[2], ALU.mult, ALU.mult), ["T0", "T2"], ["BT"])
                P.op("dve", lambda e, hp=hp: e.tensor_scalar(T[3], T[3], col(KA, hp), col(OMKA, hp), ALU.mult, ALU.add), ["T3", "rwp"], ["T3"])
                P.op("pool", lambda e: e.tensor_tensor(T[0], KK, T[3], ALU.mult), ["KK", "T3"], ["T0"])
                P.op("dve", lambda e: e.tensor_tensor(KK.bitcast(F32R), T[0], T[2], ALU.mult), ["T0", "T2"], ["KK"])
                shifted(RT, T[3], "RT", "T3", RW0 + hp * 128, t0)
                tshift(RT, T[3], col(MU, hp), "RT", "T3")
                P.op("pool", lambda e: e.tensor_tensor(T[0], T[0], RT, ALU.mult), ["T0", "RT"], ["T0"])
                P.op("dve", lambda e, hp=hp: e.tensor_scalar(T[0], T[0], col(RK, hp), None, ALU.mult), ["T0", "rwp"], ["T0"])
                P.op("act", lambda e: e.activation(out=T[2], in_=T[1], func=AF.Exp), ["T1"], ["T2"])
                P.op("dve", lambda e: e.tensor_tensor(RT.bitcast(F32R), RT, T[2], ALU.mult), ["RT", "T2"], ["RT"])
                shifted(VT, T[3], "VT", "T3", RW0 + 1024 + hp * 128, t0)
                tshift(VT, T[3], col(MU, 8 + hp), "VT", "T3")
                for g4 in range(HS // 512):
                    gs = slice(g4 * 512, (g4 + 1) * 512)
                    pq = self.psb[g4 % 2]
                    P.op("act", lambda e, gs=gs: e.copy(tqr.bitcast(F32R), T[0][:, gs]), ["T0"], ["tqr"])
                    P.op("pe", lambda e, pq=pq: e.matmul(pq[:], blk.bitcast(F32R), tqr.bitcast(F32R), start=True, stop=True), ["blk", "tqr"], ["ps%d" % (g4 % 2)])
                    P.op("dve", lambda e, pq=pq, gs=gs: e.tensor_tensor(BV[:, gs].bitcast(F32R), pq[:], VT[:, gs], ALU.mult), ["ps%d" % (g4 % 2), "VT"], ["BV"])
                for sb in range(NSB if rstop > 1 else 0):
                    ts_ = slice(sb * 128, (sb + 1) * 128)
                    pA, pB, pC = self.psb[0], self.psb[1], self.psb[2]
                    psm, ptr, pO = self.psb[3], self.psb[4], self.psb[5 + sb % 2]
                    tO = "ps%d" % (5 + sb % 2)
                    for h in range(2):
                        for dst_, src_, tg in ((BTm, BT, "BT"), (KKm, KK, "KK"), (KTm, KT, "KT"), (RTm, RT, "RT")):
                            P.op("pool" if h else "dve", lambda e, dst_=dst_, src_=src_, h=h, ts_=ts_: e.tensor_scalar(dst_[h].bitcast(F32R), src_[:, ts_], hm[:, h:h + 1], None, ALU.mult),
                                 [tg, "hm"], ["m%s%d" % (tg, h)])
                    for i_, (src_, tg) in enumerate(((BT, "BT"), (KK, "KK"), (VT, "VT"))):
                        P.op("pe", lambda e, i_=i_, src_=src_, ts_=ts_: e.transpose(ptr[:, i_ * 128:(i_ + 1) * 128], src_[:, ts_], self.ident[:]), [tg, "ident"], ["ps4"])
                    for i_ in range(3):
                        self.copy("act", tok[i_], ptr[:, i_ * 128:(i_ + 1) * 128], ["ps4"], ["tok%d" % i_])
                    for c in range(2):
                        P.op("pool", lambda e, c=c: e.tensor_copy(Vc[c][64 * c:64 * c + 64, :], tok[2][64 * c:64 * c + 64, :]), ["tok2", "small"], ["Vc%d" % c])
                        P.op("pool", lambda e, c=c: e.tensor_copy(Vpad[c][:, 64 * c:64 * c + 64], tok[2][:, 64 * c:64 * c + 64]), ["tok2", "small"], ["Vpad%d" % c])
                    if rstop < 3:
                        continue
                    for h in range(2):
                        hs = slice(h * 128, (h + 1) * 128)
                        P.op("pe", lambda e, h=h, hs=hs, ts_=ts_: e.matmul(pA[:, hs], BTm[h].bitcast(F32R), KT[:, ts_].bitcast(F32R), start=True, stop=True, skip_group_check=True), ["mBT%d" % h, "KT"], ["ps0"])
                        P.op("pe", lambda e, h=h, hs=hs, ts_=ts_: e.matmul(pA[:, 256 + h * 128:384 + h * 128], KTm[h].bitcast(F32R), BT[:, ts_].bitcast(F32R), start=True, stop=True, skip_group_check=True), ["mKT%d" % h, "BT"], ["ps0"])
                        P.op("pe", lambda e, h=h, hs=hs, ts_=ts_: e.matmul(pB[:, hs], KKm[h].bitcast(F32R), KT[:, ts_].bitcast(F32R), start=True, stop=True, skip_group_check=True), ["mKK%d" % h, "KT"], ["ps1"])
                        P.op("pe", lambda e, h=h, hs=hs, ts_=ts_: e.matmul(pC[:, hs], BTm[h].bitcast(F32R), RT[:, ts_].bitcast(F32R), start=True, stop=True, skip_group_check=True), ["mBT%d" % h, "RT"], ["ps2"])
                        P.op("pe", lambda e, h=h, hs=hs, ts_=ts_: e.matmul(pC[:, 256 + h * 128:384 + h * 128], KKm[h].bitcast(F32R), RT[:, ts_].bitcast(F32R), start=True, stop=True, skip_group_check=True), ["mKK%d" % h, "RT"], ["ps2"])
                    P.op("dve", lambda e: e.tensor_tensor(ATm, pA[:, 0:256], msk[:, 0, :], ALU.mult), ["ps0", "msk"], ["ATm"])
                    P.op("dve", lambda e: e.tensor_tensor(Am, pA[:, 256:512], msk[:, 1, :], ALU.mult), ["ps0", "msk"], ["Am"])
                    P.op("dve", lambda e: e.tensor_tensor(AkTm, pB[:, 0:256], msk[:, 0, :], ALU.mult), ["ps1", "msk"], ["AkTm"])
                    P.op("dve", lambda e: e.tensor_tensor(BrbTm, pC[:, 0:256], msk[:, 2, :], ALU.mult), ["ps2", "msk"], ["BrbTm"])
                    P.op("dve", lambda e: e.tensor_tensor(BrkTm, pC[:, 256:512], msk[:, 2, :], ALU.mult), ["ps2", "msk"], ["BrkTm"])
                    if rstop < 4:
                        continue
                    P.op("pool", lambda e: e.tensor_tensor(MTk[0], ATm, idn2, ALU.add), ["ATm", "idn2"], ["MT0"])
                    curP, curPT, curM = Am, ATm, 0
                    for lev in range(5):
                        last = lev == 4
                        nP, nPT = Pk[lev % 2], PTk[lev % 2]
                        for h in range(2):
                            hs = slice(h * 128, (h + 1) * 128)
                            P.op("pe", lambda e, hs=hs, curP=curP, curPT=curPT: e.matmul(pA[:, hs], curPT[:, hs], curP[:, hs], start=True, stop=True, skip_group_check=True),
                                 ["Pc", "ATm", "Am"], ["ps0"])
                            if not last:
                                P.op("pe", lambda e, h=h, curP=curP, curPT=curPT, hs=hs: e.matmul(pA[:, 256 + h * 128:384 + h * 128], curP[:, hs], curPT[:, hs], start=True, stop=True, skip_group_check=True),
                                     ["Pc", "ATm", "Am"], ["ps0"])
                        self.copy("act", nP, pA[:, 0:256], ["ps0"], ["Pc"])
                        if not last:
                            self.copy("dve", nPT, pA[:, 256:512], ["ps0"], ["Pc"])
                        for h in range(2):
                            hs = slice(h * 128, (h + 1) * 128)
                            P.op("pe", lambda e, hs=hs, nP=nP, curM=curM: e.matmul(pB[:, hs], nP[:, hs], MTk[curM][:, hs], start=True, stop=True, skip_group_check=True),
                                 ["Pc", "MT%d" % curM], ["ps1"])
                        P.op("dve", lambda e, curM=curM: e.tensor_tensor(MTk[1 - curM], MTk[curM], pB[:, 0:256], ALU.add), ["ps1", "MT%d" % curM], ["MT%d" % (1 - curM)])
                        curP, curPT, curM = nP, nPT, 1 - curM
                    MT = MTk[curM]
                    tM = "MT%d" % curM
                    if rstop < 5:
                        continue
                    for h in range(2):
                        P.op("pe", lambda e, h=h: e.matmul(psm[:, h * 64:(h + 1) * 64], AkTm[:, h * 128:(h + 1) * 128], tok[2][:, h * 64:(h + 1) * 64], start=True, stop=True, skip_group_check=True),
                             ["AkTm", "tok2"], ["ps3"])
                    self.copy("act", WkV, psm[:, 0:128], ["ps3"], ["WkV"])
                    rsub = int(_os.environ.get("RW_SUB", 99))
                    if rsub < 2:
                        continue
                    for h in range(2):
                        P.op("pe", lambda e, pO=pO, h=h: e.matmul(pO[:, 0:128], Vpad[h], BrkTm[:, h * 128:(h + 1) * 128], start=(h == 0), stop=False, skip_group_check=True),
                             ["Vpad%d" % h, "BrkTm"], [tO])
                    for c in range(2 if rsub >= 3 else 0):
                        cs = slice(64 * c, 64 * c + 64)
                        tcs = slice(sb * 128 + 64 * c, sb * 128 + 64 * c + 64)
                        P.op("pe", lambda e, pO=pO, cs=cs, tcs=tcs: e.matmul(pO[:, cs], Pbd, RT[:, tcs], start=False, stop=False, skip_group_check=True), ["Pbd", "RT"], [tO])
                        P.op("pe", lambda e, ts_=ts_: e.matmul(psm[:, 128:256], KT[:, ts_], Pbd, start=True, stop=True, skip_group_check=True), ["Pbd", "KT"], ["ps3"])
                        P.op("dve", lambda e, cs=cs: e.tensor_tensor(Wsb[cs, :], psm[cs, 128:256], WkV[cs, :], ALU.add), ["ps3", "WkV", "small"], ["Wsb"])
                        if rsub < 4:
                            continue
                        for h in range(2):
                            P.op("pe", lambda e, h=h, MT=MT: e.matmul(psm[:, 256 + h * 64:320 + h * 64], MT[:, h * 128:(h + 1) * 128], Wsb[:, h * 64:(h + 1) * 64],
                                                                  start=True, stop=True, skip_group_check=True),
                                 [tM, "Wsb"], ["ps3"])
                        for h in range(2):
                            self.copy("dve", Upad[c][h][cs, 64 * h:64 * h + 64], psm[cs, 256 + h * 64:320 + h * 64], ["ps3", "small"], ["Up%d%d" % (c, h)])
                        if rsub < 5:
                            continue
                        for h in range(2):
                            P.op("pe", lambda e, pO=pO, h=h, c=c: e.matmul(pO[:, 0:128], Upad[c][h], BrbTm[:, h * 128:(h + 1) * 128], start=False, stop=(c == 1 and h == 1), skip_group_check=True),
                                 ["Up%d%d" % (c, h), "BrbTm"], [tO])
                        if rsub < 6:
                            continue
                        P.op("pe", lambda e, c=c: e.matmul(psm[:, 384:512], tok[0], Upad[c][0], start=True, stop=False, skip_group_check=True), ["tok0", "Up%d0" % c], ["ps3"])
                        P.op("pe", lambda e, c=c: e.matmul(psm[:, 384:512], tok[0], Upad[c][1], start=False, stop=False, skip_group_check=True), ["tok0", "Up%d1" % c], ["ps3"])
                        P.op("pe", lambda e, c=c: e.matmul(psm[:, 384:512], tok[1], Vc[c], start=False, stop=True, skip_group_check=True), ["tok1", "Vc%d" % c], ["ps3"])
                        gi = sb * 2 + c
                        P.op("dve", lambda e, gi=gi: e.scalar_tensor_tensor(Ptmp, psm[:, 384:512], gam[:, gi:gi + 1], blkm, ALU.mult, ALU.mult), ["ps3", "gam", "blkm"], ["Ptmp"])
                        P.op("dve", lambda e, gi=gi: e.scalar_tensor_tensor(Pbd, Pbd, gam[:, gi:gi + 1], Ptmp, ALU.mult, ALU.add), ["Pbd", "gam", "Ptmp"], ["Pbd"])
                    self.copy("act", OTs[:, (sb % 4) * 128:(sb % 4 + 1) * 128], pO[:, 0:128], [tO], ["OTs"])
                    if sb % 4 == 3 and rstop > 5:
                        gs = slice((sb // 4) * 512, (sb // 4 + 1) * 512)
                        pm, pq2 = self.psb[0], self.psb[1]
                        P.op("pool", lambda e: e.tensor_tensor(tq[0], OTs, OTs, ALU.mult), ["OTs", "tq0"], ["tq0"])
                        P.op("pe", lambda e, pm=pm: e.matmul(pm[:], blk, OTs, start=True, stop=True), ["blk", "OTs"], ["ps0"])
                        P.op("pe", lambda e, pq2=pq2: e.matmul(pq2[:], blk, tq[0], start=True, stop=True), ["blk", "tq0"], ["ps1"])
                        P.op("act", lambda e, pm=pm: e.activation(out=tq[1], in_=pm[:], func=AF.Copy, scale=1.0 / 64.0), ["ps0"], ["tq1"])
                        P.op("dve", lambda e: e.tensor_tensor(tq[2], tq[1], tq[1], ALU.mult), ["tq1", "tq2"], ["tq2"])
                        P.op("dve", lambda e, pq2=pq2: e.scalar_tensor_tensor(tq[2], pq2[:], 1.0 / 64.0, tq[2], ALU.mult, ALU.subtract), ["ps1", "tq2"], ["tq2"])
                        P.op("act", lambda e: e.activation(out=tq[2], in_=tq[2], func=AF.Sqrt, bias=self.gnepsT[:], scale=1.0), ["tq2", "gneps"], ["tq2"])
                        P.op("dve", lambda e: e.reciprocal(tq[2], tq[2]), ["tq2"], ["tq2"])
                        P.op("pool", lambda e: e.tensor_tensor(tq[1], OTs, tq[1], ALU.subtract), ["OTs", "tq1"], ["tq1"])
                        P.op("dve", lambda e: e.tensor_tensor(tq[1], tq[1], tq[2], ALU.mult), ["tq1", "tq2"], ["tq1"])
                        P.op("dve", lambda e, hp=hp: e.tensor_scalar(tq[1], tq[1], col(GNG, hp), col(GNB, hp), ALU.mult, ALU.add), ["tq1", "rwp"], ["tq1"])
                        P.op("pool", lambda e, gs=gs: e.tensor_tensor(tq[1], tq[1], BV[:, gs], ALU.add), ["tq1", "BV"], ["tq1"])
                        P.op("dve", lambda e, gs=gs: e.tensor_tensor(tq[1], tq[1], GT[:, gs], ALU.mult), ["tq1", "GT"], ["tq1"])
                        self.out_rows(tq[1], 1536 + hp * 128, t0 + (sb // 4) * 512, 512, "tq1")

    def ln_store(self, yt, g_bc, b_bc, dst_rows, tagy):
        P = self.P
        st = self.AF(12288, 24).rearrange("p (a b) -> p a b", b=6)
        mv = self.AF(12288 + 24, 2)
        rs = self.AF(12288 + 26, 1)
        for c in range(4):
            P.op("dve", lambda e, c=c: e.bn_stats(st[:, c, :], yt[:, c * 512:(c + 1) * 512]), [tagy], ["lnst%d" % c])
        P.op("dve", lambda e: e.bn_aggr(mv, st), ["lnst%d" % c for c in range(4)], ["lnmv"])
        P.op("act", lambda e: e.activation(out=rs, in_=mv[:, 1:2], func=AF.Sqrt, bias=self.epsT[:], scale=1.0), ["lnmv", "epsT"], ["lnrs"])
        P.op("dve", lambda e: e.reciprocal(rs, rs), ["lnrs"], ["lnrs"])
        P.op("dve", lambda e: e.tensor_scalar(yt, yt, mv[:, 0:1], rs, ALU.subtract, ALU.mult), [tagy, "lnmv", "lnrs"], [tagy])
        P.op("pool", lambda e: e.tensor_tensor(yt, yt, g_bc, ALU.mult), [tagy, "lngb"], [tagy])
        P.op("pool", lambda e: e.tensor_tensor(yt, yt, b_bc, ALU.add), [tagy, "lngb"], [tagy])
        P.dma(dst_rows, yt, reads=[tagy])

    def back_to_tokens(self, yT, tagT, xsrc, t0, g_bc, b_bc, dst, tagdiv=1):
        P = self.P
        for i in range(4):
            xt = self.AF((i % 2) * 2048, 2048)
            yt = self.AF((2 + i % 2) * 2048, 2048)
            P.dma(xt, xsrc[t0 + i * 128:t0 + (i + 1) * 128, :], writes=["F%d" % (i % 2)], q="act")
            for q4 in range(4):
                ps = self.psb[6 + q4 % 2]
                for n in range(4):
                    k = q4 * 4 + n
                    P.op("pe", lambda e, ps=ps, n=n, k=k, i=i: e.transpose(ps[:, n * 128:(n + 1) * 128], yT[:, k, i * 128:(i + 1) * 128], self.ident[:]),
                         [tagT + str(k // tagdiv), "ident"], ["psy%d" % (q4 % 2)])
                P.op("dve", lambda e, ps=ps, q4=q4, xt=xt, yt=yt: e.scalar_tensor_tensor(yt[:, q4 * 512:(q4 + 1) * 512], xt[:, q4 * 512:(q4 + 1) * 512], ALPHA, ps[:],
                                                                                  ALU.mult, ALU.add),
                     ["psy%d" % (q4 % 2), "F%d" % (i % 2)], ["F%d" % (2 + i % 2)])
            self.ln_store(yt, g_bc, b_bc, dst[t0 + i * 128:t0 + (i + 1) * 128, :], "F%d" % (2 + i % 2))

    def load_gb(self, g, b, l):
        P = self.P
        g_bc = self.AF(8192, 2048)
        b_bc = self.AF(8192 + 2048, 2048)
        P.dma(g_bc, g[l:l + 1, :].to_broadcast([128, D]), writes=["lngb"])
        P.dma(b_bc, b[l:l + 1, :].to_broadcast([128, D]), writes=["lngb"])
        return g_bc, b_bc

    def stage_wout(self, l, xin):
        P, S = self.P, self.S
        off_c = 0
        off_m = off_c + 16384
        off_w = off_m + 8192
        g_bc, b_bc = self.load_gb(self.ln1_g, self.ln1_b, l)
        for tb in range(S // TB):
            t0 = tb * TB
            cT = self.A(off_c + (tb % 2) * 8192, 8192).rearrange("p (a b) -> p a b", b=512)
            ctag = "cT%d" % (tb % 2)
            P.dma(cT.bitcast(F32R), self.catT[:, t0:t0 + TB].rearrange("(a p) t -> p a t", p=128).bitcast(F32R), writes=[ctag], q="pool")
            mT = self.A(off_m, 8192).rearrange("p (a b) -> p a b", b=512)
            for j in range(16):
                wi = (tb * 16 + j) % 3
                wt = self.A(off_w + wi * 2048, 2048)
                P.dma(wt.bitcast(F32R), self.w_out[l, j].bitcast(F32R), writes=["w%d" % wi], q="pool")
                ps = self.psb[2 + j % 4]
                for a in range(16):
                    P.op("pe", lambda e, ps=ps, wt=wt, a=a, cT=cT: e.matmul(ps[:], wt[:, a * 128:(a + 1) * 128].bitcast(F32R), cT[:, a, :].bitcast(F32R),
                                                                        start=(a == 0), stop=(a == 15)),
                         ["w%d" % wi, ctag], ["psm%d" % (j % 4)])
                P.op("act", lambda e, ps=ps, j=j: e.activation(out=mT[:, j, :].bitcast(F32R), in_=ps[:], func=AF.Copy, scale=self.modT[:, 32 + j:33 + j]),
                     ["psm%d" % (j % 4), "modT"], ["mT%d" % j])
            self.back_to_tokens(mT, "mT", xin, t0, g_bc, b_bc, self.xB)

    def stage_ffn(self, l, xout):
        P, S = self.P, self.S
        off_yacc = 0
        off_h = 8192
        off_g = off_h + 8192
        off_w = off_g + 11264
        off_w2 = off_w + 6144
        assert off_w2 + 2 * 2816 <= ARENA_R
        g_bc, b_bc = self.load_gb(self.ln2_g, self.ln2_b, l)
        wc = 0
        for tb in range(S // TB):
            t0 = tb * TB
            hT = self.A(off_h, 8192).rearrange("p (a b) -> p a b", b=512)
            self.load_xT(self.xB, t0, hT, 3, 4, "fh")
            yacc = self.A(off_yacc, 8192).rearrange("p (a b) -> p a b", b=512)
            gT = self.A(off_g, 11264).rearrange("p (a b) -> p a b", b=512)
            for half in range(2):
                for fl in range(22):
                    f = half * 22 + fl
                    pss = []
                    for wi_, wsrc in enumerate((self.w1, self.w3)):
                        wi = wc % 3
                        wc += 1
                        wt = self.A(off_w + wi * 2048, 2048)
                        P.dma(wt.bitcast(F32R), wsrc[l, f].bitcast(F32R), writes=["w%d" % wi], q="pool")
                        ps = self.psb[2 + (2 * fl + wi_) % 4]
                        ptag = "psm%d" % ((2 * fl + wi_) % 4)
                        for a in range(16):
                            P.op("pe", lambda e, ps=ps, wt=wt, a=a: e.matmul(ps[:], wt[:, a * 128:(a + 1) * 128].bitcast(F32R), hT[:, a, :].bitcast(F32R),
                                                                         start=(a == 0), stop=(a == 15)),
                                 ["w%d" % wi, "fh%d" % a], [ptag])
                        pss.append((ps, ptag))
                    sl = self.A(off_w2 + 2 * 2816 - 512, 512) if False else None
                    gt = gT[:, fl, :]
                    P.op("act", lambda e, gt=gt, ps=pss[0][0]: e.activation(out=gt.bitcast(F32R), in_=ps[:], func=AF.Silu), [pss[0][1]], ["g%d" % fl])
                    P.op("dve", lambda e, gt=gt, ps=pss[1][0]: e.tensor_tensor(gt.bitcast(F32R), gt, ps[:], ALU.mult), [pss[1][1], "g%d" % fl], ["g%d" % fl])
                for j in range(16):
                    w2i = (half * 16 + j) % 2
                    w2t = self.A(off_w2 + w2i * 2816, 2816)
                    P.dma(w2t.bitcast(F32R), self.w2[l, j, :, half * 2816:(half + 1) * 2816].bitcast(F32R), writes=["w2_%d" % w2i], q="pool")
                    ps = self.psb[j % 2]
                    ptag = "pst%d" % (j % 2)
                    for fl in range(22):
                        P.op("pe", lambda e, ps=ps, w2t=w2t, fl=fl: e.matmul(ps[:], w2t[:, fl * 128:(fl + 1) * 128].bitcast(F32R), gT[:, fl, :].bitcast(F32R),
                                                                         start=(fl == 0), stop=(fl == 21)),
                             ["w2_%d" % w2i, "g%d" % fl], [ptag])
                    if half == 0:
                        P.op("act", lambda e, ps=ps, j=j: e.activation(out=yacc[:, j, :].bitcast(F32R), in_=ps[:], func=AF.Copy, scale=self.modT[:, 80 + j:81 + j]),
                             [ptag, "modT"], ["R%d" % (j // 4)])
                    else:
                        P.op("dve", lambda e, ps=ps, j=j: e.scalar_tensor_tensor(yacc[:, j, :].bitcast(F32R), ps[:], self.modT[:, 80 + j:81 + j], yacc[:, j, :], ALU.mult, ALU.add),
                             [ptag, "R%d" % (j // 4), "modT"], ["R%d" % (j // 4)])
            self.back_to_tokens(yacc, "R", self.xB, t0, g_bc, b_bc, xout, tagdiv=4)


def prep_common(inp, L):
    f = lambda a: np.ascontiguousarray(np.asarray(a, dtype=np.float32))
    m = {}
    m["ada_w"] = np.stack([wtile(f(inp["ada_w"][l])).reshape(96, 128, 2048) for l in range(L)])
    m["ada_bT"] = vecT(f(inp["ada_b"][:L]))
    m["w_in"] = np.stack([wtile(f(inp["w_in"][l])).reshape(46, 128, 2048) for l in range(L)])
    m["w_out"] = np.stack([wtile(f(inp["w_out"][l])).reshape(16, 128, 2048) for l in range(L)])
    m["w1"] = np.stack([wtile(f(inp["ffn_w1"][l])).reshape(44, 128, 2048) for l in range(L)])
    m["w3"] = np.stack([wtile(f(inp["ffn_w3"][l])).reshape(44, 128, 2048) for l in range(L)])
    m["w2"] = np.stack([wtile(f(inp["ffn_w2"][l])).reshape(16, 128, 44 * 128) for l in range(L)])
    for k in ("ln1_g", "ln1_b", "ln2_g", "ln2_b"):
        m[k] = f(inp[k][:L])
    m["ident"] = np.eye(128, dtype=np.float32)
    sp_, tp_ = np.arange(128)[:, None], np.arange(512)[None, :]
    m["sbmask"] = np.stack([(tp_ > mm * 128 + sp_) for mm in range(4)]).astype(np.float32)
    m["tri"] = (np.arange(128)[:, None] >= np.arange(128)[None, :]).astype(np.float32)
    m["ones"] = np.ones((128, 128), np.float32)
    a_ = np.arange(128)
    m["m32"] = ((a_[:, None] // 32 == a_[None, :] // 32) & (a_[:, None] <= a_[None, :])).astype(np.float32)
    m["rst"] = np.broadcast_to((np.arange(4096) % 32 != 0).astype(np.float32), (128, 4096)).copy()
    m["hg_lgT"] = np.ascontiguousarray(f(inp["hg_lb_logits"]).reshape(4, 4, 128).transpose(2, 0, 1).reshape(128, 16))
    m["s5_dT"] = np.ascontiguousarray(f(inp["s5_d"]).reshape(4, 4, 128).transpose(2, 0, 1).reshape(128, 16))
    m["s5_gbT"] = np.ascontiguousarray(f(inp["s5_glu_b"]).reshape(4, 4, 128).transpose(2, 0, 1).reshape(128, 16))
    m["s5_arT"] = vecT(f(inp["s5_a_re"][:L]).reshape(L, 2048))
    m["s5_aiT"] = vecT(f(inp["s5_a_im"][:L]).reshape(L, 2048))
    m["s5_ldtT"] = vecT(np.repeat(f(inp["s5_log_dt"][:L]), 64, axis=1))
    wb = np.zeros((L, 128, 2, 16, 128), np.float32)
    wcre = np.zeros((L, 128, 16, 128), np.float32)
    wcim = np.zeros((L, 128, 16, 128), np.float32)
    for g in range(32):
        a, gb = g // 2, g % 2
        r0 = 32 * (a % 4) + 16 * gb
        for ci, key in enumerate(("s5_b_re", "s5_b_im")):
            wb[:, r0:r0 + 16, ci, a, gb * 64:(gb + 1) * 64] = f(inp[key][:L, g]).transpose(0, 2, 1)
        c0 = 32 * (a % 4) + 16 * gb
        wcre[:, gb * 64:(gb + 1) * 64, a, c0:c0 + 16] = f(inp["s5_c_re"][:L, g]).transpose(0, 2, 1)
        wcim[:, gb * 64:(gb + 1) * 64, a, c0:c0 + 16] = f(inp["s5_c_im"][:L, g]).transpose(0, 2, 1)
    m["s5_wb"] = wb.reshape(L, 128, 4096)
    m["s5_wcre"] = wcre.reshape(L, 128, 2048)
    m["s5_wcim"] = wcim.reshape(L, 128, 2048)
    m["s5_glu"] = np.stack([wtile(f(inp["s5_glu_w"][l])).reshape(4, 128, 512) for l in range(L)])
    m["hm"] = np.stack([(a_ < 64), (a_ >= 64)], axis=1).astype(np.float32)
    m["blk64"] = (a_[:, None] // 64 == a_[None, :] // 64).astype(np.float32)
    m["rst64"] = np.broadcast_to((np.arange(2048) % 64 != 0).astype(np.float32), (128, 2048)).copy()
    same = a_[:, None] // 64 == a_[None, :] // 64
    m0 = (same & (a_[:, None] < a_[None, :])).astype(np.float32)
    m1 = (same & (a_[:, None] > a_[None, :])).astype(np.float32)
    m2 = (same & (a_[:, None] <= a_[None, :])).astype(np.float32)
    m["rwmask"] = np.stack([np.concatenate([mm, mm], axis=1) for mm in (m0, m1, m2)])
    m["rw_muT"] = vecT(f(inp["rw_mu"][:L]))
    for nm, key in (("rw_w0T", "rw_w0"), ("rw_a0T", "rw_a0"), ("rw_kkT", "rw_k_k"), ("rw_kaT", "rw_k_a"), ("rw_gngT", "rw_gn_g"), ("rw_gnbT", "rw_gn_b")):
        m[nm] = vecT(f(inp[key][:L]))
    m["rw_rkT"] = vecT(f(inp["rw_r_k"][:L]).reshape(L, 512))
    w2p = np.zeros((L, 128, 512), np.float32); w2p[:, 0:64] = f(inp["rw_w2"][:L])
    a2p = np.zeros((L, 128, 512), np.float32); a2p[:, 64:128] = f(inp["rw_a2"][:L])
    m["rw_w2p"], m["rw_a2p"], m["rw_g2"] = w2p, a2p, f(inp["rw_g2"][:L])
    m["hg_ngT"] = np.ascontiguousarray(f(inp["hg_norm_g"]).reshape(4, 4, 128).transpose(2, 0, 1).reshape(128, 16))
    return m


def run(inp, S=4096, L=DEPTH, ncores=8, **bk):
    b = Builder(S, L, **bk)
    nc = b.build()
    common = prep_common(inp, L)
    in_maps = []
    for c in range(ncores):
        m = dict(common)
        m["x"] = np.ascontiguousarray(np.asarray(inp["x"][c, :S], dtype=np.float32))
        m["cT"] = vecT(np.asarray(inp["c"][c], dtype=np.float32))
        in_maps.append(m)
    res = run_bass_kernel_spmd(nc, in_maps, core_ids=list(range(ncores)))
    b.results = res.results
    return np.stack([np.asarray(r["out"]) for r in res.results]).astype(np.float32), b


def kernel(**inputs):
    out, _ = run(inputs)
    return out
```

```python
import math
import numpy as np
import concourse.bass as bass
import concourse.mybir as mybir
from concourse.bass_utils import run_bass_kernel_spmd

F32 = mybir.dt.float32
F32R = mybir.dt.float32r
AF = mybir.ActivationFunctionType
ALU = mybir.AluOpType

D = 2048
NIN = 5888
DFF = 5632
DEPTH = 4
ALPHA = (2 * DEPTH) ** 0.25
LN_EPS = 1e-5
TB = 512
ENGS = ("pe", "act", "dve", "pool", "sp")
N_DMA_SEMS = 48
ARENA_R = 39424
ARENA_F = 12352


class Op:
    __slots__ = ("eng", "fn", "reads", "writes", "is_dma", "waits", "sem", "val", "vc", "idx", "bar")


class Prog:
    def __init__(self):
        self.nc = bass.Bass("TRN2", target_bir_lowering=False)
        self.ops = []
        self._stack = []

    def enter(self, cm):
        v = cm.__enter__()
        self._stack.append(cm)
        return v

    def sb(self, name, shape, dt=F32):
        return self.enter(self.nc.sbuf_tensor(name, list(shape), dt))

    def ps(self, name, shape, dt=F32):
        return self.enter(self.nc.psum_tensor(name, list(shape), dt))

    def op(self, eng, fn, reads=(), writes=(), dma=False):
        o = Op()
        o.eng, o.fn, o.reads, o.writes, o.is_dma, o.bar = eng, fn, tuple(reads), tuple(writes), dma, False
        self.ops.append(o)
        return o

    def dma(self, out, in_, reads=(), writes=(), q="sp", **kw):
        return self.op(q, lambda e: e.dma_start(out=out, in_=in_, **kw), reads, writes, dma=True)

    def barrier(self):
        o = Op()
        o.bar = True
        self.ops.append(o)

    def finish(self):
        nc = self.nc
        sems = {(e, k): self.enter(nc.semaphore("s_%s%d" % (e, k))) for e in ENGS for k in range(3)}
        dsems = [self.enter(nc.semaphore("d%d" % i)) for i in range(N_DMA_SEMS)]
        ep = 0
        cnt = {e: 1 for e in ENGS}
        dcnt = [0] * N_DMA_SEMS
        dlast = [None] * N_DMA_SEMS
        ndma = 0
        nsw = 0
        last_w, readers = {}, {}
        known = {e: {} for e in ENGS}
        per_eng = {e: [("mark", 0)] for e in ENGS}
        for i, o in enumerate(self.ops):
            if o.bar:
                tot = {("e", e, ep): cnt[e] for e in ENGS}
                for k in range(N_DMA_SEMS):
                    if dcnt[k]:
                        tot[("d", k)] = dcnt[k]
                for e in ENGS:
                    w = [(s, v) for s, v in tot.items() if known[e].get(s, 0) < v]
                    per_eng[e].append(("bar", w, ep))
                    known[e] = {s: v for s, v in tot.items() if s[0] == "d"}
                ep += 1
                cnt = {e: 1 for e in ENGS}
                last_w, readers = {}, {}
                continue
            o.idx = i
            deps = []
            if o.is_dma:
                half_ = N_DMA_SEMS // 2
                if o.eng == "pool":
                    k = half_ + nsw % half_
                    nsw += 1
                else:
                    k = ndma % half_
                    ndma += 1
                dcnt[k] += 16
                o.sem, o.val = ("d", k), dcnt[k]
                if dlast[k] is not None:
                    deps.append(dlast[k])
                dlast[k] = o
            else:
                cnt[o.eng] += 1
                o.sem, o.val = ("e", o.eng, ep), cnt[o.eng]
            for r in o.reads:
                w = last_w.get(r)
                if w is not None:
                    deps.append(w)
            for r in o.writes:
                w = last_w.get(r)
                if w is not None:
                    deps.append(w)
                deps.extend(readers.get(r, ()))
            kn = known[o.eng]
            need = {}
            for d in deps:
                if d is o:
                    continue
                if (not d.is_dma) and (not o.is_dma) and d.eng == o.eng == "pe":
                    continue
                if kn.get(d.sem, 0) >= d.val:
                    continue
                if need.get(d.sem, (0, None))[0] < d.val:
                    need[d.sem] = (d.val, d)
            waits = []
            for s_, (v, d) in sorted(need.items(), key=lambda kv: -kv[1][1].idx):
                if kn.get(s_, 0) >= v:
                    continue
                waits.append((s_, v))
                for s2, v2 in d.vc.items():
                    if kn.get(s2, 0) < v2:
                        kn[s2] = v2
                kn[s_] = max(kn.get(s_, 0), v)
            o.waits = waits
            vc = dict(kn)
            vc[o.sem] = o.val
            o.vc = vc
            for r in o.reads:
                readers.setdefault(r, []).append(o)
            for r in o.writes:
                last_w[r] = o
                readers[r] = []
            per_eng[o.eng].append(o)
        self.n_ops = sum(1 for o in self.ops if not o.bar)

        def semof(s_):
            return sems[(s_[1], s_[2] % 3)] if s_[0] == "e" else dsems[s_[1]]

        finals = [(("e", e, ep), cnt[e]) for e in ENGS]
        finals += [(("d", k), dcnt[k]) for k in range(N_DMA_SEMS) if dcnt[k]]
        blk = self.enter(nc.Block())

        def body(ename, with_final=False):
            def f(eng):
                for o in per_eng[ename]:
                    if isinstance(o, tuple):
                        if o[0] == "mark":
                            eng.nop().then_inc(sems[(ename, 0)], 1)
                            continue
                        for s_, v in o[1]:
                            eng.wait_ge(semof(s_), v)
                        e_old = o[2]
                        eng.sem_clear(sems[(ename, (e_old + 2) % 3)])
                        eng.nop().then_inc(sems[(ename, (e_old + 1) % 3)], 1)
                        continue
                    for s_, v in o.waits:
                        eng.wait_ge(semof(s_), v)
                    o.fn(eng).then_inc(semof(o.sem), 16 if o.is_dma else 1)
                if with_final:
                    for s_, v in finals:
                        eng.wait_ge(semof(s_), v)
            return f

        blk.tensor(body("pe"))
        blk.scalar(body("act"))
        blk.vector(body("dve"))
        blk.gpsimd(body("pool"))
        blk.sync(body("sp", with_final=True))
        while self._stack:
            self._stack.pop().__exit__(None, None, None)
        return nc


def wtile(w):
    K, N = w.shape
    return np.ascontiguousarray(w.reshape(K // 128, 128, N // 128, 128).transpose(2, 1, 0, 3))


def vecT(v):
    sh = v.shape
    return np.ascontiguousarray(np.swapaxes(v.reshape(sh[:-1] + (sh[-1] // 128, 128)), -1, -2))


class Builder:
    def __init__(self, S, L, stages=("ada", "proj", "mix", "wout", "ffn"), mix="hgrn+s5+sb+rwkv", dbg=False):
        self.S, self.L, self.stages, self.mixmode, self.dbg = S, L, stages, mix, dbg
        self.P = Prog()
        self.nc = self.P.nc
        self.rr = 0

    def din(self, name, shape):
        return self.nc.dram_tensor(name, list(shape), F32, kind="ExternalInput").ap()

    def dscratch(self, name, shape):
        return self.nc.dram_tensor(name, list(shape), F32, kind="Internal").ap()

    def A(self, off, n):
        return self.arenaR[:, off:off + n]

    def AF(self, off, n):
        return self.arenaF[:, off:off + n]

    def evac_eng(self):
        self.rr += 1
        return "act" if self.rr % 2 else "dve"

    def copy(self, eng, out, in_, reads, writes):
        if eng == "act":
            self.P.op("act", lambda e: e.copy(out, in_), reads, writes)
        else:
            self.P.op(eng, lambda e: e.tensor_copy(out, in_), reads, writes)

    def build(self):
        P, nc, S, L = self.P, self.nc, self.S, self.L
        NB = S // TB
        self.x_in = self.din("x", [S, D])
        self.cT = self.din("cT", [128, 16])
        self.ada_w = self.din("ada_w", [L, 96, 128, 16 * 128])
        self.ada_bT = self.din("ada_bT", [L, 128, 96])
        self.w_in = self.din("w_in", [L, 46, 128, 16 * 128])
        self.w_out = self.din("w_out", [L, 16, 128, 16 * 128])
        self.w1 = self.din("w1", [L, 44, 128, 16 * 128])
        self.w3 = self.din("w3", [L, 44, 128, 16 * 128])
        self.w2 = self.din("w2", [L, 16, 128, 44 * 128])
        self.ln1_g = self.din("ln1_g", [L, D])
        self.ln1_b = self.din("ln1_b", [L, D])
        self.ln2_g = self.din("ln2_g", [L, D])
        self.ln2_b = self.din("ln2_b", [L, D])
        self.ident_d = self.din("ident", [128, 128])
        self.sbmask_d = self.din("sbmask", [4, 128, 512])
        self.tri_d = self.din("tri", [128, 128])
        self.ones_d = self.din("ones", [128, 128])
        self.m32_d = self.din("m32", [128, 128])
        self.rst_d = self.din("rst", [128, 4096])
        self.hg_lgT = self.din("hg_lgT", [128, 16])
        self.hg_ngT = self.din("hg_ngT", [128, 16])
        self.s5_dT_d = self.din("s5_dT", [128, 16])
        self.hm_d = self.din("hm", [128, 2])
        self.blk64_d = self.din("blk64", [128, 128])
        self.rst64_d = self.din("rst64", [128, 2048])
        self.rwmask_d = self.din("rwmask", [3, 128, 256])
        self.rw_muT = self.din("rw_muT", [L, 128, 14])
        self.rw_w0T = self.din("rw_w0T", [L, 128, 4])
        self.rw_a0T = self.din("rw_a0T", [L, 128, 4])
        self.rw_kkT = self.din("rw_kkT", [L, 128, 4])
        self.rw_kaT = self.din("rw_kaT", [L, 128, 4])
        self.rw_rkT = self.din("rw_rkT", [L, 128, 4])
        self.rw_gngT = self.din("rw_gngT", [L, 128, 4])
        self.rw_gnbT = self.din("rw_gnbT", [L, 128, 4])
        self.rw_w2p = self.din("rw_w2p", [L, 128, 512])
        self.rw_a2p = self.din("rw_a2p", [L, 128, 512])
        self.rw_g2 = self.din("rw_g2", [L, 128, 512])
        self.s5_gbT_d = self.din("s5_gbT", [128, 16])
        self.s5_arT = self.din("s5_arT", [L, 128, 16])
        self.s5_aiT = self.din("s5_aiT", [L, 128, 16])
        self.s5_ldtT = self.din("s5_ldtT", [L, 128, 16])
        self.s5_wb = self.din("s5_wb", [L, 128, 2 * 16 * 128])
        self.s5_wcre = self.din("s5_wcre", [L, 128, 16 * 128])
        self.s5_wcim = self.din("s5_wcim", [L, 128, 16 * 128])
        self.s5_glu = self.din("s5_glu", [L, 4, 128, 4 * 128])
        self.out = self.nc.dram_tensor("out", [S, D], F32, kind="ExternalOutput").ap()
        self.xA = self.dscratch("xA", [S, D])
        self.xB = self.dscratch("xB", [S, D])
        if not self.dbg:
            self.projT = self.dscratch("projT", [NIN, S])
        if self.dbg:
            self.catT = self.nc.dram_tensor("catT", [D, S], F32, kind="ExternalOutput").ap()
            self.projT = self.nc.dram_tensor("projT", [NIN, S], F32, kind="ExternalOutput").ap()
        else:
            self.catT = self.dscratch("catT", [D, S])

        self.arenaR = P.sb("arenaR", [128, ARENA_R])
        self.arenaF = P.sb("arenaF", [128, ARENA_F])
        self.ident = P.sb("ident_sb", [128, 128])
        self.modT = P.sb("modT", [128, 96])
        self.cact = P.sb("cact", [128, 16])
        self.epsT = P.sb("epsT", [128, 1])
        self.oneT = P.sb("oneT", [128, 1])
        self.rmsT = P.sb("rmsT", [128, 1])
        self.ones_sb = P.sb("ones_sb", [128, 128])
        self.hg_small = P.sb("hg_small", [128, 3 * (S // 32)])
        self.hg_m32 = P.sb("hg_m32", [128, 128])
        self.hg_lg = P.sb("hg_lg", [128, 4 * 4])
        self.hg_lb = P.sb("hg_lb", [128, 4 * 4])
        self.hg_oml = P.sb("hg_oml", [128, 4 * 4])
        self.hg_ng = P.sb("hg_ng", [128, 4 * 4])
        self.hg_sum = P.sb("hg_sum", [128, 4])
        self.s5p = P.sb("s5p", [128, 20, 16])
        self.rwp = P.sb("rwp", [128, 48])
        self.rw_hm = P.sb("rw_hm", [128, 2])
        self.gnepsT = P.sb("gnepsT", [128, 1])
        self.s5_dT = P.sb("s5_dT_sb", [128, 16])
        self.s5_gbT = P.sb("s5_gbT_sb", [128, 16])
        self.psb = [P.ps("psb%d" % i, [128, 512]) for i in range(8)]
        P.dma(self.ident[:].bitcast(F32R), self.ident_d.bitcast(F32R), writes=["ident"], q="pool")
        P.dma(self.cact[:], self.cT, writes=["cact"])
        P.op("act", lambda e: e.activation(out=self.cact[:], in_=self.cact[:], func=AF.Silu), ["cact"], ["cact"])
        P.op("dve", lambda e: e.memset(self.epsT[:], LN_EPS), [], ["epsT"])
        P.op("dve", lambda e: e.memset(self.oneT[:], 1.0), [], ["oneT"])
        P.op("dve", lambda e: e.memset(self.rmsT[:], 1e-6), [], ["rmsT"])
        P.dma(self.ones_sb[:], self.ones_d, writes=["ones_sb"])
        P.dma(self.hg_lg[:], self.hg_lgT, writes=["hgp"])
        P.dma(self.hg_ng[:], self.hg_ngT, writes=["hgp"])
        P.dma(self.s5_dT[:], self.s5_dT_d, writes=["s5d"])
        P.dma(self.rw_hm[:], self.hm_d, writes=["hm"])
        P.op("dve", lambda e: e.memset(self.gnepsT[:], 64e-5), [], ["gneps"])
        P.dma(self.s5_gbT[:], self.s5_gbT_d, writes=["s5d"])
        lg3 = self.hg_lg[:].rearrange("p (l h) -> p l h", h=4)
        lb3 = self.hg_lb[:].rearrange("p (l h) -> p l h", h=4)
        P.op("act", lambda e: e.activation(out=self.hg_lg[:], in_=self.hg_lg[:], func=AF.Exp), ["hgp"], ["hgp"])
        P.op("dve", lambda e: e.tensor_tensor(self.hg_sum[:], lg3[:, 0, :], lg3[:, 1, :], ALU.add), ["hgp"], ["hgsum"])
        P.op("dve", lambda e: e.tensor_tensor(self.hg_sum[:], self.hg_sum[:], lg3[:, 2, :], ALU.add), ["hgp", "hgsum"], ["hgsum"])
        P.op("dve", lambda e: e.tensor_tensor(self.hg_sum[:], self.hg_sum[:], lg3[:, 3, :], ALU.add), ["hgp", "hgsum"], ["hgsum"])
        P.op("dve", lambda e: e.reciprocal(self.hg_sum[:], self.hg_sum[:]), ["hgsum"], ["hgsum"])
        P.op("dve", lambda e: e.memset(lb3[:, 0, :], 0.0), [], ["hglb"])
        for ll in range(1, 4):
            P.op("dve", lambda e, ll=ll: e.tensor_tensor(lg3[:, ll, :], lg3[:, ll, :], self.hg_sum[:], ALU.mult), ["hgp", "hgsum"], ["hgp"])
            P.op("dve", lambda e, ll=ll: e.tensor_tensor(lb3[:, ll, :], lb3[:, ll - 1, :], lg3[:, ll, :], ALU.add), ["hgp", "hglb"], ["hglb"])
        P.op("dve", lambda e: e.tensor_scalar(self.hg_oml[:], self.hg_lb[:], -1.0, 1.0, ALU.mult, ALU.add), ["hglb"], ["hglb"])
        P.barrier()
        for l in range(L):
            xin = self.x_in if l == 0 else self.xA
            xout = self.out if l == L - 1 else self.xA
            if "ada" in self.stages:
                self.stage_ada(l)
                P.barrier()
            if "proj" in self.stages:
                self.stage_proj(l, xin)
                P.barrier()
            if "mix" in self.stages:
                self.stage_mix(l)
                P.barrier()
            if "wout" in self.stages:
                self.stage_wout(l, xin)
                P.barrier()
            if "ffn" in self.stages:
                self.stage_ffn(l, xout)
                P.barrier()
        return P.finish()

    def stage_ada(self, l):
        P = self.P
        ps = self.psb[0]
        for j in range(96):
            wt = self.AF((j % 3) * 2048, 2048)
            rw = "adaw%d" % (j % 3)
            P.dma(wt, self.ada_w[l, j], writes=[rw], q=("sp" if j % 2 else "act"))
            for a in range(16):
                P.op("pe", lambda e, wt=wt, a=a, j=j: e.matmul(ps[:, j:j + 1], wt[:, a * 128:(a + 1) * 128],
                                                              self.cact[:, a:a + 1], start=(a == 0), stop=(a == 15)),
                     [rw, "cact"], ["ps_ada"])
        bt = self.AF(3 * 2048, 96)
        P.dma(bt, self.ada_bT[l], writes=["adab"])
        P.op("dve", lambda e: e.tensor_tensor(self.modT[:], ps[:, 0:96], bt, ALU.add), ["ps_ada", "adab"], ["modT"])
        for g in (1, 4):
            P.op("dve", lambda e, g=g: e.tensor_scalar_add(self.modT[:, g * 16:(g + 1) * 16], self.modT[:, g * 16:(g + 1) * 16], 1.0),
                 ["modT"], ["modT"])

    def load_xT(self, src, t0, hT, shift_g, scale_g, tag):
        P = self.P
        xt = [self.A(i * 2048, 2048) for i in range(4)]
        for i in range(4):
            P.dma(xt[i].bitcast(F32R), src[t0 + i * 128:t0 + (i + 1) * 128, :].bitcast(F32R), writes=["R%d" % i], q="pool")
        for k in range(16):
            ps = self.psb[k % 2]
            for i in range(4):
                P.op("pe", lambda e, ps=ps, i=i, k=k: e.transpose(ps[:, i * 128:(i + 1) * 128], xt[i][:, k * 128:(k + 1) * 128], self.ident[:]),
                     ["R%d" % i, "ident"], ["pst%d" % (k % 2)])
            eng = "dve" if k % 2 else "pool"
            eng = "dve"
            P.op(eng, lambda e, ps=ps, k=k: e.tensor_scalar(hT[:, k, :].bitcast(F32R), ps[:], self.modT[:, scale_g * 16 + k:scale_g * 16 + k + 1],
                                                            self.modT[:, shift_g * 16 + k:shift_g * 16 + k + 1], ALU.mult, ALU.add),
                 ["pst%d" % (k % 2), "modT"], ["%s%d" % (tag, k)])
        return xt

    def stage_proj(self, l, xin):
        P, S = self.P, self.S
        off_h = 8192
        off_w = off_h + 2 * 8192
        off_o = 0
        for tb in range(S // TB):
            t0 = tb * TB
            hT = self.A(off_h + (tb % 2) * 8192, 8192).rearrange("p (a b) -> p a b", b=512)
            tag = "hT%d_" % (tb % 2)
            self.load_xT(xin, t0, hT, 0, 1, tag)
            for j in range(46):
                wi = (tb * 46 + j) % 3
                wt = self.A(off_w + wi * 2048, 2048)
                P.dma(wt.bitcast(F32R), self.w_in[l, j].bitcast(F32R), writes=["w%d" % wi], q="pool")
                ps = self.psb[2 + j % 4]
                for a in range(16):
                    P.op("pe", lambda e, ps=ps, wt=wt, a=a, hT=hT: e.matmul(ps[:], wt[:, a * 128:(a + 1) * 128].bitcast(F32R), hT[:, a, :].bitcast(F32R),
                                                                        start=(a == 0), stop=(a == 15)),
                         ["w%d" % wi, tag + str(a)], ["psm%d" % (j % 4)])
                oi = j % 4
                ot = self.AF(off_o + oi * 512, 512)
                self.copy(self.evac_eng(), ot, ps[:], ["psm%d" % (j % 4)], ["ot%d" % oi])
                P.dma(self.projT[j * 128:(j + 1) * 128, t0:t0 + TB], ot, reads=["ot%d" % oi], q=("sp" if j % 2 else "act"))

    def stage_mix(self, l):
        if self.mixmode == "stub":
            P, S = self.P, self.S
            for j in range(16):
                for h in range(S // 2048 if S >= 2048 else 1):
                    n = min(S, 2048)
                    t = self.AF((j % 2) * 2048, n)
                    P.dma(t, self.projT[j * 128:(j + 1) * 128, h * n:(h + 1) * n], writes=["mx%d" % (j % 2)])
                    P.dma(self.catT[j * 128:(j + 1) * 128, h * n:(h + 1) * n], t, reads=["mx%d" % (j % 2)])
        else:
            P = self.P
            for nm in self.mixmode.split("+"):
                getattr(self, "mix_" + nm)(l)
                P.barrier()


    def out_rows(self, src_tile_ap, row0, t0, n, tag, q="sp"):
        self.P.dma(self.catT[row0:row0 + 128, t0:t0 + n], src_tile_ap, reads=[tag], q=q)

    def mix_sb(self, l):
        P, S = self.P, self.S
        NBK = S // 128
        NG = S // TB
        ND = 3
        isq = 1.0 / math.sqrt(128.0)
        qT = self.A(0, S)
        kT = self.A(S, S)
        vT = self.A(2 * S, S)
        vtok = self.A(3 * S, S).rearrange("p (a b) -> p a b", b=128)
        wTb = [self.A(4 * S + i * 512, 512) for i in range(ND)]
        sph = [self.A(4 * S + 1536 + i * 512, 512) for i in range(ND)]
        spl = [self.A(4 * S + 3072 + i * 512, 512) for i in range(ND)]
        tri = self.A(4 * S + 4608, 128)
        ones = self.A(4 * S + 4736, 128)
        eb = [self.AF(i * 512, 512) for i in range(ND)]
        spb = [self.AF(1536 + i * 512, 512) for i in range(ND)]
        btb = [self.AF(3072 + i * 512, 512) for i in range(ND)]
        Cb = self.AF(4608, 512)
        ob = self.AF(5120, 512)
        msk = self.AF(5632, 2048).rearrange("p (a b) -> p a b", b=512)
        P.dma(msk, self.sbmask_d.rearrange("a p t -> p a t"), writes=["msk"])
        P.dma(tri.bitcast(F32R), self.tri_d.bitcast(F32R), writes=["tri"], q="pool")
        P.dma(ones.bitcast(F32R), self.ones_d.bitcast(F32R), writes=["ones"], q="pool")
        pa, pc, po = self.psb[0:3], self.psb[3:6], self.psb[6:8]
        it = 0
        for h in range(4):
            P.dma(qT.bitcast(F32R), self.projT[2560 + h * 128:2560 + (h + 1) * 128, :].bitcast(F32R), writes=["qT"], q="pool")
            P.dma(kT.bitcast(F32R), self.projT[3072 + h * 128:3072 + (h + 1) * 128, :].bitcast(F32R), writes=["kT"], q="pool")
            P.dma(vT.bitcast(F32R), self.projT[3584 + h * 128:3584 + (h + 1) * 128, :].bitcast(F32R), writes=["vT"], q="pool")
            P.op("act", lambda e: e.activation(out=kT.bitcast(F32R), in_=kT, func=AF.Copy, scale=-isq), ["kT"], ["kT"])
            for b in range(NBK):
                ps = self.psb[b % 2]
                P.op("pe", lambda e, ps=ps, b=b: e.transpose(ps[:, 0:128], vT[:, b * 128:(b + 1) * 128], self.ident[:]), ["vT", "ident"], ["ps%d" % (b % 2)])
                self.copy("dve" if b % 2 else "act", vtok[:, b, :].bitcast(F32R), ps[:, 0:128], ["ps%d" % (b % 2)], ["vtok%d" % b])
            def pair(g, kb, it, h=h):
                qg = qT[:, g * TB:(g + 1) * TB].bitcast(F32R)
                first = kb == 4 * g + 3
                psO = po[g % 2]
                otag = "ps%d" % (6 + g % 2)
                i3 = it % ND
                kblk = kT[:, kb * 128:(kb + 1) * 128].bitcast(F32R)
                A_, C_ = pa[i3], pc[i3]
                e_, sp_, w_, sh_, sl_, bt_ = eb[i3], spb[i3], wTb[i3], sph[i3], spl[i3], btb[i3]
                tA, tC = "ps%d" % i3, "ps%d" % (3 + i3)
                m = kb - 4 * g
                P.op("pe", lambda e: e.matmul(A_[:], kblk, qg, start=True, stop=False, skip_group_check=True), ["kT", "qT"], [tA])
                yield
                P.op("act", lambda e: e.activation(out=e_, in_=A_[:], func=AF.Exp, scale=-1.0), [tA], ["e%d" % i3])
                yield
                P.op("act", lambda e: e.activation(out=sp_, in_=e_, func=AF.Ln, bias=self.oneT[:], scale=1.0), ["e%d" % i3, "oneT"], ["sp%d" % i3])
                P.op("act", lambda e: e.activation(out=sh_.bitcast(F32R), in_=e_, func=AF.Ln, bias=self.oneT[:], scale=1.0), ["e%d" % i3, "oneT"], ["sh%d" % i3])
                yield
                if m >= 0:
                    P.op("pool", lambda e: e.tensor_tensor(sp_, sp_, msk[:, m, :], ALU.mult), ["sp%d" % i3, "msk"], ["sp%d" % i3])
                    P.op("dve", lambda e: e.tensor_tensor(sh_.bitcast(F32R), sh_, msk[:, m, :], ALU.mult), ["sh%d" % i3, "msk"], ["sh%d" % i3])
                yield
                P.op("dve", lambda e: e.tensor_tensor(sl_.bitcast(F32R), sp_, sh_, ALU.subtract), ["sp%d" % i3, "sh%d" % i3], ["sl%d" % i3])
                yield
                P.op("pe", lambda e: e.matmul(A_[:], tri.bitcast(F32R), sh_.bitcast(F32R), start=False, stop=False, skip_group_check=True), ["tri", "sh%d" % i3, "e%d" % i3], [tA])
                P.op("pe", lambda e: e.matmul(A_[:], tri.bitcast(F32R), sl_.bitcast(F32R), start=False, stop=True, skip_group_check=True), ["tri", "sl%d" % i3], [tA])
                if kb != 0:
                    P.op("pe", lambda e: e.matmul(C_[:], ones.bitcast(F32R), sh_.bitcast(F32R), start=True, stop=False), ["ones", "sh%d" % i3], [tC])
                    P.op("pe", lambda e: e.matmul(C_[:], ones.bitcast(F32R), sl_.bitcast(F32R), start=False, stop=True), ["ones", "sl%d" % i3], [tC])
                yield
                if first:
                    P.op("act", lambda e: e.activation(out=w_.bitcast(F32R), in_=A_[:], func=AF.Exp, scale=-1.0), [tA], ["w%d" % i3])
                    if kb != 0:
                        P.op("dve", lambda e: e.tensor_copy(Cb, C_[:]), [tC], ["Cb"])
                    yield
                    yield
                else:
                    P.op("dve", lambda e: e.tensor_tensor(bt_, A_[:], Cb, ALU.add), [tA, "Cb"], ["bt%d" % i3])
                    if kb != 0:
                        P.op("dve", lambda e: e.tensor_tensor(Cb, Cb, C_[:], ALU.add), [tC, "Cb"], ["Cb"])
                    yield
                    P.op("act", lambda e: e.activation(out=w_.bitcast(F32R), in_=bt_, func=AF.Exp, scale=-1.0), ["bt%d" % i3], ["w%d" % i3])
                    yield
                if m >= 0:
                    P.op("pool", lambda e: e.tensor_tensor(w_.bitcast(F32R), w_, msk[:, m, :], ALU.mult), ["w%d" % i3, "msk"], ["w%d" % i3])
                yield
                P.op("pe", lambda e: e.matmul(psO[:], vtok[:, kb, :].bitcast(F32R), w_.bitcast(F32R), start=first, stop=(kb == 0)),
                     ["vtok%d" % kb, "w%d" % i3], [otag])
                yield
                if kb == 0:
                    self.copy("dve", ob, psO[:], [otag], ["ob"])
                    self.out_rows(ob, 1024 + h * 128, g * TB, TB, "ob")

            pend = []
            for g in range(NG):
                for kb in range(4 * g + 3, -1, -1):
                    pend.append((g, kb, it))
                    it += 1
            active = []
            while pend or active:
                while pend and len(active) < ND:
                    active.append(pair(*pend.pop(0)))
                for gen in list(active):
                    try:
                        next(gen)
                    except StopIteration:
                        active.remove(gen)

    def mix_hgrn(self, l):
        P, S = self.P, self.S
        NC = S // 32
        NB = S // 128
        Qh = self.A(0, S)
        Qt = self.A(S, S)
        Kh = self.A(2 * S, S)
        Kt = self.A(3 * S, S)
        vT = self.A(4 * S, S)
        sm = 5 * S
        ktok = [self.A(sm + i * 128, 128) for i in range(2)]
        vtok = [self.A(sm + 256 + i * 128, 128) for i in range(2)]
        scm = [self.A(sm + 512 + i * 128, 128) for i in range(2)]
        St = [self.A(sm + 768 + i * 128, 128) for i in range(2)]
        T0, T1, T2 = self.AF(0, S), self.AF(S, S), self.AF(2 * S, S)
        fo = 3 * S if 3 * S + 1200 <= ARENA_F else None
        assert fo is None or True
        small = self.hg_small
        eref, erefi, elast = small[:, 0:NC], small[:, NC:2 * NC], small[:, 2 * NC:3 * NC]
        m32 = self.hg_m32[:]
        ob = [T0[:, i * 512:(i + 1) * 512] for i in range(3)]
        rst = self.A(sm + 1024, S)
        P.dma(rst.bitcast(F32R), self.rst_d[:, 0:S].bitcast(F32R), writes=["rst"], q="pool")
        P.dma(m32, self.m32_d, writes=["m32"])
        c3 = lambda t: t.rearrange("p (c t) -> p c t", t=32)
        for h in range(4):
            lbc = self.hg_lb[:, 4 * l + h:4 * l + h + 1]
            oml = self.hg_oml[:, 4 * l + h:4 * l + h + 1]
            P.dma(T0, self.projT[512 + h * 128:512 + (h + 1) * 128, :], writes=["T0"])
            P.dma(T1, self.projT[h * 128:(h + 1) * 128, :], writes=["T1"], q="act")
            P.dma(vT.bitcast(F32R), self.projT[1024 + h * 128:1024 + (h + 1) * 128, :].bitcast(F32R), writes=["vT"], q="pool")
            P.op("act", lambda e: e.activation(out=T2, in_=T0, func=AF.Sigmoid), ["T0"], ["T2"])
            P.op("dve", lambda e, oml=oml, lbc=lbc: e.tensor_scalar(T2, T2, oml, lbc, ALU.mult, ALU.add), ["T2", "hgp"], ["T2"])
            P.op("act", lambda e: e.activation(out=T2, in_=T2, func=AF.Ln), ["T2"], ["T2"])
            P.op("act", lambda e: e.activation(out=T0, in_=T0, func=AF.Sigmoid, scale=-1.0), ["T0"], ["T0"])
            P.op("dve", lambda e, oml=oml: e.tensor_scalar(T0, T0, oml, None, ALU.mult), ["T0", "hgp"], ["T0"])
            P.op("dve", lambda e: e.tensor_tensor_scan(T2, rst, T2, 0.0, ALU.mult, ALU.add), ["T2", "rst"], ["T2"])
            P.op("act", lambda e: e.activation(out=T1, in_=T1, func=AF.Silu), ["T1"], ["T1"])
            P.op("act", lambda e: e.activation(out=eref, in_=c3(T2)[:, :, 15], func=AF.Exp, scale=-1.0), ["T2"], ["hgs"])
            P.op("act", lambda e: e.activation(out=erefi, in_=c3(T2)[:, :, 15], func=AF.Exp), ["T2"], ["hgs"])
            P.op("act", lambda e: e.activation(out=elast, in_=c3(T2)[:, :, 31], func=AF.Exp), ["T2"], ["hgs"])
            P.op("act", lambda e: e.activation(out=Qh.bitcast(F32R), in_=T2, func=AF.Exp), ["T2"], ["Qh"])
            P.op("dve", lambda e: e.tensor_tensor(Qh.bitcast(F32R), Qh, T1, ALU.mult), ["Qh", "T1"], ["Qh"])
            P.op("pool", lambda e: e.tensor_tensor(c3(Qt).bitcast(F32R), c3(Qh), eref.unsqueeze(2).to_broadcast([128, NC, 32]), ALU.mult), ["Qh", "hgs"], ["Qt"])
            P.op("act", lambda e: e.activation(out=Kt.bitcast(F32R), in_=T2, func=AF.Exp, scale=-1.0), ["T2"], ["Kt"])
            P.op("dve", lambda e: e.tensor_tensor(Kt.bitcast(F32R), Kt, T0, ALU.mult), ["Kt", "T0"], ["Kt"])
            P.op("pool", lambda e: e.tensor_tensor(c3(Kh).bitcast(F32R), c3(Kt), elast.unsqueeze(2).to_broadcast([128, NC, 32]), ALU.mult), ["Kt", "hgs"], ["Kh"])
            P.op("dve", lambda e: e.tensor_tensor(c3(Kt).bitcast(F32R), c3(Kt), erefi.unsqueeze(2).to_broadcast([128, NC, 32]), ALU.mult), ["Kt", "hgs", "Kh"], ["Kt"])
            P.op("dve", lambda e: e.tensor_scalar(St[0].bitcast(F32R), m32, 0.0, None, ALU.mult), ["m32"], ["St0"])
            for b in range(NB):
                i2 = b % 2
                tsl = slice(b * 128, (b + 1) * 128)
                pk, pv, psc, po_, pkv = self.psb[0 + i2], self.psb[2 + i2], self.psb[4 + i2], self.psb[6], self.psb[7]
                P.op("pe", lambda e, pk=pk, tsl=tsl: e.transpose(pk[:, 0:128], Kh[:, tsl], self.ident[:]), ["Kh", "ident"], ["ps%d" % i2])
                P.op("pe", lambda e, pv=pv, tsl=tsl: e.transpose(pv[:, 0:128], vT[:, tsl], self.ident[:]), ["vT", "ident"], ["ps%d" % (2 + i2)])
                self.copy("act", ktok[i2].bitcast(F32R), pk[:, 0:128], ["ps%d" % i2], ["ktok%d" % i2])
                self.copy("act", vtok[i2].bitcast(F32R), pv[:, 0:128], ["ps%d" % (2 + i2)], ["vtok%d" % i2])
                P.op("pe", lambda e, psc=psc, tsl=tsl: e.matmul(psc[:, 0:128], Kt[:, tsl].bitcast(F32R), Qt[:, tsl].bitcast(F32R), start=True, stop=True), ["Kt", "Qt"], ["ps%d" % (4 + i2)])
                P.op("dve", lambda e, psc=psc, i2=i2: e.tensor_tensor(scm[i2].bitcast(F32R), psc[:, 0:128], m32, ALU.mult), ["ps%d" % (4 + i2), "m32"], ["scm%d" % i2])
                bank = po_[:, i2 * 128:(i2 + 1) * 128]
                otag = "po%d" % i2
                P.op("pe", lambda e, bank=bank, i2=i2: e.matmul(bank, vtok[i2].bitcast(F32R), scm[i2].bitcast(F32R), start=True, stop=False, skip_group_check=True),
                     ["vtok%d" % i2, "scm%d" % i2], [otag])
                for c in range(4):
                    cg = b * 4 + c
                    cur, nxt = St[cg % 2], St[(cg + 1) % 2]
                    csl = slice(b * 128 + c * 32, b * 128 + (c + 1) * 32)
                    P.op("pe", lambda e, bank=bank, c=c, cur=cur, csl=csl: e.matmul(bank[:, c * 32:(c + 1) * 32], cur.bitcast(F32R), Qh[:, csl].bitcast(F32R), start=False, stop=(c == 3), skip_group_check=True),
                         ["St%d" % (cg % 2), "Qh"], [otag])
                    kvp = pkv[:, (cg % 4) * 128:(cg % 4 + 1) * 128]
                    kvt = "pkv%d" % (cg % 4)
                    tp = (32 * c, 0)
                    P.op("pe", lambda e, kvp=kvp, c=c, i2=i2, tp=tp: e.matmul(kvp, ktok[i2][32 * c:32 * c + 32, :].bitcast(F32R), vtok[i2][32 * c:32 * c + 32, :].bitcast(F32R),
                                                                         start=True, stop=True, tile_position=tp, skip_group_check=True),
                         ["ktok%d" % i2, "vtok%d" % i2], [kvt])
                    P.op("dve", lambda e, nxt=nxt, cur=cur, kvp=kvp, cg=cg: e.scalar_tensor_tensor(nxt.bitcast(F32R), cur, elast[:, cg:cg + 1], kvp, ALU.mult, ALU.add),
                         ["St%d" % (cg % 2), kvt, "hgs"], ["St%d" % ((cg + 1) % 2)])
                if b % 4 == 3:
                    pass
                self.copy("act", T2[:, tsl], bank, [otag], ["T2"])
            P.dma(T1, self.projT[1536 + h * 128:1536 + (h + 1) * 128, :], writes=["T1"])
            P.op("act", lambda e: e.activation(out=T1, in_=T1, func=AF.Silu), ["T1"], ["T1"])
            P.op("dve", lambda e, h=h: e.tensor_scalar(T1, T1, self.hg_ng[:, 4 * l + h:4 * l + h + 1], None, ALU.mult), ["T1", "hgp"], ["T1"])
            for g in range(S // TB):
                gs = slice(g * TB, (g + 1) * TB)
                sq = ob[0]
                P.op("pool", lambda e, gs=gs: e.tensor_tensor(sq, T2[:, gs], T2[:, gs], ALU.mult), ["T2", "T0"], ["T0"])
                pr = self.psb[g % 2]
                P.op("pe", lambda e, pr=pr: e.matmul(pr[:], self.ones_sb[:], sq, start=True, stop=True), ["T0", "ones_sb"], ["ps%d" % (g % 2)])
                rt = ob[1]
                P.op("act", lambda e, pr=pr: e.activation(out=rt, in_=pr[:], func=AF.Sqrt, bias=self.rmsT[:], scale=1.0 / 128.0), ["ps%d" % (g % 2), "rmsT", "T0"], ["T0"])
                P.op("dve", lambda e: e.reciprocal(rt, rt), ["T0"], ["T0"])
                P.op("dve", lambda e, gs=gs: e.tensor_tensor(rt, rt, T2[:, gs], ALU.mult), ["T0", "T2"], ["T0"])
                P.op("pool", lambda e, gs=gs: e.tensor_tensor(ob[2], rt, T1[:, gs], ALU.mult), ["T0", "T1"], ["T0"])
                self.out_rows(ob[2], h * 128, g * TB, TB, "T0")


    def mix_s5(self, l):
        P, S = self.P, self.S
        NBLK = S // 128
        PI_ = math.pi
        sp = self.s5p
        def sv(i):
            return sp[:, i, :]
        AR, AI, DT, MAG, TH, CS, SN, LR, LI, DEN, ZR, ZI, K, TMP, TMP2, GPR, GPI, HLR, HLI, PIH = range(20)
        P.dma(sv(AR), self.s5_arT[l], writes=["s5p"])
        P.dma(sv(AI), self.s5_aiT[l], writes=["s5p"])
        P.dma(sv(DT), self.s5_ldtT[l], writes=["s5p"])
        o = lambda eng, fn: P.op(eng, fn, ["s5p"], ["s5p"])
        o("act", lambda e: e.activation(out=sv(DT), in_=sv(DT), func=AF.Exp))
        o("dve", lambda e: e.tensor_tensor(sv(MAG), sv(AR), sv(DT), ALU.mult))
        o("act", lambda e: e.activation(out=sv(MAG), in_=sv(MAG), func=AF.Exp))
        o("dve", lambda e: e.tensor_tensor(sv(TH), sv(AI), sv(DT), ALU.mult))
        o("dve", lambda e: e.memset(sv(K), 0.0))
        for mth in range(1, 8):
            o("dve", lambda e, mth=mth: e.tensor_single_scalar(sv(TMP), sv(TH), (2 * mth - 1) * PI_, ALU.is_gt))
            o("dve", lambda e: e.tensor_tensor(sv(K), sv(K), sv(TMP), ALU.add))
        C1 = 6.28125
        C2 = 2 * PI_ - C1
        o("dve", lambda e: e.scalar_tensor_tensor(sv(TH), sv(K), -C1, sv(TH), ALU.mult, ALU.add))
        o("dve", lambda e: e.scalar_tensor_tensor(sv(TH), sv(K), -C2, sv(TH), ALU.mult, ALU.add))
        o("dve", lambda e: e.memset(sv(PIH), PI_ / 2))
        o("dve", lambda e: e.tensor_scalar_min(sv(TH), sv(TH), PI_))
        o("dve", lambda e: e.tensor_scalar_max(sv(TH), sv(TH), -PI_))
        o("act", lambda e: e.activation(out=sv(SN), in_=sv(TH), func=AF.Sin))
        o("dve", lambda e: e.tensor_scalar(sv(TMP), sv(TH), -1.0, None, ALU.mult))
        o("dve", lambda e: e.tensor_tensor(sv(TMP), sv(TMP), sv(TH), ALU.max))
        o("dve", lambda e: e.tensor_scalar(sv(TMP), sv(TMP), -1.0, PI_ / 2, ALU.mult, ALU.add))
        o("act", lambda e: e.activation(out=sv(CS), in_=sv(TMP), func=AF.Sin))
        o("dve", lambda e: e.tensor_tensor(sv(LR), sv(MAG), sv(CS), ALU.mult))
        o("dve", lambda e: e.tensor_tensor(sv(LI), sv(MAG), sv(SN), ALU.mult))
        o("dve", lambda e: e.tensor_tensor(sv(DEN), sv(AR), sv(AR), ALU.mult))
        o("dve", lambda e: e.tensor_tensor(sv(TMP), sv(AI), sv(AI), ALU.mult))
        o("dve", lambda e: e.tensor_tensor(sv(DEN), sv(DEN), sv(TMP), ALU.add))
        o("dve", lambda e: e.reciprocal(sv(DEN), sv(DEN)))
        o("dve", lambda e: e.tensor_scalar_add(sv(TMP), sv(LR), -1.0))
        o("dve", lambda e: e.tensor_tensor(sv(ZR), sv(TMP), sv(AR), ALU.mult))
        o("dve", lambda e: e.tensor_tensor(sv(TMP2), sv(LI), sv(AI), ALU.mult))
        o("dve", lambda e: e.tensor_tensor(sv(ZR), sv(ZR), sv(TMP2), ALU.add))
        o("dve", lambda e: e.tensor_tensor(sv(ZR), sv(ZR), sv(DEN), ALU.mult))
        o("dve", lambda e: e.tensor_tensor(sv(ZI), sv(TMP), sv(AI), ALU.mult))
        o("dve", lambda e: e.tensor_tensor(sv(TMP2), sv(LI), sv(AR), ALU.mult))
        o("dve", lambda e: e.tensor_tensor(sv(ZI), sv(TMP2), sv(ZI), ALU.subtract))
        o("dve", lambda e: e.tensor_tensor(sv(ZI), sv(ZI), sv(DEN), ALU.mult))
        Er = self.AF(0, 2048).rearrange("p (a j) -> p a j", j=128)
        Ei = self.AF(2048, 2048).rearrange("p (a j) -> p a j", j=128)
        D0 = self.AF(4096, 2048).rearrange("p (a j) -> p a j", j=128)
        t1 = self.AF(6144, 1024)
        t2 = self.AF(7168, 1024)
        xr = self.AF(8192, 1024)
        xi = self.AF(9216, 1024)
        pw = self.AF(10240, 64).rearrange("p (c a) -> p c a", a=16)
        bc = lambda v, n: v.unsqueeze(2).to_broadcast([128, 16, n])
        ot = lambda eng, fn: P.op(eng, fn, ["s5p", "s5t"], ["s5t", "t10", "t11"])
        ot("dve", lambda e: e.memset(Er[:, :, 0:1], 1.0))
        ot("dve", lambda e: e.memset(Ei[:, :, 0:1], 0.0))
        ot("dve", lambda e: e.tensor_copy(pw[:, 0, :], sv(CS)))
        ot("dve", lambda e: e.tensor_scalar(pw[:, 1, :], sv(SN), -1.0, None, ALU.mult))
        n = 1
        while n < 128:
            ot("dve", lambda e, n=n: e.tensor_tensor(Er[:, :, n:2 * n], Er[:, :, 0:n], bc(pw[:, 0, :], n), ALU.mult))
            ot("dve", lambda e, n=n: e.tensor_tensor(t1.rearrange("p (a j) -> p a j", a=16)[:, :, 0:n], Ei[:, :, 0:n], bc(pw[:, 1, :], n), ALU.mult))
            ot("dve", lambda e, n=n: e.tensor_tensor(Er[:, :, n:2 * n], Er[:, :, n:2 * n], t1.rearrange("p (a j) -> p a j", a=16)[:, :, 0:n], ALU.subtract))
            ot("dve", lambda e, n=n: e.tensor_tensor(Ei[:, :, n:2 * n], Er[:, :, 0:n], bc(pw[:, 1, :], n), ALU.mult))
            ot("dve", lambda e, n=n: e.tensor_tensor(t1.rearrange("p (a j) -> p a j", a=16)[:, :, 0:n], Ei[:, :, 0:n], bc(pw[:, 0, :], n), ALU.mult))
            ot("dve", lambda e, n=n: e.tensor_tensor(Ei[:, :, n:2 * n], Ei[:, :, n:2 * n], t1.rearrange("p (a j) -> p a j", a=16)[:, :, 0:n], ALU.add))
            ot("dve", lambda e: e.tensor_tensor(pw[:, 2, :], pw[:, 0, :], pw[:, 0, :], ALU.mult))
            ot("dve", lambda e: e.tensor_tensor(pw[:, 3, :], pw[:, 1, :], pw[:, 1, :], ALU.mult))
            ot("dve", lambda e: e.tensor_tensor(pw[:, 1, :], pw[:, 0, :], pw[:, 1, :], ALU.mult))
            ot("dve", lambda e: e.tensor_scalar(pw[:, 1, :], pw[:, 1, :], 2.0, None, ALU.mult))
            ot("dve", lambda e: e.tensor_tensor(pw[:, 0, :], pw[:, 2, :], pw[:, 3, :], ALU.subtract))
            n *= 2
        ot("dve", lambda e: e.tensor_copy(D0, bc(sv(MAG), 128)))
        ot("dve", lambda e: e.memset(D0[:, :, 0:1], 0.0))
        uT = self.A(0, 4 * S).rearrange("p (q t) -> p q t", t=S)
        wb = self.A(4 * S + 15360, 4096).rearrange("p (c a n) -> p c a n", c=2, a=16)
        wc = self.A(4 * S + 1024, 4096).rearrange("p (c a n) -> p c a n", c=2, a=16)
        wg = self.A(4 * S + 5120, 2048).rearrange("p (n a k) -> p n a k", n=4, a=4)
        hrb = self.A(4 * S + 7168, 1024).rearrange("p (a j) -> p a j", j=128)
        hib = self.A(4 * S + 8192, 1024).rearrange("p (a j) -> p a j", j=128)
        yg = self.A(4 * S + 9216, 2048).rearrange("p (q t) -> p q t", t=512)
        assert 4 * S + 15360 + 4096 <= ARENA_R
        wcf = self.AF(10304, 2048).rearrange("p (a n) -> p a n", a=16)
        wcf2 = self.AF(12352 - 0, 0) if False else None
        P.dma(uT.bitcast(F32R), self.projT[2048:2560, :].rearrange("(q p) t -> p q t", p=128).bitcast(F32R), writes=["uT"], q="pool")
        P.dma(wb.bitcast(F32R), self.s5_wb[l].bitcast(F32R), writes=["wb"], q="pool")
        P.dma(wg.bitcast(F32R), self.s5_glu[l].rearrange("n p (a k) -> p n a k", a=4).bitcast(F32R), writes=["wg"], q="pool")
        cre = self.A(4 * S + 11264, 2048).rearrange("p (a n) -> p a n", a=16)
        cim = self.A(4 * S + 13312, 2048).rearrange("p (a n) -> p a n", a=16)
        P.dma(cre.bitcast(F32R), self.s5_wcre[l].bitcast(F32R), writes=["cre"], q="pool")
        P.dma(cim.bitcast(F32R), self.s5_wcim[l].bitcast(F32R), writes=["cim"], q="pool")
        ow = lambda eng, fn, w: P.op(eng, fn, ["s5p", "cre", "cim", "wcf", "wc"], [w])
        ow("dve", lambda e: e.tensor_tensor(wcf, cim, bc(sv(ZI), 128), ALU.mult), "wcf")
        ow("dve", lambda e: e.tensor_tensor(wc[:, 0].bitcast(F32R), cre, bc(sv(ZR), 128), ALU.mult), "wc")
        ow("dve", lambda e: e.tensor_tensor(wc[:, 0].bitcast(F32R), wc[:, 0], wcf, ALU.subtract), "wc")
        ow("dve", lambda e: e.tensor_tensor(wcf, cim, bc(sv(ZR), 128), ALU.mult), "wcf")
        ow("dve", lambda e: e.tensor_tensor(wc[:, 1].bitcast(F32R), cre, bc(sv(ZI), 128), ALU.mult), "wc")
        ow("dve", lambda e: e.tensor_tensor(wc[:, 1].bitcast(F32R), wc[:, 1], wcf, ALU.add), "wc")
        ow("dve", lambda e: e.tensor_scalar(wc[:, 1].bitcast(F32R), wc[:, 1], -1.0, None, ALU.mult), "wc")
        o("dve", lambda e: e.memset(sv(GPR), 0.0))
        o("dve", lambda e: e.memset(sv(GPI), 0.0))
        PR, PIm, PY, PG = self.psb[0:2], self.psb[2:4], self.psb[4], self.psb[5:7]
        import os as _os
        stop = int(_os.environ.get("S5_STOP", 99))
        for b in range(int(_os.environ.get("S5_NBLK", NBLK))):
            tsl = slice(b * 128, (b + 1) * 128)
            for hf in range(2):
                asl = slice(hf * 8, hf * 8 + 8)
                for a8 in range(8):
                    a = hf * 8 + a8
                    for c, PP, tg in ((0, PR, "pr"), (1, PIm, "pi")):
                        dst = PP[a8 // 4][:, (a8 % 4) * 128:(a8 % 4 + 1) * 128]
                        P.op("pe", lambda e, dst=dst, c=c, a=a, tsl=tsl: e.matmul(dst, wb[:, c, a, :].bitcast(F32R), uT[:, a // 4, tsl].bitcast(F32R),
                                                                          start=True, stop=True, skip_group_check=True),
                             ["wb", "uT"], [tg + str(a8 // 4)])
                for k2 in range(2 if stop >= 2 else 0):
                    sl = slice(k2 * 512, (k2 + 1) * 512)
                    er = Er[:, hf * 8 + k2 * 4: hf * 8 + k2 * 4 + 4, :]
                    ei = Ei[:, hf * 8 + k2 * 4: hf * 8 + k2 * 4 + 4, :]
                    v4 = lambda t: t.rearrange("p (a j) -> p a j", j=128)
                    prr, pii = v4(PR[k2][:]), v4(PIm[k2][:])
                    P.op("dve", lambda e, sl=sl, er=er, prr=prr: e.tensor_tensor(v4(xr[:, sl]), prr, er, ALU.mult), ["pr%d" % k2, "s5t"], ["xr%d" % k2])
                    P.op("act", lambda e, sl=sl, ei=ei, pii=pii: e.copy(v4(t1[:, sl]), pii), ["pi%d" % k2], ["t1%d" % k2])
                    P.op("pool", lambda e, sl=sl, ei=ei: e.tensor_tensor(v4(t2[:, sl]), v4(t1[:, sl]), ei, ALU.mult), ["t1%d" % k2, "s5t"], ["t2%d" % k2])
                    P.op("dve", lambda e, sl=sl: e.tensor_tensor(xr[:, sl], xr[:, sl], t2[:, sl], ALU.subtract), ["xr%d" % k2, "t2%d" % k2], ["xr%d" % k2])
                    P.op("dve", lambda e, sl=sl, ei=ei, prr=prr: e.tensor_tensor(v4(xi[:, sl]), prr, ei, ALU.mult), ["pr%d" % k2, "s5t"], ["xi%d" % k2])
                    P.op("pool", lambda e, sl=sl, er=er: e.tensor_tensor(v4(t2[:, sl]), v4(t1[:, sl]), er, ALU.mult), ["t1%d" % k2, "s5t", "xr%d" % k2], ["t2%d" % k2])
                    P.op("dve", lambda e, sl=sl: e.tensor_tensor(xi[:, sl], xi[:, sl], t2[:, sl], ALU.add), ["xi%d" % k2, "t2%d" % k2], ["xi%d" % k2])
                if stop < 3:
                    continue
                x3r = xr.rearrange("p (a j) -> p a j", j=128)
                x3i = xi.rearrange("p (a j) -> p a j", j=128)
                tagx = ["xr0", "xr1", "xi0", "xi1"]
                P.op("dve", lambda e, asl=asl: e.tensor_tensor(sv(TMP)[:, asl], sv(MAG)[:, asl], sv(GPR)[:, asl], ALU.mult), ["s5p"], ["s5p"])
                P.op("dve", lambda e, asl=asl: e.tensor_tensor(x3r[:, :, 0], x3r[:, :, 0], sv(TMP)[:, asl], ALU.add), ["s5p"] + tagx, ["xr0", "xr1"])
                P.op("dve", lambda e, asl=asl: e.tensor_tensor(sv(TMP)[:, asl], sv(MAG)[:, asl], sv(GPI)[:, asl], ALU.mult), ["s5p"], ["s5p"])
                P.op("dve", lambda e, asl=asl: e.tensor_tensor(x3i[:, :, 0], x3i[:, :, 0], sv(TMP)[:, asl], ALU.add), ["s5p"] + tagx, ["xi0", "xi1"])
                d0 = D0[:, asl, :].rearrange("p a j -> p (a j)")
                P.op("dve", lambda e, d0=d0: e.tensor_tensor_scan(xr, d0, xr, 0.0, ALU.mult, ALU.add), ["xr0", "xr1", "s5t"], ["xr0", "xr1"])
                P.op("dve", lambda e, d0=d0: e.tensor_tensor_scan(xi, d0, xi, 0.0, ALU.mult, ALU.add), ["xi0", "xi1", "s5t"], ["xi0", "xi1"])
                if stop < 4:
                    continue
                erh, eih = Er[:, asl, :], Ei[:, asl, :]
                P.op("pool", lambda e, erh=erh: e.tensor_tensor(t1.rearrange("p (a j) -> p a j", j=128), x3r, erh, ALU.mult), ["xr0", "xr1", "s5t", "t10", "t11"], ["t10", "t11"])
                P.op("pool", lambda e, eih=eih: e.tensor_tensor(t2.rearrange("p (a j) -> p a j", j=128), x3i, eih, ALU.mult), ["xi0", "xi1", "s5t", "t20", "t21"], ["t20", "t21"])
                P.op("dve", lambda e: e.tensor_tensor(hrb.bitcast(F32R), t1.rearrange("p (a j) -> p a j", j=128), t2.rearrange("p (a j) -> p a j", j=128), ALU.add),
                     ["t10", "t11", "t20", "t21"], ["hr"])
                P.op("pool", lambda e, erh=erh: e.tensor_tensor(t1.rearrange("p (a j) -> p a j", j=128), x3i, erh, ALU.mult), ["xi0", "xi1", "s5t", "t10", "t11", "hr"], ["t10", "t11"])
                P.op("pool", lambda e, eih=eih: e.tensor_tensor(t2.rearrange("p (a j) -> p a j", j=128), x3r, eih, ALU.mult), ["xr0", "xr1", "s5t", "t20", "t21", "hr"], ["t20", "t21"])
                P.op("dve", lambda e: e.tensor_tensor(hib.bitcast(F32R), t1.rearrange("p (a j) -> p a j", j=128), t2.rearrange("p (a j) -> p a j", j=128), ALU.subtract),
                     ["t10", "t11", "t20", "t21"], ["hi"])
                if stop < 5:
                    continue
                P.op("dve", lambda e, asl=asl: e.tensor_copy(sv(HLR)[:, asl], hrb[:, :, 127]), ["hr", "s5p"], ["s5p"])
                P.op("dve", lambda e, asl=asl: e.tensor_copy(sv(HLI)[:, asl], hib[:, :, 127]), ["hi", "s5p"], ["s5p"])
                for (dst, x1, c1, x2, c2, op_) in ((GPR, HLR, CS, HLI, SN, ALU.subtract), (GPI, HLR, SN, HLI, CS, ALU.add)):
                    P.op("dve", lambda e, asl=asl, x1=x1, c1=c1: e.tensor_tensor(sv(TMP)[:, asl], sv(x1)[:, asl], sv(c1)[:, asl], ALU.mult), ["s5p"], ["s5p"])
                    P.op("dve", lambda e, asl=asl, x2=x2, c2=c2: e.tensor_tensor(sv(TMP2)[:, asl], sv(x2)[:, asl], sv(c2)[:, asl], ALU.mult), ["s5p"], ["s5p"])
                    P.op("dve", lambda e, asl=asl, dst=dst, op_=op_: e.tensor_tensor(sv(dst)[:, asl], sv(TMP)[:, asl], sv(TMP2)[:, asl], op_), ["s5p"], ["s5p"])
                if stop < 6:
                    continue
                for ql in range(2):
                    q = 2 * hf + ql
                    dst = PY[:, q * 128:(q + 1) * 128]
                    n_ = 0
                    for a4 in range(4):
                        a8 = ql * 4 + a4
                        a = hf * 8 + a8
                        for c, hb, tg in ((0, hrb, "hr"), (1, hib, "hi")):
                            P.op("pe", lambda e, dst=dst, c=c, a=a, a8=a8, hb=hb, n_=n_: e.matmul(dst, wc[:, c, a, :].bitcast(F32R), hb[:, a8, :].bitcast(F32R),
                                                                                          start=(n_ == 0), stop=(n_ == 7), skip_group_check=True),
                                 ["wc", tg], ["py"])
                            n_ += 1
            if stop < 7:
                continue
            bs = (b % 4) * 128
            for q in range(4):
                P.op("dve", lambda e, q=q, tsl=tsl: e.scalar_tensor_tensor(t1[:, q * 128:(q + 1) * 128], uT[:, q, tsl], self.s5_dT[:, 4 * l + q:4 * l + q + 1], PY[:, q * 128:(q + 1) * 128], ALU.mult, ALU.add),
                     ["uT", "py", "s5d", "t10", "t11"], ["t10"])
                P.op("act", lambda e, q=q, bs=bs: e.activation(out=yg[:, q, bs:bs + 128].bitcast(F32R), in_=t1[:, q * 128:(q + 1) * 128], func=AF.Gelu), ["t10"], ["yg%d" % q])
            if b % 4 == 3:
                t0 = (b // 4) * TB
                for n4 in range(4):
                    pg = PG[n4 % 2]
                    for q in range(4):
                        P.op("pe", lambda e, pg=pg, n4=n4, q=q: e.matmul(pg[:], wg[:, n4, q, :].bitcast(F32R), yg[:, q, :].bitcast(F32R), start=(q == 0), stop=(q == 3)),
                             ["wg"] + ["yg%d" % qq for qq in range(4)], ["pg%d" % (n4 % 2)])
                    sg = t2[:, (n4 % 2) * 512:(n4 % 2 + 1) * 512]
                    P.op("act", lambda e, pg=pg, sg=sg, n4=n4: e.activation(out=sg, in_=pg[:], func=AF.Sigmoid, bias=self.s5_gbT[:, 4 * l + n4:4 * l + n4 + 1], scale=1.0),
                         ["pg%d" % (n4 % 2), "s5d", "t20", "t21"], ["t2%d" % (n4 % 2)])
                    P.op("dve", lambda e, sg=sg, n4=n4: e.tensor_tensor(sg, sg, yg[:, n4, :], ALU.mult), ["t2%d" % (n4 % 2), "yg%d" % n4], ["t2%d" % (n4 % 2)])
                    self.out_rows(sg, 512 + n4 * 128, t0, TB, "t2%d" % (n4 % 2), q=("sp" if n4 % 2 else "act"))


    def mix_rwkv(self, l):
        P, S = self.P, self.S
        HS = min(S, 512)
        NH = S // HS
        NSB = HS // 128
        NCH = HS // 64
        E05 = math.exp(-0.5)
        RA, FA = self.A, self.AF
        KT, BT, KK, RT, VT, GT, BV = [RA(i * HS, HS) for i in range(7)]
        ro = 7 * HS
        W2p, A2p, G2w = RA(ro, 512), RA(ro + 512, 512), RA(ro + 1024, 512)
        XWA = RA(ro + 1536, HS)
        XG = RA(ro + 1536 + HS, HS)
        mo = ro + 1536 + 2 * HS
        BTm2 = [[RA(mo + bi * 768 + h * 128, 128) for h in range(2)] for bi in range(2)]
        KKm2 = [[RA(mo + bi * 768 + 256 + h * 128, 128) for h in range(2)] for bi in range(2)]
        KTm2 = [[RA(mo + bi * 768 + 512 + h * 128, 128) for h in range(2)] for bi in range(2)]
        mo = mo + 512
        blk = RA(mo + 1024, 128)
        rst = RA(mo + 1152, HS)
        tqr = RA(mo + 1152 + HS, 512)
        assert mo + 1152 + HS + 512 <= ARENA_R
        T = [FA(i * HS, HS) for i in range(4)]
        fo = 4 * HS
        def ft(n):
            nonlocal fo
            r = FA(fo, n)
            fo += n
            return r
        Am, ATm = ft(256), ft(256)
        AkTm2 = [ft(256) for _ in range(2)]
        BrbTm2 = [ft(256) for _ in range(2)]
        BrkTm2 = [ft(256) for _ in range(2)]
        Pk = [ft(256) for _ in range(2)]
        PTk = [ft(256) for _ in range(2)]
        MTk = [ft(256) for _ in range(2)]
        MTfin = [ft(256) for _ in range(2)]
        tok2 = [[ft(128) for _ in range(3)] for _ in range(2)]
        Vc2 = [[ft(128) for _ in range(2)] for _ in range(2)]
        Vpad2 = [[ft(128) for _ in range(2)] for _ in range(2)]
        Upad = [[ft(128) for _ in range(2)] for _ in range(2)]
        Wsb = ft(128)
        WkV = ft(128)
        Pbd = ft(128)
        Ptmp = ft(128)
        blkm = ft(128)
        gam = ft(NCH)
        msk = ft(3 * 256).rearrange("p (a b) -> p a b", b=256)
        idn2 = ft(256)
        OTs = ft(512)
        tq = [ft(512) for _ in range(3)]
        assert fo <= ARENA_F, fo
        rp = self.rwp
        MU, W0, A0, KKs, KA, OMKA, RK, GNG, GNB = 0, 14, 18, 22, 26, 30, 34, 38, 42
        col = lambda base, i: rp[:, base + i:base + i + 1]
        hm = self.rw_hm
        P.dma(rp[:, 0:14], self.rw_muT[l], writes=["rwp"])
        for base, src in ((W0, self.rw_w0T), (A0, self.rw_a0T), (KKs, self.rw_kkT), (KA, self.rw_kaT), (RK, self.rw_rkT), (GNG, self.rw_gngT), (GNB, self.rw_gnbT)):
            P.dma(rp[:, base:base + 4], src[l], writes=["rwp"])
        P.op("dve", lambda e: e.tensor_scalar(rp[:, OMKA:OMKA + 4], rp[:, KA:KA + 4], -1.0, 1.0, ALU.mult, ALU.add), ["rwp"], ["rwp"])
        P.dma(W2p.bitcast(F32R), self.rw_w2p[l].bitcast(F32R), writes=["rww"], q="pool")
        P.dma(A2p.bitcast(F32R), self.rw_a2p[l].bitcast(F32R), writes=["rww"], q="pool")
        P.dma(G2w.bitcast(F32R), self.rw_g2[l].bitcast(F32R), writes=["rww"], q="pool")
        P.dma(blk.bitcast(F32R), self.blk64_d.bitcast(F32R), writes=["blk"], q="pool")
        P.dma(rst.bitcast(F32R), self.rst64_d[:, 0:HS].bitcast(F32R), writes=["rst"], q="pool")
        P.dma(msk, self.rwmask_d.rearrange("a p t -> p a t"), writes=["msk"])
        P.dma(idn2[:, 0:128], self.ident_d, writes=["idn2"])
        P.dma(idn2[:, 128:256], self.ident_d, writes=["idn2"])
        P.dma(blkm, self.blk64_d, writes=["blkm"])
        for t_ in Vc2[0] + Vc2[1] + Vpad2[0] + Vpad2[1] + Upad[0] + Upad[1] + [Wsb]:
            P.op("dve", lambda e, t_=t_: e.memset(t_, 0.0), [], ["small"])

        def shifted(dst, dstp, tagX, tagP, row0, t0):
            P.dma(dst.bitcast(F32R), self.projT[row0:row0 + 128, t0:t0 + HS].bitcast(F32R), writes=[tagX], q="pool")
            if t0 == 0:
                P.op("dve", lambda e: e.memset(dstp[:, 0:1], 0.0), [], [tagP])
                P.dma(dstp[:, 1:HS], self.projT[row0:row0 + 128, 0:HS - 1], writes=[tagP])
            else:
                P.dma(dstp, self.projT[row0:row0 + 128, t0 - 1:t0 + HS - 1], writes=[tagP])

        def tshift(X, Xp, mucol, tagX, tagP):
            P.op("pool", lambda e: e.tensor_tensor(Xp, Xp, X, ALU.subtract), [tagX, tagP], [tagP])
            P.op("dve", lambda e: e.scalar_tensor_tensor(X.bitcast(F32R), Xp, mucol, X, ALU.mult, ALU.add), [tagX, tagP, "rwp"], [tagX])

        RW0 = 4096
        import os as _os
        rstop = int(_os.environ.get("RW_STOP", 99))
        for hp in range(4 if rstop > 0 else 0):
            P.op("dve", lambda e: e.memset(Pbd, 0.0), [], ["Pbd"])
            for half in range(NH):
                t0 = half * HS
                shifted(XWA, T[0], "XWA", "T0", RW0 + 1536, t0)
                tshift(XWA, T[0], col(MU, 12), "XWA", "T0")
                P.op("act", lambda e: e.activation(out=XWA[0:64].bitcast(F32R), in_=XWA[0:64], func=AF.Tanh), ["XWA"], ["XWA"])
                shifted(XG, T[1], "XG", "T1", RW0 + 1664, t0)
                tshift(XG, T[1], col(MU, 13), "XG", "T1")
                P.op("act", lambda e: e.activation(out=XG.bitcast(F32R), in_=XG, func=AF.Sigmoid), ["XG"], ["XG"])
                csl = slice(hp * 128, (hp + 1) * 128)
                for g4 in range(HS // 512):
                    gs = slice(g4 * 512, (g4 + 1) * 512)
                    pw_, pa_, pg_ = self.psb[0], self.psb[1], self.psb[2]
                    P.op("pe", lambda e, gs=gs, csl=csl, pw_=pw_: e.matmul(pw_[:], W2p[:, csl].bitcast(F32R), XWA[:, gs].bitcast(F32R), start=True, stop=True), ["rww", "XWA"], ["ps0"])
                    P.op("pe", lambda e, gs=gs, csl=csl, pa_=pa_: e.matmul(pa_[:], A2p[:, csl].bitcast(F32R), XWA[:, gs].bitcast(F32R), start=True, stop=True), ["rww", "XWA"], ["ps1"])
                    P.op("pe", lambda e, gs=gs, csl=csl, pg_=pg_: e.matmul(pg_[:], G2w[:, csl].bitcast(F32R), XG[:, gs].bitcast(F32R), start=True, stop=True), ["rww", "XG"], ["ps2"])
                    P.op("act", lambda e, gs=gs, pw_=pw_, hp=hp: e.activation(out=T[2][:, gs], in_=pw_[:], func=AF.Sigmoid, bias=col(W0, hp), scale=1.0), ["ps0", "rwp"], ["T2"])
                    P.op("act", lambda e, gs=gs, pa_=pa_, hp=hp: e.activation(out=T[3][:, gs], in_=pa_[:], func=AF.Sigmoid, bias=col(A0, hp), scale=1.0), ["ps1", "rwp"], ["T3"])
                    P.op("act", lambda e, gs=gs, pg_=pg_: e.copy(GT[:, gs].bitcast(F32R), pg_[:]), ["ps2"], ["GT"])
                P.op("dve", lambda e: e.tensor_scalar(T[2], T[2], -E05, None, ALU.mult), ["T2"], ["T2"])
                shifted(KK, T[1], "KK", "T1", RW0 + 512 + hp * 128, t0)
                tshift(KK, T[1], col(MU, 4 + hp), "KK", "T1")
                P.op("dve", lambda e, hp=hp: e.tensor_scalar(T[0], KK, col(KKs, hp), None, ALU.mult), ["KK", "rwp"], ["T0"])
                for g4 in range(HS // 512):
                    gs = slice(g4 * 512, (g4 + 1) * 512)
                    pq = self.psb[g4 % 2]
                    P.op("pool", lambda e, gs=gs: e.tensor_tensor(tqr.bitcast(F32R), T[0][:, gs], T[0][:, gs], ALU.mult), ["T0"], ["tqr"])
                    P.op("pe", lambda e, pq=pq: e.matmul(pq[:], blk.bitcast(F32R), tqr.bitcast(F32R), start=True, stop=True), ["blk", "tqr"], ["ps%d" % (g4 % 2)])
                    P.op("dve", lambda e, pq=pq: e.tensor_scalar_max(tq[1], pq[:], 1e-24), ["ps%d" % (g4 % 2)], ["tq1"])
                    P.op("act", lambda e: e.activation(out=tq[1], in_=tq[1], func=AF.Sqrt), ["tq1"], ["tq1"])
                    P.op("dve", lambda e: e.reciprocal(tq[1], tq[1]), ["tq1"], ["tq1"])
                    P.op("dve", lambda e, gs=gs: e.tensor_tensor(T[0][:, gs], T[0][:, gs], tq[1], ALU.mult), ["T0", "tq1"], ["T0"])
                P.op("dve", lambda e: e.tensor_tensor_scan(T[1], rst, T[2], 0.0, ALU.mult, ALU.add), ["T2", "rst"], ["T1"])
                c3 = lambda t_: t_.rearrange("p (c t) -> p c t", t=64)
                P.op("act", lambda e: e.activation(out=gam, in_=c3(T[1])[:, :, 63], func=AF.Exp), ["T1"], ["gam"])
                P.op("pool", lambda e: e.tensor_tensor(T[2], T[1], T[2], ALU.subtract), ["T1", "T2"], ["T2"])
                P.op("act", lambda e: e.activation(out=T[2], in_=T[2], func=AF.Exp), ["T2"], ["T2"])
                P.op("dve", lambda e: e.tensor_tensor(KT.bitcast(F32R), T[0], T[2], ALU.mult), ["T0", "T2"], ["KT"])
                P.op("act", lambda e: e.activation(out=T[2], in_=T[1], func=AF.Exp, scale=-1.0), ["T1"], ["T2"])
                P.op("pool", lambda e: e.tensor_tensor(T[0], T[0], T[3], ALU.mult), ["T0", "T3"], ["T0"])
                P.op("dve", lambda e: e.scalar_tensor_tensor(BT.bitcast(F32R), T[0], -1.0, T[2], ALU.mult, ALU.mult), ["T0", "T2"], ["BT"])
                P.op("dve", lambda e, hp=hp: e.tensor_scalar(T[3], T[3], col(KA, hp), col(OMKA, hp), ALU.mult, ALU.add), ["T3", "rwp"], ["T3"])
                P.op("pool", lambda e: e.tensor_tensor(T[0], KK, T[3], ALU.mult), ["KK", "T3"], ["T0"])
                P.op("dve", lambda e: e.tensor_tensor(KK.bitcast(F32R), T[0], T[2], ALU.mult), ["T0", "T2"], ["KK"])
                shifted(RT, T[3], "RT", "T3", RW0 + hp * 128, t0)
                tshift(RT, T[3], col(MU, hp), "RT", "T3")
                P.op("pool", lambda e: e.tensor_tensor(T[0], T[0], RT, ALU.mult), ["T0", "RT"], ["T0"])
                P.op("dve", lambda e, hp=hp: e.tensor_scalar(T[0], T[0], col(RK, hp), None, ALU.mult), ["T0", "rwp"], ["T0"])
                P.op("act", lambda e: e.activation(out=T[2], in_=T[1], func=AF.Exp), ["T1"], ["T2"])
                P.op("dve", lambda e: e.tensor_tensor(RT.bitcast(F32R), RT, T[2], ALU.mult), ["RT", "T2"], ["RT"])
                shifted(VT, T[3], "VT", "T3", RW0 + 1024 + hp * 128, t0)
                tshift(VT, T[3], col(MU, 8 + hp), "VT", "T3")
                for g4 in range(HS // 512):
                    gs = slice(g4 * 512, (g4 + 1) * 512)
                    pq = self.psb[g4 % 2]
                    P.op("act", lambda e, gs=gs: e.copy(tqr.bitcast(F32R), T[0][:, gs]), ["T0"], ["tqr"])
                    P.op("pe", lambda e, pq=pq: e.matmul(pq[:], blk.bitcast(F32R), tqr.bitcast(F32R), start=True, stop=True), ["blk", "tqr"], ["ps%d" % (g4 % 2)])
                    P.op("dve", lambda e, pq=pq, gs=gs: e.tensor_tensor(BV[:, gs].bitcast(F32R), pq[:], VT[:, gs], ALU.mult), ["ps%d" % (g4 % 2), "VT"], ["BV"])
                def front(sb, bi, hp=hp):
                    ts_ = slice(sb * 128, (sb + 1) * 128)
                    pA, pB, pC, ptr = self.psb[0], self.psb[1], self.psb[2], self.psb[4]
                    BTm, KKm, KTm = BTm2[bi], KKm2[bi], KTm2[bi]
                    tok, Vc, Vpad = tok2[bi], Vc2[bi], Vpad2[bi]
                    AkTm, BrbTm, BrkTm = AkTm2[bi], BrbTm2[bi], BrkTm2[bi]
                    B = str(bi)
                    for h in range(2):
                        for dst_, src_, tg in ((BTm, BT, "BT"), (KKm, KK, "KK"), (KTm, KT, "KT")):
                            P.op("pool" if h else "dve", lambda e, dst_=dst_, src_=src_, h=h, ts_=ts_: e.tensor_scalar(dst_[h].bitcast(F32R), src_[:, ts_], hm[:, h:h + 1], None, ALU.mult),
                                 [tg, "hm"], ["m%s%d%s" % (tg, h, B)])
                    for i_, (src_, tg) in enumerate(((BT, "BT"), (KK, "KK"), (VT, "VT"))):
                        P.op("pe", lambda e, i_=i_, src_=src_, ts_=ts_, ptr=ptr: e.transpose(ptr[:, i_ * 128:(i_ + 1) * 128], src_[:, ts_], self.ident[:]), [tg, "ident"], ["ps4"])
                    yield
                    for i_ in range(3):
                        self.copy("act", tok[i_], ptr[:, i_ * 128:(i_ + 1) * 128], ["ps4"], ["tok%d%s" % (i_, B)])
                    yield
                    for c in range(2):
                        P.op("pool", lambda e, c=c, Vc=Vc, tok=tok: e.tensor_copy(Vc[c][64 * c:64 * c + 64, :], tok[2][64 * c:64 * c + 64, :]), ["tok2" + B, "small"], ["Vc%d%s" % (c, B)])
                        P.op("pool", lambda e, c=c, Vpad=Vpad, tok=tok: e.tensor_copy(Vpad[c][:, 64 * c:64 * c + 64], tok[2][:, 64 * c:64 * c + 64]), ["tok2" + B, "small"], ["Vpad%d%s" % (c, B)])
                    for h in range(2):
                        hs = slice(h * 128, (h + 1) * 128)
                        P.op("pe", lambda e, h=h, hs=hs, ts_=ts_, BTm=BTm: e.matmul(pA[:, hs], BTm[h].bitcast(F32R), KT[:, ts_].bitcast(F32R), start=True, stop=True, skip_group_check=True), ["mBT%d%s" % (h, B), "KT"], ["ps0"])
                        P.op("pe", lambda e, h=h, hs=hs, ts_=ts_, KTm=KTm: e.matmul(pA[:, 256 + h * 128:384 + h * 128], KTm[h].bitcast(F32R), BT[:, ts_].bitcast(F32R), start=True, stop=True, skip_group_check=True), ["mKT%d%s" % (h, B), "BT"], ["ps0"])
                        P.op("pe", lambda e, h=h, hs=hs, ts_=ts_, KKm=KKm: e.matmul(pB[:, hs], KKm[h].bitcast(F32R), KT[:, ts_].bitcast(F32R), start=True, stop=True, skip_group_check=True), ["mKK%d%s" % (h, B), "KT"], ["ps1"])
                        P.op("pe", lambda e, h=h, hs=hs, ts_=ts_, BTm=BTm: e.matmul(pC[:, hs], BTm[h].bitcast(F32R), RT[:, ts_].bitcast(F32R), start=True, stop=True, skip_group_check=True), ["mBT%d%s" % (h, B), "RT"], ["ps2"])
                        P.op("pe", lambda e, h=h, hs=hs, ts_=ts_, KKm=KKm: e.matmul(pC[:, 256 + h * 128:384 + h * 128], KKm[h].bitcast(F32R), RT[:, ts_].bitcast(F32R), start=True, stop=True, skip_group_check=True), ["mKK%d%s" % (h, B), "RT"], ["ps2"])
                    yield
                    P.op("dve", lambda e: e.tensor_tensor(ATm, pA[:, 0:256], msk[:, 0, :], ALU.mult), ["ps0", "msk"], ["ATm"])
                    P.op("dve", lambda e: e.tensor_tensor(Am, pA[:, 256:512], msk[:, 1, :], ALU.mult), ["ps0", "msk"], ["Am"])
                    P.op("dve", lambda e, AkTm=AkTm: e.tensor_tensor(AkTm, pB[:, 0:256], msk[:, 0, :], ALU.mult), ["ps1", "msk"], ["AkTm" + B])
                    P.op("dve", lambda e, BrbTm=BrbTm: e.tensor_tensor(BrbTm, pC[:, 0:256], msk[:, 2, :], ALU.mult), ["ps2", "msk"], ["BrbTm" + B])
                    P.op("dve", lambda e, BrkTm=BrkTm: e.tensor_tensor(BrkTm, pC[:, 256:512], msk[:, 2, :], ALU.mult), ["ps2", "msk"], ["BrkTm" + B])
                    P.op("pool", lambda e: e.tensor_tensor(MTk[0], ATm, idn2, ALU.add), ["ATm", "idn2"], ["MT0"])
                    yield
                    curP, curPT, curM = Am, ATm, 0
                    for lev in range(5):
                        last = lev == 4
                        nP, nPT = Pk[lev % 2], PTk[lev % 2]
                        for h in range(2):
                            hs = slice(h * 128, (h + 1) * 128)
                            P.op("pe", lambda e, hs=hs, curP=curP, curPT=curPT: e.matmul(pA[:, hs], curPT[:, hs], curP[:, hs], start=True, stop=True, skip_group_check=True),
                                 ["Pc", "ATm", "Am"], ["ps0"])
                            if not last:
                                P.op("pe", lambda e, h=h, curP=curP, curPT=curPT, hs=hs: e.matmul(pA[:, 256 + h * 128:384 + h * 128], curP[:, hs], curPT[:, hs], start=True, stop=True, skip_group_check=True),
                                     ["Pc", "ATm", "Am"], ["ps0"])
                        yield
                        self.copy("act", nP, pA[:, 0:256], ["ps0"], ["Pc"])
                        if not last:
                            self.copy("dve", nPT, pA[:, 256:512], ["ps0"], ["Pc"])
                        yield
                        for h in range(2):
                            hs = slice(h * 128, (h + 1) * 128)
                            P.op("pe", lambda e, hs=hs, nP=nP, curM=curM: e.matmul(pB[:, hs], nP[:, hs], MTk[curM][:, hs], start=True, stop=True, skip_group_check=True),
                                 ["Pc", "MT%d" % curM], ["ps1"])
                        yield
                        if last:
                            P.op("dve", lambda e, curM=curM, bi=bi: e.tensor_tensor(MTfin[bi], MTk[curM], pB[:, 0:256], ALU.add), ["ps1", "MT%d" % curM], ["MTfin" + B])
                        else:
                            P.op("dve", lambda e, curM=curM: e.tensor_tensor(MTk[1 - curM], MTk[curM], pB[:, 0:256], ALU.add), ["ps1", "MT%d" % curM], ["MT%d" % (1 - curM)])
                        yield
                        curP, curPT, curM = nP, nPT, 1 - curM

                def back(sb, bi, hp=hp, t0=t0):
                    ts_ = slice(sb * 128, (sb + 1) * 128)
                    psm, pO = self.psb[3], self.psb[5 + sb % 2]
                    tO = "ps%d" % (5 + sb % 2)
                    tok, Vc, Vpad = tok2[bi], Vc2[bi], Vpad2[bi]
                    AkTm, BrbTm, BrkTm, MT = AkTm2[bi], BrbTm2[bi], BrkTm2[bi], MTfin[bi]
                    B = str(bi)
                    tM = "MTfin" + B
                    for h in range(2):
                        P.op("pe", lambda e, h=h, AkTm=AkTm, tok=tok: e.matmul(psm[:, h * 64:(h + 1) * 64], AkTm[:, h * 128:(h + 1) * 128], tok[2][:, h * 64:(h + 1) * 64], start=True, stop=True, skip_group_check=True),
                             ["AkTm" + B, "tok2" + B], ["ps3"])
                    yield
                    self.copy("act", WkV, psm[:, 0:128], ["ps3"], ["WkV"])
                    for h in range(2):
                        P.op("pe", lambda e, pO=pO, h=h, Vpad=Vpad, BrkTm=BrkTm: e.matmul(pO[:, 0:128], Vpad[h], BrkTm[:, h * 128:(h + 1) * 128], start=(h == 0), stop=False, skip_group_check=True),
                             ["Vpad%d%s" % (h, B), "BrkTm" + B], [tO])
                    yield
                    for c in range(2):
                        cs = slice(64 * c, 64 * c + 64)
                        tcs = slice(sb * 128 + 64 * c, sb * 128 + 64 * c + 64)
                        P.op("pe", lambda e, pO=pO, cs=cs, tcs=tcs: e.matmul(pO[:, cs], Pbd, RT[:, tcs], start=False, stop=False, skip_group_check=True), ["Pbd", "RT"], [tO])
                        P.op("pe", lambda e, ts_=ts_: e.matmul(psm[:, 128:256], KT[:, ts_], Pbd, start=True, stop=True, skip_group_check=True), ["Pbd", "KT"], ["ps3"])
                        yield
                        P.op("dve", lambda e, cs=cs: e.tensor_tensor(Wsb[cs, :], psm[cs, 128:256], WkV[cs, :], ALU.add), ["ps3", "WkV", "small"], ["Wsb"])
                        yield
                        for h in range(2):
                            P.op("pe", lambda e, h=h, MT=MT: e.matmul(psm[:, 256 + h * 64:320 + h * 64], MT[:, h * 128:(h + 1) * 128], Wsb[:, h * 64:(h + 1) * 64],
                                                                  start=True, stop=True, skip_group_check=True),
                                 [tM, "Wsb"], ["ps3"])
                        yield
                        for h in range(2):
                            self.copy("dve", Upad[c][h][cs, 64 * h:64 * h + 64], psm[cs, 256 + h * 64:320 + h * 64], ["ps3", "small"], ["Up%d%d" % (c, h)])
                        yield
                        for h in range(2):
                            P.op("pe", lambda e, pO=pO, h=h, c=c, BrbTm=BrbTm: e.matmul(pO[:, 0:128], Upad[c][h], BrbTm[:, h * 128:(h + 1) * 128], start=False, stop=(c == 1 and h == 1), skip_group_check=True),
                                 ["Up%d%d" % (c, h), "BrbTm" + B], [tO])
                        P.op("pe", lambda e, c=c, tok=tok: e.matmul(psm[:, 384:512], tok[0], Upad[c][0], start=True, stop=False, skip_group_check=True), ["tok0" + B, "Up%d0" % c], ["ps3"])
                        P.op("pe", lambda e, c=c, tok=tok: e.matmul(psm[:, 384:512], tok[0], Upad[c][1], start=False, stop=False, skip_group_check=True), ["tok0" + B, "Up%d1" % c], ["ps3"])
                        P.op("pe", lambda e, c=c, tok=tok, Vc=Vc: e.matmul(psm[:, 384:512], tok[1], Vc[c], start=False, stop=True, skip_group_check=True), ["tok1" + B, "Vc%d%s" % (c, B)], ["ps3"])
                        yield
                        gi = sb * 2 + c
                        P.op("dve", lambda e, gi=gi: e.scalar_tensor_tensor(Ptmp, psm[:, 384:512], gam[:, gi:gi + 1], blkm, ALU.mult, ALU.mult), ["ps3", "gam", "blkm"], ["Ptmp"])
                        P.op("dve", lambda e, gi=gi: e.scalar_tensor_tensor(Pbd, Pbd, gam[:, gi:gi + 1], Ptmp, ALU.mult, ALU.add), ["Pbd", "gam", "Ptmp"], ["Pbd"])
                        yield
                    self.copy("act", OTs[:, (sb % 4) * 128:(sb % 4 + 1) * 128], pO[:, 0:128], [tO], ["OTs"])
                    yield
                    if sb % 4 == 3:
                        gs = slice((sb // 4) * 512, (sb // 4 + 1) * 512)
                        pm, pq2 = self.psb[6 - sb % 2], self.psb[7]
                        tpm, tpq = "ps%d" % (6 - sb % 2), "ps7"
                        P.op("pool", lambda e: e.tensor_tensor(tq[0], OTs, OTs, ALU.mult), ["OTs", "tq0"], ["tq0"])
                        P.op("pe", lambda e, pm=pm: e.matmul(pm[:], blk, OTs, start=True, stop=True), ["blk", "OTs"], [tpm])
                        P.op("pe", lambda e, pq2=pq2: e.matmul(pq2[:], blk, tq[0], start=True, stop=True), ["blk", "tq0"], [tpq])
                        yield
                        P.op("act", lambda e, pm=pm: e.activation(out=tq[1], in_=pm[:], func=AF.Copy, scale=1.0 / 64.0), [tpm], ["tq1"])
                        P.op("dve", lambda e: e.tensor_tensor(tq[2], tq[1], tq[1], ALU.mult), ["tq1", "tq2"], ["tq2"])
                        P.op("dve", lambda e, pq2=pq2: e.scalar_tensor_tensor(tq[2], pq2[:], 1.0 / 64.0, tq[2], ALU.mult, ALU.subtract), [tpq, "tq2"], ["tq2"])
                        P.op("act", lambda e: e.activation(out=tq[2], in_=tq[2], func=AF.Sqrt, bias=self.gnepsT[:], scale=1.0), ["tq2", "gneps"], ["tq2"])
                        P.op("dve", lambda e: e.reciprocal(tq[2], tq[2]), ["tq2"], ["tq2"])
                        yield
                        P.op("pool", lambda e: e.tensor_tensor(tq[1], OTs, tq[1], ALU.subtract), ["OTs", "tq1"], ["tq1"])
                        P.op("dve", lambda e: e.tensor_tensor(tq[1], tq[1], tq[2], ALU.mult), ["tq1", "tq2"], ["tq1"])
                        P.op("dve", lambda e, hp=hp: e.tensor_scalar(tq[1], tq[1], col(GNG, hp), col(GNB, hp), ALU.mult, ALU.add), ["tq1", "rwp"], ["tq1"])
                        P.op("pool", lambda e, gs=gs: e.tensor_tensor(tq[1], tq[1], BV[:, gs], ALU.add), ["tq1", "BV"], ["tq1"])
                        P.op("dve", lambda e, gs=gs: e.tensor_tensor(tq[1], tq[1], GT[:, gs], ALU.mult), ["tq1", "GT"], ["tq1"])
                        self.out_rows(tq[1], 1536 + hp * 128, t0 + (sb // 4) * 512, 512, "tq1")

                def drive(gens):
                    gens = [g for g in gens if g is not None]
                    while gens:
                        for g in list(gens):
                            try:
                                next(g)
                            except StopIteration:
                                gens.remove(g)

                drive([front(0, 0)])
                for sb in range(NSB):
                    drive([back(sb, sb % 2), front(sb + 1, (sb + 1) % 2) if sb + 1 < NSB else None])

    def ln_store(self, yt, g_bc, b_bc, dst_rows, tagy):
        P = self.P
        st = self.AF(12288, 24).rearrange("p (a b) -> p a b", b=6)
        mv = self.AF(12288 + 24, 2)
        rs = self.AF(12288 + 26, 1)
        for c in range(4):
            P.op("dve", lambda e, c=c: e.bn_stats(st[:, c, :], yt[:, c * 512:(c + 1) * 512]), [tagy], ["lnst%d" % c])
        P.op("dve", lambda e: e.bn_aggr(mv, st), ["lnst%d" % c for c in range(4)], ["lnmv"])
        P.op("act", lambda e: e.activation(out=rs, in_=mv[:, 1:2], func=AF.Sqrt, bias=self.epsT[:], scale=1.0), ["lnmv", "epsT"], ["lnrs"])
        P.op("dve", lambda e: e.reciprocal(rs, rs), ["lnrs"], ["lnrs"])
        P.op("dve", lambda e: e.tensor_scalar(yt, yt, mv[:, 0:1], rs, ALU.subtract, ALU.mult), [tagy, "lnmv", "lnrs"], [tagy])
        P.op("pool", lambda e: e.tensor_tensor(yt, yt, g_bc, ALU.mult), [tagy, "lngb"], [tagy])
        P.op("pool", lambda e: e.tensor_tensor(yt, yt, b_bc, ALU.add), [tagy, "lngb"], [tagy])
        P.dma(dst_rows, yt, reads=[tagy])

    def back_to_tokens(self, yT, tagT, xsrc, t0, g_bc, b_bc, dst, tagdiv=1):
        P = self.P
        for i in range(4):
            xt = self.AF((i % 2) * 2048, 2048)
            yt = self.AF((2 + i % 2) * 2048, 2048)
            P.dma(xt, xsrc[t0 + i * 128:t0 + (i + 1) * 128, :], writes=["F%d" % (i % 2)], q="act")
            for q4 in range(4):
                ps = self.psb[6 + q4 % 2]
                for n in range(4):
                    k = q4 * 4 + n
                    P.op("pe", lambda e, ps=ps, n=n, k=k, i=i: e.transpose(ps[:, n * 128:(n + 1) * 128], yT[:, k, i * 128:(i + 1) * 128], self.ident[:]),
                         [tagT + str(k // tagdiv), "ident"], ["psy%d" % (q4 % 2)])
                P.op("dve", lambda e, ps=ps, q4=q4, xt=xt, yt=yt: e.scalar_tensor_tensor(yt[:, q4 * 512:(q4 + 1) * 512], xt[:, q4 * 512:(q4 + 1) * 512], ALPHA, ps[:],
                                                                                  ALU.mult, ALU.add),
                     ["psy%d" % (q4 % 2), "F%d" % (i % 2)], ["F%d" % (2 + i % 2)])
            self.ln_store(yt, g_bc, b_bc, dst[t0 + i * 128:t0 + (i + 1) * 128, :], "F%d" % (2 + i % 2))

    def load_gb(self, g, b, l):
        P = self.P
        g_bc = self.AF(8192, 2048)
        b_bc = self.AF(8192 + 2048, 2048)
        P.dma(g_bc, g[l:l + 1, :].to_broadcast([128, D]), writes=["lngb"])
        P.dma(b_bc, b[l:l + 1, :].to_broadcast([128, D]), writes=["lngb"])
        return g_bc, b_bc

    def stage_wout(self, l, xin):
        P, S = self.P, self.S
        off_c = 0
        off_m = off_c + 16384
        off_w = off_m + 8192
        g_bc, b_bc = self.load_gb(self.ln1_g, self.ln1_b, l)
        for tb in range(S // TB):
            t0 = tb * TB
            cT = self.A(off_c + (tb % 2) * 8192, 8192).rearrange("p (a b) -> p a b", b=512)
            ctag = "cT%d" % (tb % 2)
            P.dma(cT.bitcast(F32R), self.catT[:, t0:t0 + TB].rearrange("(a p) t -> p a t", p=128).bitcast(F32R), writes=[ctag], q="pool")
            mT = self.A(off_m, 8192).rearrange("p (a b) -> p a b", b=512)
            for j in range(16):
                wi = (tb * 16 + j) % 3
                wt = self.A(off_w + wi * 2048, 2048)
                P.dma(wt.bitcast(F32R), self.w_out[l, j].bitcast(F32R), writes=["w%d" % wi], q="pool")
                ps = self.psb[2 + j % 4]
                for a in range(16):
                    P.op("pe", lambda e, ps=ps, wt=wt, a=a, cT=cT: e.matmul(ps[:], wt[:, a * 128:(a + 1) * 128].bitcast(F32R), cT[:, a, :].bitcast(F32R),
                                                                        start=(a == 0), stop=(a == 15)),
                         ["w%d" % wi, ctag], ["psm%d" % (j % 4)])
                P.op("act", lambda e, ps=ps, j=j: e.activation(out=mT[:, j, :].bitcast(F32R), in_=ps[:], func=AF.Copy, scale=self.modT[:, 32 + j:33 + j]),
                     ["psm%d" % (j % 4), "modT"], ["mT%d" % j])
            self.back_to_tokens(mT, "mT", xin, t0, g_bc, b_bc, self.xB)

    def stage_ffn(self, l, xout):
        P, S = self.P, self.S
        off_yacc = 0
        off_h = 8192
        off_g = off_h + 8192
        off_w = off_g + 11264
        off_w2 = off_w + 6144
        assert off_w2 + 2 * 2816 <= ARENA_R
        g_bc, b_bc = self.load_gb(self.ln2_g, self.ln2_b, l)
        wc = 0
        for tb in range(S // TB):
            t0 = tb * TB
            hT = self.A(off_h, 8192).rearrange("p (a b) -> p a b", b=512)
            self.load_xT(self.xB, t0, hT, 3, 4, "fh")
            yacc = self.A(off_yacc, 8192).rearrange("p (a b) -> p a b", b=512)
            gT = self.A(off_g, 11264).rearrange("p (a b) -> p a b", b=512)
            for half in range(2):
                for fl in range(22):
                    f = half * 22 + fl
                    pss = []
                    for wi_, wsrc in enumerate((self.w1, self.w3)):
                        wi = wc % 3
                        wc += 1
                        wt = self.A(off_w + wi * 2048, 2048)
                        P.dma(wt.bitcast(F32R), wsrc[l, f].bitcast(F32R), writes=["w%d" % wi], q="pool")
                        ps = self.psb[2 + (2 * fl + wi_) % 4]
                        ptag = "psm%d" % ((2 * fl + wi_) % 4)
                        for a in range(16):
                            P.op("pe", lambda e, ps=ps, wt=wt, a=a: e.matmul(ps[:], wt[:, a * 128:(a + 1) * 128].bitcast(F32R), hT[:, a, :].bitcast(F32R),
                                                                         start=(a == 0), stop=(a == 15)),
                                 ["w%d" % wi, "fh%d" % a], [ptag])
                        pss.append((ps, ptag))
                    sl = self.A(off_w2 + 2 * 2816 - 512, 512) if False else None
                    gt = gT[:, fl, :]
                    P.op("act", lambda e, gt=gt, ps=pss[0][0]: e.activation(out=gt.bitcast(F32R), in_=ps[:], func=AF.Silu), [pss[0][1]], ["g%d" % fl])
                    P.op("dve", lambda e, gt=gt, ps=pss[1][0]: e.tensor_tensor(gt.bitcast(F32R), gt, ps[:], ALU.mult), [pss[1][1], "g%d" % fl], ["g%d" % fl])
                for j in range(16):
                    w2i = (half * 16 + j) % 2
                    w2t = self.A(off_w2 + w2i * 2816, 2816)
                    P.dma(w2t.bitcast(F32R), self.w2[l, j, :, half * 2816:(half + 1) * 2816].bitcast(F32R), writes=["w2_%d" % w2i], q="pool")
                    ps = self.psb[j % 2]
                    ptag = "pst%d" % (j % 2)
                    for fl in range(22):
                        P.op("pe", lambda e, ps=ps, w2t=w2t, fl=fl: e.matmul(ps[:], w2t[:, fl * 128:(fl + 1) * 128].bitcast(F32R), gT[:, fl, :].bitcast(F32R),
                                                                         start=(fl == 0), stop=(fl == 21)),
                             ["w2_%d" % w2i, "g%d" % fl], [ptag])
                    if half == 0:
                        P.op("act", lambda e, ps=ps, j=j: e.activation(out=yacc[:, j, :].bitcast(F32R), in_=ps[:], func=AF.Copy, scale=self.modT[:, 80 + j:81 + j]),
                             [ptag, "modT"], ["R%d" % (j // 4)])
                    else:
                        P.op("dve", lambda e, ps=ps, j=j: e.scalar_tensor_tensor(yacc[:, j, :].bitcast(F32R), ps[:], self.modT[:, 80 + j:81 + j], yacc[:, j, :], ALU.mult, ALU.add),
                             [ptag, "R%d" % (j // 4), "modT"], ["R%d" % (j // 4)])
            self.back_to_tokens(yacc, "R", self.xB, t0, g_bc, b_bc, xout, tagdiv=4)


def prep_common(inp, L):
    f = lambda a: np.ascontiguousarray(np.asarray(a, dtype=np.float32))
    m = {}
    m["ada_w"] = np.stack([wtile(f(inp["ada_w"][l])).reshape(96, 128, 2048) for l in range(L)])
    m["ada_bT"] = vecT(f(inp["ada_b"][:L]))
    m["w_in"] = np.stack([wtile(f(inp["w_in"][l])).reshape(46, 128, 2048) for l in range(L)])
    m["w_out"] = np.stack([wtile(f(inp["w_out"][l])).reshape(16, 128, 2048) for l in range(L)])
    m["w1"] = np.stack([wtile(f(inp["ffn_w1"][l])).reshape(44, 128, 2048) for l in range(L)])
    m["w3"] = np.stack([wtile(f(inp["ffn_w3"][l])).reshape(44, 128, 2048) for l in range(L)])
    m["w2"] = np.stack([wtile(f(inp["ffn_w2"][l])).reshape(16, 128, 44 * 128) for l in range(L)])
    for k in ("ln1_g", "ln1_b", "ln2_g", "ln2_b"):
        m[k] = f(inp[k][:L])
    m["ident"] = np.eye(128, dtype=np.float32)
    sp_, tp_ = np.arange(128)[:, None], np.arange(512)[None, :]
    m["sbmask"] = np.stack([(tp_ > mm * 128 + sp_) for mm in range(4)]).astype(np.float32)
    m["tri"] = (np.arange(128)[:, None] >= np.arange(128)[None, :]).astype(np.float32)
    m["ones"] = np.ones((128, 128), np.float32)
    a_ = np.arange(128)
    m["m32"] = ((a_[:, None] // 32 == a_[None, :] // 32) & (a_[:, None] <= a_[None, :])).astype(np.float32)
    m["rst"] = np.broadcast_to((np.arange(4096) % 32 != 0).astype(np.float32), (128, 4096)).copy()
    m["hg_lgT"] = np.ascontiguousarray(f(inp["hg_lb_logits"]).reshape(4, 4, 128).transpose(2, 0, 1).reshape(128, 16))
    m["s5_dT"] = np.ascontiguousarray(f(inp["s5_d"]).reshape(4, 4, 128).transpose(2, 0, 1).reshape(128, 16))
    m["s5_gbT"] = np.ascontiguousarray(f(inp["s5_glu_b"]).reshape(4, 4, 128).transpose(2, 0, 1).reshape(128, 16))
    m["s5_arT"] = vecT(f(inp["s5_a_re"][:L]).reshape(L, 2048))
    m["s5_aiT"] = vecT(f(inp["s5_a_im"][:L]).reshape(L, 2048))
    m["s5_ldtT"] = vecT(np.repeat(f(inp["s5_log_dt"][:L]), 64, axis=1))
    wb = np.zeros((L, 128, 2, 16, 128), np.float32)
    wcre = np.zeros((L, 128, 16, 128), np.float32)
    wcim = np.zeros((L, 128, 16, 128), np.float32)
    for g in range(32):
        a, gb = g // 2, g % 2
        r0 = 32 * (a % 4) + 16 * gb
        for ci, key in enumerate(("s5_b_re", "s5_b_im")):
            wb[:, r0:r0 + 16, ci, a, gb * 64:(gb + 1) * 64] = f(inp[key][:L, g]).transpose(0, 2, 1)
        c0 = 32 * (a % 4) + 16 * gb
        wcre[:, gb * 64:(gb + 1) * 64, a, c0:c0 + 16] = f(inp["s5_c_re"][:L, g]).transpose(0, 2, 1)
        wcim[:, gb * 64:(gb + 1) * 64, a, c0:c0 + 16] = f(inp["s5_c_im"][:L, g]).transpose(0, 2, 1)
    m["s5_wb"] = wb.reshape(L, 128, 4096)
    m["s5_wcre"] = wcre.reshape(L, 128, 2048)
    m["s5_wcim"] = wcim.reshape(L, 128, 2048)
    m["s5_glu"] = np.stack([wtile(f(inp["s5_glu_w"][l])).reshape(4, 128, 512) for l in range(L)])
    m["hm"] = np.stack([(a_ < 64), (a_ >= 64)], axis=1).astype(np.float32)
    m["blk64"] = (a_[:, None] // 64 == a_[None, :] // 64).astype(np.float32)
    m["rst64"] = np.broadcast_to((np.arange(2048) % 64 != 0).astype(np.float32), (128, 2048)).copy()
    same = a_[:, None] // 64 == a_[None, :] // 64
    m0 = (same & (a_[:, None] < a_[None, :])).astype(np.float32)
    m1 = (same & (a_[:, None] > a_[None, :])).astype(np.float32)
    m2 = (same & (a_[:, None] <= a_[None, :])).astype(np.float32)
    m["rwmask"] = np.stack([np.concatenate([mm, mm], axis=1) for mm in (m0, m1, m2)])
    m["rw_muT"] = vecT(f(inp["rw_mu"][:L]))
    for nm, key in (("rw_w0T", "rw_w0"), ("rw_a0T", "rw_a0"), ("rw_kkT", "rw_k_k"), ("rw_kaT", "rw_k_a"), ("rw_gngT", "rw_gn_g"), ("rw_gnbT", "rw_gn_b")):
        m[nm] = vecT(f(inp[key][:L]))
    m["rw_rkT"] = vecT(f(inp["rw_r_k"][:L]).reshape(L, 512))
    w2p = np.zeros((L, 128, 512), np.float32); w2p[:, 0:64] = f(inp["rw_w2"][:L])
    a2p = np.zeros((L, 128, 512), np.float32); a2p[:, 64:128] = f(inp["rw_a2"][:L])
    m["rw_w2p"], m["rw_a2p"], m["rw_g2"] = w2p, a2p, f(inp["rw_g2"][:L])
    m["hg_ngT"] = np.ascontiguousarray(f(inp["hg_norm_g"]).reshape(4, 4, 128).transpose(2, 0, 1).reshape(128, 16))
    return m


def run(inp, S=4096, L=DEPTH, ncores=8, **bk):
    b = Builder(S, L, **bk)
    nc = b.build()
    common = prep_common(inp, L)
    in_maps = []
    for c in range(ncores):
        m = dict(common)
        m["x"] = np.ascontiguousarray(np.asarray(inp["x"][c, :S], dtype=np.float32))
        m["cT"] = vecT(np.asarray(inp["c"][c], dtype=np.float32))
        in_maps.append(m)
    res = run_bass_kernel_spmd(nc, in_maps, core_ids=list(range(ncores)))
    b.results = res.results
    return np.stack([np.asarray(r["out"]) for r in res.results]).astype(np.float32), b


def kernel(**inputs):
    out, _ = run(inputs)
    return out
```

```python
import math
import numpy as np
import concourse.bass as bass
import concourse.mybir as mybir
from concourse.bass_utils import run_bass_kernel_spmd

F32 = mybir.dt.float32
F32R = mybir.dt.float32r
AF = mybir.ActivationFunctionType
ALU = mybir.AluOpType

D = 2048
NIN = 5888
DFF = 5632
DEPTH = 4
ALPHA = (2 * DEPTH) ** 0.25
LN_EPS = 1e-5
TB = 512
ENGS = ("pe", "act", "dve", "pool", "sp")
N_DMA_SEMS = 48
ARENA_R = 39424
ARENA_F = 12352


class Op:
    __slots__ = ("eng", "fn", "reads", "writes", "is_dma", "waits", "sem", "val", "vc", "idx", "bar")


class Prog:
    def __init__(self):
        self.nc = bass.Bass("TRN2", target_bir_lowering=False)
        self.ops = []
        self._stack = []

    def enter(self, cm):
        v = cm.__enter__()
        self._stack.append(cm)
        return v

    def sb(self, name, shape, dt=F32):
        return self.enter(self.nc.sbuf_tensor(name, list(shape), dt))

    def ps(self, name, shape, dt=F32):
        return self.enter(self.nc.psum_tensor(name, list(shape), dt))

    def op(self, eng, fn, reads=(), writes=(), dma=False):
        o = Op()
        o.eng, o.fn, o.reads, o.writes, o.is_dma, o.bar = eng, fn, tuple(reads), tuple(writes), dma, False
        self.ops.append(o)
        return o

    def dma(self, out, in_, reads=(), writes=(), q="sp", **kw):
        return self.op(q, lambda e: e.dma_start(out=out, in_=in_, **kw), reads, writes, dma=True)

    def barrier(self):
        o = Op()
        o.bar = True
        self.ops.append(o)

    def finish(self):
        nc = self.nc
        sems = {(e, k): self.enter(nc.semaphore("s_%s%d" % (e, k))) for e in ENGS for k in range(3)}
        dsems = [self.enter(nc.semaphore("d%d" % i)) for i in range(N_DMA_SEMS)]
        ep = 0
        cnt = {e: 1 for e in ENGS}
        dcnt = [0] * N_DMA_SEMS
        dlast = [None] * N_DMA_SEMS
        ndma = 0
        nsw = 0
        last_w, readers = {}, {}
        known = {e: {} for e in ENGS}
        per_eng = {e: [("mark", 0)] for e in ENGS}
        for i, o in enumerate(self.ops):
            if o.bar:
                tot = {("e", e, ep): cnt[e] for e in ENGS}
                for k in range(N_DMA_SEMS):
                    if dcnt[k]:
                        tot[("d", k)] = dcnt[k]
                for e in ENGS:
                    w = [(s, v) for s, v in tot.items() if known[e].get(s, 0) < v]
                    per_eng[e].append(("bar", w, ep))
                    known[e] = {s: v for s, v in tot.items() if s[0] == "d"}
                ep += 1
                cnt = {e: 1 for e in ENGS}
                last_w, readers = {}, {}
                continue
            o.idx = i
            deps = []
            if o.is_dma:
                half_ = N_DMA_SEMS // 2
                if o.eng == "pool":
                    k = half_ + nsw % half_
                    nsw += 1
                else:
                    k = ndma % half_
                    ndma += 1
                dcnt[k] += 16
                o.sem, o.val = ("d", k), dcnt[k]
                if dlast[k] is not None:
                    deps.append(dlast[k])
                dlast[k] = o
            else:
                cnt[o.eng] += 1
                o.sem, o.val = ("e", o.eng, ep), cnt[o.eng]
            for r in o.reads:
                w = last_w.get(r)
                if w is not None:
                    deps.append(w)
            for r in o.writes:
                w = last_w.get(r)
                if w is not None:
                    deps.append(w)
                deps.extend(readers.get(r, ()))
            kn = known[o.eng]
            need = {}
            for d in deps:
                if d is o:
                    continue
                if (not d.is_dma) and (not o.is_dma) and d.eng == o.eng == "pe":
                    continue
                if kn.get(d.sem, 0) >= d.val:
                    continue
                if need.get(d.sem, (0, None))[0] < d.val:
                    need[d.sem] = (d.val, d)
            waits = []
            for s_, (v, d) in sorted(need.items(), key=lambda kv: -kv[1][1].idx):
                if kn.get(s_, 0) >= v:
                    continue
                waits.append((s_, v))
                for s2, v2 in d.vc.items():
                    if kn.get(s2, 0) < v2:
                        kn[s2] = v2
                kn[s_] = max(kn.get(s_, 0), v)
            o.waits = waits
            vc = dict(kn)
            vc[o.sem] = o.val
            o.vc = vc
            for r in o.reads:
                readers.setdefault(r, []).append(o)
            for r in o.writes:
                last_w[r] = o
                readers[r] = []
            per_eng[o.eng].append(o)
        self.n_ops = sum(1 for o in self.ops if not o.bar)

        def semof(s_):
            return sems[(s_[1], s_[2] % 3)] if s_[0] == "e" else dsems[s_[1]]

        finals = [(("e", e, ep), cnt[e]) for e in ENGS]
        finals += [(("d", k), dcnt[k]) for k in range(N_DMA_SEMS) if dcnt[k]]
        blk = self.enter(nc.Block())

        def body(ename, with_final=False):
            def f(eng):
                for o in per_eng[ename]:
                    if isinstance(o, tuple):
                        if o[0] == "mark":
                            eng.nop().then_inc(sems[(ename, 0)], 1)
                            continue
                        for s_, v in o[1]:
                            eng.wait_ge(semof(s_), v)
                        e_old = o[2]
                        eng.sem_clear(sems[(ename, (e_old + 2) % 3)])
                        eng.nop().then_inc(sems[(ename, (e_old + 1) % 3)], 1)
                        continue
                    for s_, v in o.waits:
                        eng.wait_ge(semof(s_), v)
                    o.fn(eng).then_inc(semof(o.sem), 16 if o.is_dma else 1)
                if with_final:
                    for s_, v in finals:
                        eng.wait_ge(semof(s_), v)
            return f

        blk.tensor(body("pe"))
        blk.scalar(body("act"))
        blk.vector(body("dve"))
        blk.gpsimd(body("pool"))
        blk.sync(body("sp", with_final=True))
        while self._stack:
            self._stack.pop().__exit__(None, None, None)
        return nc


def wtile(w):
    K, N = w.shape
    return np.ascontiguousarray(w.reshape(K // 128, 128, N // 128, 128).transpose(2, 1, 0, 3))


def vecT(v):
    sh = v.shape
    return np.ascontiguousarray(np.swapaxes(v.reshape(sh[:-1] + (sh[-1] // 128, 128)), -1, -2))


class Builder:
    def __init__(self, S, L, stages=("ada", "proj", "mix", "wout", "ffn"), mix="hgrn+s5+sb+rwkv", dbg=False):
        self.S, self.L, self.stages, self.mixmode, self.dbg = S, L, stages, mix, dbg
        self.P = Prog()
        self.nc = self.P.nc
        self.rr = 0

    def din(self, name, shape):
        return self.nc.dram_tensor(name, list(shape), F32, kind="ExternalInput").ap()

    def dscratch(self, name, shape):
        return self.nc.dram_tensor(name, list(shape), F32, kind="Internal").ap()

    def A(self, off, n):
        return self.arenaR[:, off:off + n]

    def AF(self, off, n):
        return self.arenaF[:, off:off + n]

    def evac_eng(self):
        self.rr += 1
        return "act" if self.rr % 2 else "dve"

    def copy(self, eng, out, in_, reads, writes):
        if eng == "act":
            self.P.op("act", lambda e: e.copy(out, in_), reads, writes)
        else:
            self.P.op(eng, lambda e: e.tensor_copy(out, in_), reads, writes)

    def build(self):
        P, nc, S, L = self.P, self.nc, self.S, self.L
        NB = S // TB
        self.x_in = self.din("x", [S, D])
        self.cT = self.din("cT", [128, 16])
        self.ada_w = self.din("ada_w", [L, 96, 128, 16 * 128])
        self.ada_bT = self.din("ada_bT", [L, 128, 96])
        self.w_in = self.din("w_in", [L, 46, 128, 16 * 128])
        self.w_out = self.din("w_out", [L, 16, 128, 16 * 128])
        self.w1 = self.din("w1", [L, 44, 128, 16 * 128])
        self.w3 = self.din("w3", [L, 44, 128, 16 * 128])
        self.w2 = self.din("w2", [L, 16, 128, 44 * 128])
        self.ln1_g = self.din("ln1_g", [L, D])
        self.ln1_b = self.din("ln1_b", [L, D])
        self.ln2_g = self.din("ln2_g", [L, D])
        self.ln2_b = self.din("ln2_b", [L, D])
        self.ident_d = self.din("ident", [128, 128])
        self.sbmask_d = self.din("sbmask", [4, 128, 512])
        self.tri_d = self.din("tri", [128, 128])
        self.ones_d = self.din("ones", [128, 128])
        self.m32_d = self.din("m32", [128, 128])
        self.rst_d = self.din("rst", [128, 4096])
        self.hg_lgT = self.din("hg_lgT", [128, 16])
        self.hg_ngT = self.din("hg_ngT", [128, 16])
        self.s5_dT_d = self.din("s5_dT", [128, 16])
        self.hm_d = self.din("hm", [128, 2])
        self.blk64_d = self.din("blk64", [128, 128])
        self.rst64_d = self.din("rst64", [128, 2048])
        self.rwmask_d = self.din("rwmask", [3, 128, 256])
        self.rw_muT = self.din("rw_muT", [L, 128, 14])
        self.rw_w0T = self.din("rw_w0T", [L, 128, 4])
        self.rw_a0T = self.din("rw_a0T", [L, 128, 4])
        self.rw_kkT = self.din("rw_kkT", [L, 128, 4])
        self.rw_kaT = self.din("rw_kaT", [L, 128, 4])
        self.rw_rkT = self.din("rw_rkT", [L, 128, 4])
        self.rw_gngT = self.din("rw_gngT", [L, 128, 4])
        self.rw_gnbT = self.din("rw_gnbT", [L, 128, 4])
        self.rw_w2p = self.din("rw_w2p", [L, 128, 512])
        self.rw_a2p = self.din("rw_a2p", [L, 128, 512])
        self.rw_g2 = self.din("rw_g2", [L, 128, 512])
        self.s5_gbT_d = self.din("s5_gbT", [128, 16])
        self.s5_arT = self.din("s5_arT", [L, 128, 16])
        self.s5_aiT = self.din("s5_aiT", [L, 128, 16])
        self.s5_ldtT = self.din("s5_ldtT", [L, 128, 16])
        self.s5_wb = self.din("s5_wb", [L, 128, 2 * 16 * 128])
        self.s5_wcre = self.din("s5_wcre", [L, 128, 16 * 128])
        self.s5_wcim = self.din("s5_wcim", [L, 128, 16 * 128])
        self.s5_glu = self.din("s5_glu", [L, 4, 128, 4 * 128])
        self.out = self.nc.dram_tensor("out", [S, D], F32, kind="ExternalOutput").ap()
        self.xA = self.dscratch("xA", [S, D])
        self.xB = self.dscratch("xB", [S, D])
        if not self.dbg:
            self.projT = self.dscratch("projT", [NIN, S])
        if self.dbg:
            self.catT = self.nc.dram_tensor("catT", [D, S], F32, kind="ExternalOutput").ap()
            self.projT = self.nc.dram_tensor("projT", [NIN, S], F32, kind="ExternalOutput").ap()
        else:
            self.catT = self.dscratch("catT", [D, S])

        self.arenaR = P.sb("arenaR", [128, ARENA_R])
        self.arenaF = P.sb("arenaF", [128, ARENA_F])
        self.ident = P.sb("ident_sb", [128, 128])
        self.modT = P.sb("modT", [128, 96])
        self.cact = P.sb("cact", [128, 16])
        self.epsT = P.sb("epsT", [128, 1])
        self.oneT = P.sb("oneT", [128, 1])
        self.rmsT = P.sb("rmsT", [128, 1])
        self.ones_sb = P.sb("ones_sb", [128, 128])
        self.hg_small = P.sb("hg_small", [128, 3 * (S // 32)])
        self.hg_m32 = P.sb("hg_m32", [128, 128])
        self.hg_lg = P.sb("hg_lg", [128, 4 * 4])
        self.hg_lb = P.sb("hg_lb", [128, 4 * 4])
        self.hg_oml = P.sb("hg_oml", [128, 4 * 4])
        self.hg_ng = P.sb("hg_ng", [128, 4 * 4])
        self.hg_sum = P.sb("hg_sum", [128, 4])
        self.s5p = P.sb("s5p", [128, 20, 16])
        self.rwp = P.sb("rwp", [128, 48])
        self.rw_hm = P.sb("rw_hm", [128, 2])
        self.gnepsT = P.sb("gnepsT", [128, 1])
        self.s5_dT = P.sb("s5_dT_sb", [128, 16])
        self.s5_gbT = P.sb("s5_gbT_sb", [128, 16])
        self.psb = [P.ps("psb%d" % i, [128, 512]) for i in range(8)]
        P.dma(self.ident[:].bitcast(F32R), self.ident_d.bitcast(F32R), writes=["ident"], q="pool")
        P.dma(self.cact[:], self.cT, writes=["cact"])
        P.op("act", lambda e: e.activation(out=self.cact[:], in_=self.cact[:], func=AF.Silu), ["cact"], ["cact"])
        P.op("dve", lambda e: e.memset(self.epsT[:], LN_EPS), [], ["epsT"])
        P.op("dve", lambda e: e.memset(self.oneT[:], 1.0), [], ["oneT"])
        P.op("dve", lambda e: e.memset(self.rmsT[:], 1e-6), [], ["rmsT"])
        P.dma(self.ones_sb[:], self.ones_d, writes=["ones_sb"])
        P.dma(self.hg_lg[:], self.hg_lgT, writes=["hgp"])
        P.dma(self.hg_ng[:], self.hg_ngT, writes=["hgp"])
        P.dma(self.s5_dT[:], self.s5_dT_d, writes=["s5d"])
        P.dma(self.rw_hm[:], self.hm_d, writes=["hm"])
        P.op("dve", lambda e: e.memset(self.gnepsT[:], 64e-5), [], ["gneps"])
        P.dma(self.s5_gbT[:], self.s5_gbT_d, writes=["s5d"])
        lg3 = self.hg_lg[:].rearrange("p (l h) -> p l h", h=4)
        lb3 = self.hg_lb[:].rearrange("p (l h) -> p l h", h=4)
        P.op("act", lambda e: e.activation(out=self.hg_lg[:], in_=self.hg_lg[:], func=AF.Exp), ["hgp"], ["hgp"])
        P.op("dve", lambda e: e.tensor_tensor(self.hg_sum[:], lg3[:, 0, :], lg3[:, 1, :], ALU.add), ["hgp"], ["hgsum"])
        P.op("dve", lambda e: e.tensor_tensor(self.hg_sum[:], self.hg_sum[:], lg3[:, 2, :], ALU.add), ["hgp", "hgsum"], ["hgsum"])
        P.op("dve", lambda e: e.tensor_tensor(self.hg_sum[:], self.hg_sum[:], lg3[:, 3, :], ALU.add), ["hgp", "hgsum"], ["hgsum"])
        P.op("dve", lambda e: e.reciprocal(self.hg_sum[:], self.hg_sum[:]), ["hgsum"], ["hgsum"])
        P.op("dve", lambda e: e.memset(lb3[:, 0, :], 0.0), [], ["hglb"])
        for ll in range(1, 4):
            P.op("dve", lambda e, ll=ll: e.tensor_tensor(lg3[:, ll, :], lg3[:, ll, :], self.hg_sum[:], ALU.mult), ["hgp", "hgsum"], ["hgp"])
            P.op("dve", lambda e, ll=ll: e.tensor_tensor(lb3[:, ll, :], lb3[:, ll - 1, :], lg3[:, ll, :], ALU.add), ["hgp", "hglb"], ["hglb"])
        P.op("dve", lambda e: e.tensor_scalar(self.hg_oml[:], self.hg_lb[:], -1.0, 1.0, ALU.mult, ALU.add), ["hglb"], ["hglb"])
        P.barrier()
        for l in range(L):
            xin = self.x_in if l == 0 else self.xA
            xout = self.out if l == L - 1 else self.xA
            if "ada" in self.stages:
                self.stage_ada(l)
                P.barrier()
            if "proj" in self.stages:
                self.stage_proj(l, xin)
                P.barrier()
            if "mix" in self.stages:
                self.stage_mix(l)
                P.barrier()
            if "wout" in self.stages:
                self.stage_wout(l, xin)
                P.barrier()
            if "ffn" in self.stages:
                self.stage_ffn(l, xout)
                P.barrier()
        return P.finish()

    def stage_ada(self, l):
        P = self.P
        ps = self.psb[0]
        for j in range(96):
            wt = self.AF((j % 3) * 2048, 2048)
            rw = "adaw%d" % (j % 3)
            P.dma(wt, self.ada_w[l, j], writes=[rw], q=("sp" if j % 2 else "act"))
            for a in range(16):
                P.op("pe", lambda e, wt=wt, a=a, j=j: e.matmul(ps[:, j:j + 1], wt[:, a * 128:(a + 1) * 128],
                                                              self.cact[:, a:a + 1], start=(a == 0), stop=(a == 15)),
                     [rw, "cact"], ["ps_ada"])
        bt = self.AF(3 * 2048, 96)
        P.dma(bt, self.ada_bT[l], writes=["adab"])
        P.op("dve", lambda e: e.tensor_tensor(self.modT[:], ps[:, 0:96], bt, ALU.add), ["ps_ada", "adab"], ["modT"])
        for g in (1, 4):
            P.op("dve", lambda e, g=g: e.tensor_scalar_add(self.modT[:, g * 16:(g + 1) * 16], self.modT[:, g * 16:(g + 1) * 16], 1.0),
                 ["modT"], ["modT"])

    def load_xT(self, src, t0, hT, shift_g, scale_g, tag):
        P = self.P
        xt = [self.A(i * 2048, 2048) for i in range(4)]
        for i in range(4):
            P.dma(xt[i].bitcast(F32R), src[t0 + i * 128:t0 + (i + 1) * 128, :].bitcast(F32R), writes=["R%d" % i], q="pool")
        for k in range(16):
            ps = self.psb[k % 2]
            for i in range(4):
                P.op("pe", lambda e, ps=ps, i=i, k=k: e.transpose(ps[:, i * 128:(i + 1) * 128], xt[i][:, k * 128:(k + 1) * 128], self.ident[:]),
                     ["R%d" % i, "ident"], ["pst%d" % (k % 2)])
            eng = "dve" if k % 2 else "pool"
            eng = "dve"
            P.op(eng, lambda e, ps=ps, k=k: e.tensor_scalar(hT[:, k, :].bitcast(F32R), ps[:], self.modT[:, scale_g * 16 + k:scale_g * 16 + k + 1],
                                                            self.modT[:, shift_g * 16 + k:shift_g * 16 + k + 1], ALU.mult, ALU.add),
                 ["pst%d" % (k % 2), "modT"], ["%s%d" % (tag, k)])
        return xt

    def stage_proj(self, l, xin):
        P, S = self.P, self.S
        off_h = 8192
        off_w = off_h + 2 * 8192
        off_o = 0
        for tb in range(S // TB):
            t0 = tb * TB
            hT = self.A(off_h + (tb % 2) * 8192, 8192).rearrange("p (a b) -> p a b", b=512)
            tag = "hT%d_" % (tb % 2)
            self.load_xT(xin, t0, hT, 0, 1, tag)
            for j in range(46):
                wi = (tb * 46 + j) % 3
                wt = self.A(off_w + wi * 2048, 2048)
                P.dma(wt.bitcast(F32R), self.w_in[l, j].bitcast(F32R), writes=["w%d" % wi], q="pool")
                ps = self.psb[2 + j % 4]
                for a in range(16):
                    P.op("pe", lambda e, ps=ps, wt=wt, a=a, hT=hT: e.matmul(ps[:], wt[:, a * 128:(a + 1) * 128].bitcast(F32R), hT[:, a, :].bitcast(F32R),
                                                                        start=(a == 0), stop=(a == 15)),
                         ["w%d" % wi, tag + str(a)], ["psm%d" % (j % 4)])
                oi = j % 4
                ot = self.AF(off_o + oi * 512, 512)
                self.copy(self.evac_eng(), ot, ps[:], ["psm%d" % (j % 4)], ["ot%d" % oi])
                P.dma(self.projT[j * 128:(j + 1) * 128, t0:t0 + TB], ot, reads=["ot%d" % oi], q=("sp" if j % 2 else "act"))

    def stage_mix(self, l):
        if self.mixmode == "stub":
            P, S = self.P, self.S
            for j in range(16):
                for h in range(S // 2048 if S >= 2048 else 1):
                    n = min(S, 2048)
                    t = self.AF((j % 2) * 2048, n)
                    P.dma(t, self.projT[j * 128:(j + 1) * 128, h * n:(h + 1) * n], writes=["mx%d" % (j % 2)])
                    P.dma(self.catT[j * 128:(j + 1) * 128, h * n:(h + 1) * n], t, reads=["mx%d" % (j % 2)])
        else:
            P = self.P
            for nm in self.mixmode.split("+"):
                getattr(self, "mix_" + nm)(l)
                P.barrier()


    def out_rows(self, src_tile_ap, row0, t0, n, tag, q="sp"):
        self.P.dma(self.catT[row0:row0 + 128, t0:t0 + n], src_tile_ap, reads=[tag], q=q)

    def mix_sb(self, l):
        P, S = self.P, self.S
        NBK = S // 128
        NG = S // TB
        ND = 3
        isq = 1.0 / math.sqrt(128.0)
        qT = self.A(0, S)
        kT = self.A(S, S)
        vT = self.A(2 * S, S)
        vtok = self.A(3 * S, S).rearrange("p (a b) -> p a b", b=128)
        wTb = [self.A(4 * S + i * 512, 512) for i in range(ND)]
        sph = [self.A(4 * S + 1536 + i * 512, 512) for i in range(ND)]
        spl = [self.A(4 * S + 3072 + i * 512, 512) for i in range(ND)]
        tri = self.A(4 * S + 4608, 128)
        ones = self.A(4 * S + 4736, 128)
        eb = [self.AF(i * 512, 512) for i in range(ND)]
        spb = [self.AF(1536 + i * 512, 512) for i in range(ND)]
        btb = [self.AF(3072 + i * 512, 512) for i in range(ND)]
        Cb = self.AF(4608, 512)
        ob = self.AF(5120, 512)
        msk = self.AF(5632, 2048).rearrange("p (a b) -> p a b", b=512)
        P.dma(msk, self.sbmask_d.rearrange("a p t -> p a t"), writes=["msk"])
        P.dma(tri.bitcast(F32R), self.tri_d.bitcast(F32R), writes=["tri"], q="pool")
        P.dma(ones.bitcast(F32R), self.ones_d.bitcast(F32R), writes=["ones"], q="pool")
        pa, pc, po = self.psb[0:3], self.psb[3:6], self.psb[6:8]
        it = 0
        for h in range(4):
            P.dma(qT.bitcast(F32R), self.projT[2560 + h * 128:2560 + (h + 1) * 128, :].bitcast(F32R), writes=["qT"], q="pool")
            P.dma(kT.bitcast(F32R), self.projT[3072 + h * 128:3072 + (h + 1) * 128, :].bitcast(F32R), writes=["kT"], q="pool")
            P.dma(vT.bitcast(F32R), self.projT[3584 + h * 128:3584 + (h + 1) * 128, :].bitcast(F32R), writes=["vT"], q="pool")
            P.op("act", lambda e: e.activation(out=kT.bitcast(F32R), in_=kT, func=AF.Copy, scale=-isq), ["kT"], ["kT"])
            for b in range(NBK):
                ps = self.psb[b % 2]
                P.op("pe", lambda e, ps=ps, b=b: e.transpose(ps[:, 0:128], vT[:, b * 128:(b + 1) * 128], self.ident[:]), ["vT", "ident"], ["ps%d" % (b % 2)])
                self.copy("dve" if b % 2 else "act", vtok[:, b, :].bitcast(F32R), ps[:, 0:128], ["ps%d" % (b % 2)], ["vtok%d" % b])
            def pair(g, kb, it, h=h):
                qg = qT[:, g * TB:(g + 1) * TB].bitcast(F32R)
                first = kb == 4 * g + 3
                psO = po[g % 2]
                otag = "ps%d" % (6 + g % 2)
                i3 = it % ND
                kblk = kT[:, kb * 128:(kb + 1) * 128].bitcast(F32R)
                A_, C_ = pa[i3], pc[i3]
                e_, sp_, w_, sh_, sl_, bt_ = eb[i3], spb[i3], wTb[i3], sph[i3], spl[i3], btb[i3]
                tA, tC = "ps%d" % i3, "ps%d" % (3 + i3)
                m = kb - 4 * g
                P.op("pe", lambda e: e.matmul(A_[:], kblk, qg, start=True, stop=False, skip_group_check=True), ["kT", "qT"], [tA])
                yield
                P.op("act", lambda e: e.activation(out=e_, in_=A_[:], func=AF.Exp, scale=-1.0), [tA], ["e%d" % i3])
                yield
                P.op("act", lambda e: e.activation(out=sp_, in_=e_, func=AF.Ln, bias=self.oneT[:], scale=1.0), ["e%d" % i3, "oneT"], ["sp%d" % i3])
                P.op("act", lambda e: e.activation(out=sh_.bitcast(F32R), in_=e_, func=AF.Ln, bias=self.oneT[:], scale=1.0), ["e%d" % i3, "oneT"], ["sh%d" % i3])
                yield
                if m >= 0:
                    P.op("pool", lambda e: e.tensor_tensor(sp_, sp_, msk[:, m, :], ALU.mult), ["sp%d" % i3, "msk"], ["sp%d" % i3])
                    P.op("dve", lambda e: e.tensor_tensor(sh_.bitcast(F32R), sh_, msk[:, m, :], ALU.mult), ["sh%d" % i3, "msk"], ["sh%d" % i3])
                yield
                P.op("dve", lambda e: e.tensor_tensor(sl_.bitcast(F32R), sp_, sh_, ALU.subtract), ["sp%d" % i3, "sh%d" % i3], ["sl%d" % i3])
                yield
                P.op("pe", lambda e: e.matmul(A_[:], tri.bitcast(F32R), sh_.bitcast(F32R), start=False, stop=False, skip_group_check=True), ["tri", "sh%d" % i3, "e%d" % i3], [tA])
                P.op("pe", lambda e: e.matmul(A_[:], tri.bitcast(F32R), sl_.bitcast(F32R), start=False, stop=True, skip_group_check=True), ["tri", "sl%d" % i3], [tA])
                if kb != 0:
                    P.op("pe", lambda e: e.matmul(C_[:], ones.bitcast(F32R), sh_.bitcast(F32R), start=True, stop=False), ["ones", "sh%d" % i3], [tC])
                    P.op("pe", lambda e: e.matmul(C_[:], ones.bitcast(F32R), sl_.bitcast(F32R), start=False, stop=True), ["ones", "sl%d" % i3], [tC])
                yield
                if first:
                    P.op("act", lambda e: e.activation(out=w_.bitcast(F32R), in_=A_[:], func=AF.Exp, scale=-1.0), [tA], ["w%d" % i3])
                    if kb != 0:
                        P.op("dve", lambda e: e.tensor_copy(Cb, C_[:]), [tC], ["Cb"])
                    yield
                    yield
                else:
                    P.op("dve", lambda e: e.tensor_tensor(bt_, A_[:], Cb, ALU.add), [tA, "Cb"], ["bt%d" % i3])
                    if kb != 0:
                        P.op("dve", lambda e: e.tensor_tensor(Cb, Cb, C_[:], ALU.add), [tC, "Cb"], ["Cb"])
                    yield
                    P.op("act", lambda e: e.activation(out=w_.bitcast(F32R), in_=bt_, func=AF.Exp, scale=-1.0), ["bt%d" % i3], ["w%d" % i3])
                    yield
                if m >= 0:
                    P.op("pool", lambda e: e.tensor_tensor(w_.bitcast(F32R), w_, msk[:, m, :], ALU.mult), ["w%d" % i3, "msk"], ["w%d" % i3])
                yield
                P.op("pe", lambda e: e.matmul(psO[:], vtok[:, kb, :].bitcast(F32R), w_.bitcast(F32R), start=first, stop=(kb == 0)),
                     ["vtok%d" % kb, "w%d" % i3], [otag])
                yield
                if kb == 0:
                    self.copy("dve", ob, psO[:], [otag], ["ob"])
                    self.out_rows(ob, 1024 + h * 128, g * TB, TB, "ob")

            pend = []
            for g in range(NG):
                for kb in range(4 * g + 3, -1, -1):
                    pend.append((g, kb, it))
                    it += 1
            active = []
            while pend or active:
                while pend and len(active) < ND:
                    active.append(pair(*pend.pop(0)))
                for gen in list(active):
                    try:
                        next(gen)
                    except StopIteration:
                        active.remove(gen)

    def mix_hgrn(self, l):
        P, S = self.P, self.S
        NC = S // 32
        NB = S // 128
        Qh = self.A(0, S)
        Qt = self.A(S, S)
        Kh = self.A(2 * S, S)
        Kt = self.A(3 * S, S)
        vT = self.A(4 * S, S)
        sm = 5 * S
        ktok = [self.A(sm + i * 128, 128) for i in range(2)]
        vtok = [self.A(sm + 256 + i * 128, 128) for i in range(2)]
        scm = [self.A(sm + 512 + i * 128, 128) for i in range(2)]
        St = [self.A(sm + 768 + i * 128, 128) for i in range(2)]
        T0, T1, T2 = self.AF(0, S), self.AF(S, S), self.AF(2 * S, S)
        fo = 3 * S if 3 * S + 1200 <= ARENA_F else None
        assert fo is None or True
        small = self.hg_small
        eref, erefi, elast = small[:, 0:NC], small[:, NC:2 * NC], small[:, 2 * NC:3 * NC]
        m32 = self.hg_m32[:]
        ob = [T0[:, i * 512:(i + 1) * 512] for i in range(3)]
        rst = self.A(sm + 1024, S)
        P.dma(rst.bitcast(F32R), self.rst_d[:, 0:S].bitcast(F32R), writes=["rst"], q="pool")
        P.dma(m32, self.m32_d, writes=["m32"])
        c3 = lambda t: t.rearrange("p (c t) -> p c t", t=32)
        for h in range(4):
            lbc = self.hg_lb[:, 4 * l + h:4 * l + h + 1]
            oml = self.hg_oml[:, 4 * l + h:4 * l + h + 1]
            P.dma(T0, self.projT[512 + h * 128:512 + (h + 1) * 128, :], writes=["T0"])
            P.dma(T1, self.projT[h * 128:(h + 1) * 128, :], writes=["T1"], q="act")
            P.dma(vT.bitcast(F32R), self.projT[1024 + h * 128:1024 + (h + 1) * 128, :].bitcast(F32R), writes=["vT"], q="pool")
            P.op("act", lambda e: e.activation(out=T2, in_=T0, func=AF.Sigmoid), ["T0"], ["T2"])
            P.op("dve", lambda e, oml=oml, lbc=lbc: e.tensor_scalar(T2, T2, oml, lbc, ALU.mult, ALU.add), ["T2", "hgp"], ["T2"])
            P.op("act", lambda e: e.activation(out=T2, in_=T2, func=AF.Ln), ["T2"], ["T2"])
            P.op("act", lambda e: e.activation(out=T0, in_=T0, func=AF.Sigmoid, scale=-1.0), ["T0"], ["T0"])
            P.op("dve", lambda e, oml=oml: e.tensor_scalar(T0, T0, oml, None, ALU.mult), ["T0", "hgp"], ["T0"])
            P.op("dve", lambda e: e.tensor_tensor_scan(T2, rst, T2, 0.0, ALU.mult, ALU.add), ["T2", "rst"], ["T2"])
            P.op("act", lambda e: e.activation(out=T1, in_=T1, func=AF.Silu), ["T1"], ["T1"])
            P.op("act", lambda e: e.activation(out=eref, in_=c3(T2)[:, :, 15], func=AF.Exp, scale=-1.0), ["T2"], ["hgs"])
            P.op("act", lambda e: e.activation(out=erefi, in_=c3(T2)[:, :, 15], func=AF.Exp), ["T2"], ["hgs"])
            P.op("act", lambda e: e.activation(out=elast, in_=c3(T2)[:, :, 31], func=AF.Exp), ["T2"], ["hgs"])
            P.op("act", lambda e: e.activation(out=Qh.bitcast(F32R), in_=T2, func=AF.Exp), ["T2"], ["Qh"])
            P.op("dve", lambda e: e.tensor_tensor(Qh.bitcast(F32R), Qh, T1, ALU.mult), ["Qh", "T1"], ["Qh"])
            P.op("pool", lambda e: e.tensor_tensor(c3(Qt).bitcast(F32R), c3(Qh), eref.unsqueeze(2).to_broadcast([128, NC, 32]), ALU.mult), ["Qh", "hgs"], ["Qt"])
            P.op("act", lambda e: e.activation(out=Kt.bitcast(F32R), in_=T2, func=AF.Exp, scale=-1.0), ["T2"], ["Kt"])
            P.op("dve", lambda e: e.tensor_tensor(Kt.bitcast(F32R), Kt, T0, ALU.mult), ["Kt", "T0"], ["Kt"])
            P.op("pool", lambda e: e.tensor_tensor(c3(Kh).bitcast(F32R), c3(Kt), elast.unsqueeze(2).to_broadcast([128, NC, 32]), ALU.mult), ["Kt", "hgs"], ["Kh"])
            P.op("dve", lambda e: e.tensor_tensor(c3(Kt).bitcast(F32R), c3(Kt), erefi.unsqueeze(2).to_broadcast([128, NC, 32]), ALU.mult), ["Kt", "hgs", "Kh"], ["Kt"])
            P.op("dve", lambda e: e.tensor_scalar(St[0].bitcast(F32R), m32, 0.0, None, ALU.mult), ["m32"], ["St0"])
            for b in range(NB):
                i2 = b % 2
                tsl = slice(b * 128, (b + 1) * 128)
                pk, pv, psc, po_, pkv = self.psb[0 + i2], self.psb[2 + i2], self.psb[4 + i2], self.psb[6], self.psb[7]
                P.op("pe", lambda e, pk=pk, tsl=tsl: e.transpose(pk[:, 0:128], Kh[:, tsl], self.ident[:]), ["Kh", "ident"], ["ps%d" % i2])
                P.op("pe", lambda e, pv=pv, tsl=tsl: e.transpose(pv[:, 0:128], vT[:, tsl], self.ident[:]), ["vT", "ident"], ["ps%d" % (2 + i2)])
                self.copy("act", ktok[i2].bitcast(F32R), pk[:, 0:128], ["ps%d" % i2], ["ktok%d" % i2])
                self.copy("act", vtok[i2].bitcast(F32R), pv[:, 0:128], ["ps%d" % (2 + i2)], ["vtok%d" % i2])
                P.op("pe", lambda e, psc=psc, tsl=tsl: e.matmul(psc[:, 0:128], Kt[:, tsl].bitcast(F32R), Qt[:, tsl].bitcast(F32R), start=True, stop=True), ["Kt", "Qt"], ["ps%d" % (4 + i2)])
                P.op("dve", lambda e, psc=psc, i2=i2: e.tensor_tensor(scm[i2].bitcast(F32R), psc[:, 0:128], m32, ALU.mult), ["ps%d" % (4 + i2), "m32"], ["scm%d" % i2])
                bank = po_[:, i2 * 128:(i2 + 1) * 128]
                otag = "po%d" % i2
                P.op("pe", lambda e, bank=bank, i2=i2: e.matmul(bank, vtok[i2].bitcast(F32R), scm[i2].bitcast(F32R), start=True, stop=False, skip_group_check=True),
                     ["vtok%d" % i2, "scm%d" % i2], [otag])
                for c in range(4):
                    cg = b * 4 + c
                    cur, nxt = St[cg % 2], St[(cg + 1) % 2]
                    csl = slice(b * 128 + c * 32, b * 128 + (c + 1) * 32)
                    P.op("pe", lambda e, bank=bank, c=c, cur=cur, csl=csl: e.matmul(bank[:, c * 32:(c + 1) * 32], cur.bitcast(F32R), Qh[:, csl].bitcast(F32R), start=False, stop=(c == 3), skip_group_check=True),
                         ["St%d" % (cg % 2), "Qh"], [otag])
                    kvp = pkv[:, (cg % 4) * 128:(cg % 4 + 1) * 128]
                    kvt = "pkv%d" % (cg % 4)
                    tp = (32 * c, 0)
                    P.op("pe", lambda e, kvp=kvp, c=c, i2=i2, tp=tp: e.matmul(kvp, ktok[i2][32 * c:32 * c + 32, :].bitcast(F32R), vtok[i2][32 * c:32 * c + 32, :].bitcast(F32R),
                                                                         start=True, stop=True, tile_position=tp, skip_group_check=True),
                         ["ktok%d" % i2, "vtok%d" % i2], [kvt])
                    P.op("dve", lambda e, nxt=nxt, cur=cur, kvp=kvp, cg=cg: e.scalar_tensor_tensor(nxt.bitcast(F32R), cur, elast[:, cg:cg + 1], kvp, ALU.mult, ALU.add),
                         ["St%d" % (cg % 2), kvt, "hgs"], ["St%d" % ((cg + 1) % 2)])
                if b % 4 == 3:
                    pass
                self.copy("act", T2[:, tsl], bank, [otag], ["T2"])
            P.dma(T1, self.projT[1536 + h * 128:1536 + (h + 1) * 128, :], writes=["T1"])
            P.op("act", lambda e: e.activation(out=T1, in_=T1, func=AF.Silu), ["T1"], ["T1"])
            P.op("dve", lambda e, h=h: e.tensor_scalar(T1, T1, self.hg_ng[:, 4 * l + h:4 * l + h + 1], None, ALU.mult), ["T1", "hgp"], ["T1"])
            for g in range(S // TB):
                gs = slice(g * TB, (g + 1) * TB)
                sq = ob[0]
                P.op("pool", lambda e, gs=gs: e.tensor_tensor(sq, T2[:, gs], T2[:, gs], ALU.mult), ["T2", "T0"], ["T0"])
                pr = self.psb[g % 2]
                P.op("pe", lambda e, pr=pr: e.matmul(pr[:], self.ones_sb[:], sq, start=True, stop=True), ["T0", "ones_sb"], ["ps%d" % (g % 2)])
                rt = ob[1]
                P.op("act", lambda e, pr=pr: e.activation(out=rt, in_=pr[:], func=AF.Sqrt, bias=self.rmsT[:], scale=1.0 / 128.0), ["ps%d" % (g % 2), "rmsT", "T0"], ["T0"])
                P.op("dve", lambda e: e.reciprocal(rt, rt), ["T0"], ["T0"])
                P.op("dve", lambda e, gs=gs: e.tensor_tensor(rt, rt, T2[:, gs], ALU.mult), ["T0", "T2"], ["T0"])
                P.op("pool", lambda e, gs=gs: e.tensor_tensor(ob[2], rt, T1[:, gs], ALU.mult), ["T0", "T1"], ["T0"])
                self.out_rows(ob[2], h * 128, g * TB, TB, "T0")


    def mix_s5(self, l):
        P, S = self.P, self.S
        NBLK = S // 128
        PI_ = math.pi
        sp = self.s5p
        def sv(i):
            return sp[:, i, :]
        AR, AI, DT, MAG, TH, CS, SN, LR, LI, DEN, ZR, ZI, K, TMP, TMP2, GPR, GPI, HLR, HLI, PIH = range(20)
        P.dma(sv(AR), self.s5_arT[l], writes=["s5p"])
        P.dma(sv(AI), self.s5_aiT[l], writes=["s5p"])
        P.dma(sv(DT), self.s5_ldtT[l], writes=["s5p"])
        o = lambda eng, fn: P.op(eng, fn, ["s5p"], ["s5p"])
        o("act", lambda e: e.activation(out=sv(DT), in_=sv(DT), func=AF.Exp))
        o("dve", lambda e: e.tensor_tensor(sv(MAG), sv(AR), sv(DT), ALU.mult))
        o("act", lambda e: e.activation(out=sv(MAG), in_=sv(MAG), func=AF.Exp))
        o("dve", lambda e: e.tensor_tensor(sv(TH), sv(AI), sv(DT), ALU.mult))
        o("dve", lambda e: e.memset(sv(K), 0.0))
        for mth in range(1, 8):
            o("dve", lambda e, mth=mth: e.tensor_single_scalar(sv(TMP), sv(TH), (2 * mth - 1) * PI_, ALU.is_gt))
            o("dve", lambda e: e.tensor_tensor(sv(K), sv(K), sv(TMP), ALU.add))
        C1 = 6.28125
        C2 = 2 * PI_ - C1
        o("dve", lambda e: e.scalar_tensor_tensor(sv(TH), sv(K), -C1, sv(TH), ALU.mult, ALU.add))
        o("dve", lambda e: e.scalar_tensor_tensor(sv(TH), sv(K), -C2, sv(TH), ALU.mult, ALU.add))
        o("dve", lambda e: e.memset(sv(PIH), PI_ / 2))
        o("dve", lambda e: e.tensor_scalar_min(sv(TH), sv(TH), PI_))
        o("dve", lambda e: e.tensor_scalar_max(sv(TH), sv(TH), -PI_))
        o("act", lambda e: e.activation(out=sv(SN), in_=sv(TH), func=AF.Sin))
        o("dve", lambda e: e.tensor_scalar(sv(TMP), sv(TH), -1.0, None, ALU.mult))
        o("dve", lambda e: e.tensor_tensor(sv(TMP), sv(TMP), sv(TH), ALU.max))
        o("dve", lambda e: e.tensor_scalar(sv(TMP), sv(TMP), -1.0, PI_ / 2, ALU.mult, ALU.add))
        o("act", lambda e: e.activation(out=sv(CS), in_=sv(TMP), func=AF.Sin))
        o("dve", lambda e: e.tensor_tensor(sv(LR), sv(MAG), sv(CS), ALU.mult))
        o("dve", lambda e: e.tensor_tensor(sv(LI), sv(MAG), sv(SN), ALU.mult))
        o("dve", lambda e: e.tensor_tensor(sv(DEN), sv(AR), sv(AR), ALU.mult))
        o("dve", lambda e: e.tensor_tensor(sv(TMP), sv(AI), sv(AI), ALU.mult))
        o("dve", lambda e: e.tensor_tensor(sv(DEN), sv(DEN), sv(TMP), ALU.add))
        o("dve", lambda e: e.reciprocal(sv(DEN), sv(DEN)))
        o("dve", lambda e: e.tensor_scalar_add(sv(TMP), sv(LR), -1.0))
        o("dve", lambda e: e.tensor_tensor(sv(ZR), sv(TMP), sv(AR), ALU.mult))
        o("dve", lambda e: e.tensor_tensor(sv(TMP2), sv(LI), sv(AI), ALU.mult))
        o("dve", lambda e: e.tensor_tensor(sv(ZR), sv(ZR), sv(TMP2), ALU.add))
        o("dve", lambda e: e.tensor_tensor(sv(ZR), sv(ZR), sv(DEN), ALU.mult))
        o("dve", lambda e: e.tensor_tensor(sv(ZI), sv(TMP), sv(AI), ALU.mult))
        o("dve", lambda e: e.tensor_tensor(sv(TMP2), sv(LI), sv(AR), ALU.mult))
        o("dve", lambda e: e.tensor_tensor(sv(ZI), sv(TMP2), sv(ZI), ALU.subtract))
        o("dve", lambda e: e.tensor_tensor(sv(ZI), sv(ZI), sv(DEN), ALU.mult))
        Er = self.AF(0, 2048).rearrange("p (a j) -> p a j", j=128)
        Ei = self.AF(2048, 2048).rearrange("p (a j) -> p a j", j=128)
        D0 = self.AF(4096, 2048).rearrange("p (a j) -> p a j", j=128)
        t1 = self.AF(6144, 1024)
        t2 = self.AF(7168, 1024)
        xr = self.AF(8192, 1024)
        xi = self.AF(9216, 1024)
        pw = self.AF(10240, 64).rearrange("p (c a) -> p c a", a=16)
        bc = lambda v, n: v.unsqueeze(2).to_broadcast([128, 16, n])
        ot = lambda eng, fn: P.op(eng, fn, ["s5p", "s5t"], ["s5t", "t10", "t11"])
        ot("dve", lambda e: e.memset(Er[:, :, 0:1], 1.0))
        ot("dve", lambda e: e.memset(Ei[:, :, 0:1], 0.0))
        ot("dve", lambda e: e.tensor_copy(pw[:, 0, :], sv(CS)))
        ot("dve", lambda e: e.tensor_scalar(pw[:, 1, :], sv(SN), -1.0, None, ALU.mult))
        n = 1
        while n < 128:
            ot("dve", lambda e, n=n: e.tensor_tensor(Er[:, :, n:2 * n], Er[:, :, 0:n], bc(pw[:, 0, :], n), ALU.mult))
            ot("dve", lambda e, n=n: e.tensor_tensor(t1.rearrange("p (a j) -> p a j", a=16)[:, :, 0:n], Ei[:, :, 0:n], bc(pw[:, 1, :], n), ALU.mult))
            ot("dve", lambda e, n=n: e.tensor_tensor(Er[:, :, n:2 * n], Er[:, :, n:2 * n], t1.rearrange("p (a j) -> p a j", a=16)[:, :, 0:n], ALU.subtract))
            ot("dve", lambda e, n=n: e.tensor_tensor(Ei[:, :, n:2 * n], Er[:, :, 0:n], bc(pw[:, 1, :], n), ALU.mult))
            ot("dve", lambda e, n=n: e.tensor_tensor(t1.rearrange("p (a j) -> p a j", a=16)[:, :, 0:n], Ei[:, :, 0:n], bc(pw[:, 0, :], n), ALU.mult))
            ot("dve", lambda e, n=n: e.tensor_tensor(Ei[:, :, n:2 * n], Ei[:, :, n:2 * n], t1.rearrange("p (a j) -> p a j", a=16)[:, :, 0:n], ALU.add))
            ot("dve", lambda e: e.tensor_tensor(pw[:, 2, :], pw[:, 0, :], pw[:, 0, :], ALU.mult))
            ot("dve", lambda e: e.tensor_tensor(pw[:, 3, :], pw[:, 1, :], pw[:, 1, :], ALU.mult))
            ot("dve", lambda e: e.tensor_tensor(pw[:, 1, :], pw[:, 0, :], pw[:, 1, :], ALU.mult))
            ot("dve", lambda e: e.tensor_scalar(pw[:, 1, :], pw[:, 1, :], 2.0, None, ALU.mult))
            ot("dve", lambda e: e.tensor_tensor(pw[:, 0, :], pw[:, 2, :], pw[:, 3, :], ALU.subtract))
            n *= 2
        ot("dve", lambda e: e.tensor_copy(D0, bc(sv(MAG), 128)))
        ot("dve", lambda e: e.memset(D0[:, :, 0:1], 0.0))
        uT = self.A(0, 4 * S).rearrange("p (q t) -> p q t", t=S)
        wb = self.A(4 * S + 15360, 4096).rearrange("p (c a n) -> p c a n", c=2, a=16)
        wc = self.A(4 * S + 1024, 4096).rearrange("p (c a n) -> p c a n", c=2, a=16)
        wg = self.A(4 * S + 5120, 2048).rearrange("p (n a k) -> p n a k", n=4, a=4)
        hrb = self.A(4 * S + 7168, 1024).rearrange("p (a j) -> p a j", j=128)
        hib = self.A(4 * S + 8192, 1024).rearrange("p (a j) -> p a j", j=128)
        yg = self.A(4 * S + 9216, 2048).rearrange("p (q t) -> p q t", t=512)
        assert 4 * S + 15360 + 4096 <= ARENA_R
        wcf = self.AF(10304, 2048).rearrange("p (a n) -> p a n", a=16)
        wcf2 = self.AF(12352 - 0, 0) if False else None
        P.dma(uT.bitcast(F32R), self.projT[2048:2560, :].rearrange("(q p) t -> p q t", p=128).bitcast(F32R), writes=["uT"], q="pool")
        P.dma(wb.bitcast(F32R), self.s5_wb[l].bitcast(F32R), writes=["wb"], q="pool")
        P.dma(wg.bitcast(F32R), self.s5_glu[l].rearrange("n p (a k) -> p n a k", a=4).bitcast(F32R), writes=["wg"], q="pool")
        cre = self.A(4 * S + 11264, 2048).rearrange("p (a n) -> p a n", a=16)
        cim = self.A(4 * S + 13312, 2048).rearrange("p (a n) -> p a n", a=16)
        P.dma(cre.bitcast(F32R), self.s5_wcre[l].bitcast(F32R), writes=["cre"], q="pool")
        P.dma(cim.bitcast(F32R), self.s5_wcim[l].bitcast(F32R), writes=["cim"], q="pool")
        ow = lambda eng, fn, w: P.op(eng, fn, ["s5p", "cre", "cim", "wcf", "wc"], [w])
        ow("dve", lambda e: e.tensor_tensor(wcf, cim, bc(sv(ZI), 128), ALU.mult), "wcf")
        ow("dve", lambda e: e.tensor_tensor(wc[:, 0].bitcast(F32R), cre, bc(sv(ZR), 128), ALU.mult), "wc")
        ow("dve", lambda e: e.tensor_tensor(wc[:, 0].bitcast(F32R), wc[:, 0], wcf, ALU.subtract), "wc")
        ow("dve", lambda e: e.tensor_tensor(wcf, cim, bc(sv(ZR), 128), ALU.mult), "wcf")
        ow("dve", lambda e: e.tensor_tensor(wc[:, 1].bitcast(F32R), cre, bc(sv(ZI), 128), ALU.mult), "wc")
        ow("dve", lambda e: e.tensor_tensor(wc[:, 1].bitcast(F32R), wc[:, 1], wcf, ALU.add), "wc")
        ow("dve", lambda e: e.tensor_scalar(wc[:, 1].bitcast(F32R), wc[:, 1], -1.0, None, ALU.mult), "wc")
        o("dve", lambda e: e.memset(sv(GPR), 0.0))
        o("dve", lambda e: e.memset(sv(GPI), 0.0))
        PR, PIm, PY, PG = self.psb[0:2], self.psb[2:4], self.psb[4], self.psb[5:7]
        import os as _os
        stop = int(_os.environ.get("S5_STOP", 99))
        for b in range(int(_os.environ.get("S5_NBLK", NBLK))):
            tsl = slice(b * 128, (b + 1) * 128)
            for hf in range(2):
                asl = slice(hf * 8, hf * 8 + 8)
                for a8 in range(8):
                    a = hf * 8 + a8
                    for c, PP, tg in ((0, PR, "pr"), (1, PIm, "pi")):
                        dst = PP[a8 // 4][:, (a8 % 4) * 128:(a8 % 4 + 1) * 128]
                        P.op("pe", lambda e, dst=dst, c=c, a=a, tsl=tsl: e.matmul(dst, wb[:, c, a, :].bitcast(F32R), uT[:, a // 4, tsl].bitcast(F32R),
                                                                          start=True, stop=True, skip_group_check=True),
                             ["wb", "uT"], [tg + str(a8 // 4)])
                for k2 in range(2 if stop >= 2 else 0):
                    sl = slice(k2 * 512, (k2 + 1) * 512)
                    er = Er[:, hf * 8 + k2 * 4: hf * 8 + k2 * 4 + 4, :]
                    ei = Ei[:, hf * 8 + k2 * 4: hf * 8 + k2 * 4 + 4, :]
                    v4 = lambda t: t.rearrange("p (a j) -> p a j", j=128)
                    prr, pii = v4(PR[k2][:]), v4(PIm[k2][:])
                    P.op("dve", lambda e, sl=sl, er=er, prr=prr: e.tensor_tensor(v4(xr[:, sl]), prr, er, ALU.mult), ["pr%d" % k2, "s5t"], ["xr%d" % k2])
                    P.op("act", lambda e, sl=sl, ei=ei, pii=pii: e.copy(v4(t1[:, sl]), pii), ["pi%d" % k2], ["t1%d" % k2])
                    P.op("pool", lambda e, sl=sl, ei=ei: e.tensor_tensor(v4(t2[:, sl]), v4(t1[:, sl]), ei, ALU.mult), ["t1%d" % k2, "s5t"], ["t2%d" % k2])
                    P.op("dve", lambda e, sl=sl: e.tensor_tensor(xr[:, sl], xr[:, sl], t2[:, sl], ALU.subtract), ["xr%d" % k2, "t2%d" % k2], ["xr%d" % k2])
                    P.op("dve", lambda e, sl=sl, ei=ei, prr=prr: e.tensor_tensor(v4(xi[:, sl]), prr, ei, ALU.mult), ["pr%d" % k2, "s5t"], ["xi%d" % k2])
                    P.op("pool", lambda e, sl=sl, er=er: e.tensor_tensor(v4(t2[:, sl]), v4(t1[:, sl]), er, ALU.mult), ["t1%d" % k2, "s5t", "xr%d" % k2], ["t2%d" % k2])
                    P.op("dve", lambda e, sl=sl: e.tensor_tensor(xi[:, sl], xi[:, sl], t2[:, sl], ALU.add), ["xi%d" % k2, "t2%d" % k2], ["xi%d" % k2])
                if stop < 3:
                    continue
                x3r = xr.rearrange("p (a j) -> p a j", j=128)
                x3i = xi.rearrange("p (a j) -> p a j", j=128)
                tagx = ["xr0", "xr1", "xi0", "xi1"]
                P.op("dve", lambda e, asl=asl: e.tensor_tensor(sv(TMP)[:, asl], sv(MAG)[:, asl], sv(GPR)[:, asl], ALU.mult), ["s5p"], ["s5p"])
                P.op("dve", lambda e, asl=asl: e.tensor_tensor(x3r[:, :, 0], x3r[:, :, 0], sv(TMP)[:, asl], ALU.add), ["s5p"] + tagx, ["xr0", "xr1"])
                P.op("dve", lambda e, asl=asl: e.tensor_tensor(sv(TMP)[:, asl], sv(MAG)[:, asl], sv(GPI)[:, asl], ALU.mult), ["s5p"], ["s5p"])
                P.op("dve", lambda e, asl=asl: e.tensor_tensor(x3i[:, :, 0], x3i[:, :, 0], sv(TMP)[:, asl], ALU.add), ["s5p"] + tagx, ["xi0", "xi1"])
                d0 = D0[:, asl, :].rearrange("p a j -> p (a j)")
                P.op("dve", lambda e, d0=d0: e.tensor_tensor_scan(xr, d0, xr, 0.0, ALU.mult, ALU.add), ["xr0", "xr1", "s5t"], ["xr0", "xr1"])
                P.op("dve", lambda e, d0=d0: e.tensor_tensor_scan(xi, d0, xi, 0.0, ALU.mult, ALU.add), ["xi0", "xi1", "s5t"], ["xi0", "xi1"])
                if stop < 4:
                    continue
                erh, eih = Er[:, asl, :], Ei[:, asl, :]
                P.op("pool", lambda e, erh=erh: e.tensor_tensor(t1.rearrange("p (a j) -> p a j", j=128), x3r, erh, ALU.mult), ["xr0", "xr1", "s5t", "t10", "t11"], ["t10", "t11"])
                P.op("pool", lambda e, eih=eih: e.tensor_tensor(t2.rearrange("p (a j) -> p a j", j=128), x3i, eih, ALU.mult), ["xi0", "xi1", "s5t", "t20", "t21"], ["t20", "t21"])
                P.op("dve", lambda e: e.tensor_tensor(hrb.bitcast(F32R), t1.rearrange("p (a j) -> p a j", j=128), t2.rearrange("p (a j) -> p a j", j=128), ALU.add),
                     ["t10", "t11", "t20", "t21"], ["hr"])
                P.op("pool", lambda e, erh=erh: e.tensor_tensor(t1.rearrange("p (a j) -> p a j", j=128), x3i, erh, ALU.mult), ["xi0", "xi1", "s5t", "t10", "t11", "hr"], ["t10", "t11"])
                P.op("pool", lambda e, eih=eih: e.tensor_tensor(t2.rearrange("p (a j) -> p a j", j=128), x3r, eih, ALU.mult), ["xr0", "xr1", "s5t", "t20", "t21", "hr"], ["t20", "t21"])
                P.op("dve", lambda e: e.tensor_tensor(hib.bitcast(F32R), t1.rearrange("p (a j) -> p a j", j=128), t2.rearrange("p (a j) -> p a j", j=128), ALU.subtract),
                     ["t10", "t11", "t20", "t21"], ["hi"])
                if stop < 5:
                    continue
                P.op("dve", lambda e, asl=asl: e.tensor_copy(sv(HLR)[:, asl], hrb[:, :, 127]), ["hr", "s5p"], ["s5p"])
                P.op("dve", lambda e, asl=asl: e.tensor_copy(sv(HLI)[:, asl], hib[:, :, 127]), ["hi", "s5p"], ["s5p"])
                for (dst, x1, c1, x2, c2, op_) in ((GPR, HLR, CS, HLI, SN, ALU.subtract), (GPI, HLR, SN, HLI, CS, ALU.add)):
                    P.op("dve", lambda e, asl=asl, x1=x1, c1=c1: e.tensor_tensor(sv(TMP)[:, asl], sv(x1)[:, asl], sv(c1)[:, asl], ALU.mult), ["s5p"], ["s5p"])
                    P.op("dve", lambda e, asl=asl, x2=x2, c2=c2: e.tensor_tensor(sv(TMP2)[:, asl], sv(x2)[:, asl], sv(c2)[:, asl], ALU.mult), ["s5p"], ["s5p"])
                    P.op("dve", lambda e, asl=asl, dst=dst, op_=op_: e.tensor_tensor(sv(dst)[:, asl], sv(TMP)[:, asl], sv(TMP2)[:, asl], op_), ["s5p"], ["s5p"])
                if stop < 6:
                    continue
                for ql in range(2):
                    q = 2 * hf + ql
                    dst = PY[:, q * 128:(q + 1) * 128]
                    n_ = 0
                    for a4 in range(4):
                        a8 = ql * 4 + a4
                        a = hf * 8 + a8
                        for c, hb, tg in ((0, hrb, "hr"), (1, hib, "hi")):
                            P.op("pe", lambda e, dst=dst, c=c, a=a, a8=a8, hb=hb, n_=n_: e.matmul(dst, wc[:, c, a, :].bitcast(F32R), hb[:, a8, :].bitcast(F32R),
                                                                                          start=(n_ == 0), stop=(n_ == 7), skip_group_check=True),
                                 ["wc", tg], ["py"])
                            n_ += 1
            if stop < 7:
                continue
            bs = (b % 4) * 128
            for q in range(4):
                P.op("dve", lambda e, q=q, tsl=tsl: e.scalar_tensor_tensor(t1[:, q * 128:(q + 1) * 128], uT[:, q, tsl], self.s5_dT[:, 4 * l + q:4 * l + q + 1], PY[:, q * 128:(q + 1) * 128], ALU.mult, ALU.add),
                     ["uT", "py", "s5d", "t10", "t11"], ["t10"])
                P.op("act", lambda e, q=q, bs=bs: e.activation(out=yg[:, q, bs:bs + 128].bitcast(F32R), in_=t1[:, q * 128:(q + 1) * 128], func=AF.Gelu), ["t10"], ["yg%d" % q])
            if b % 4 == 3:
                t0 = (b // 4) * TB
                for n4 in range(4):
                    pg = PG[n4 % 2]
                    for q in range(4):
                        P.op("pe", lambda e, pg=pg, n4=n4, q=q: e.matmul(pg[:], wg[:, n4, q, :].bitcast(F32R), yg[:, q, :].bitcast(F32R), start=(q == 0), stop=(q == 3)),
                             ["wg"] + ["yg%d" % qq for qq in range(4)], ["pg%d" % (n4 % 2)])
                    sg = t2[:, (n4 % 2) * 512:(n4 % 2 + 1) * 512]
                    P.op("act", lambda e, pg=pg, sg=sg, n4=n4: e.activation(out=sg, in_=pg[:], func=AF.Sigmoid, bias=self.s5_gbT[:, 4 * l + n4:4 * l + n4 + 1], scale=1.0),
                         ["pg%d" % (n4 % 2), "s5d", "t20", "t21"], ["t2%d" % (n4 % 2)])
                    P.op("dve", lambda e, sg=sg, n4=n4: e.tensor_tensor(sg, sg, yg[:, n4, :], ALU.mult), ["t2%d" % (n4 % 2), "yg%d" % n4], ["t2%d" % (n4 % 2)])
                    self.out_rows(sg, 512 + n4 * 128, t0, TB, "t2%d" % (n4 % 2), q=("sp" if n4 % 2 else "act"))


    def mix_rwkv(self, l):
        P, S = self.P, self.S
        HS = min(S, 512)
        NH = S // HS
        NSB = HS // 128
        NCH = HS // 64
        E05 = math.exp(-0.5)
        RA, FA = self.A, self.AF
        KT, BT, KK, RT, VT, GT, BV = [RA(i * HS, HS) for i in range(7)]
        ro = 7 * HS
        W2p, A2p, G2w = RA(ro, 512), RA(ro + 512, 512), RA(ro + 1024, 512)
        XWA = RA(ro + 1536, HS)
        XG = RA(ro + 1536 + HS, HS)
        mo = ro + 1536 + 2 * HS
        BTm2 = [[RA(mo + bi * 768 + h * 128, 128) for h in range(2)] for bi in range(2)]
        KKm2 = [[RA(mo + bi * 768 + 256 + h * 128, 128) for h in range(2)] for bi in range(2)]
        KTm2 = [[RA(mo + bi * 768 + 512 + h * 128, 128) for h in range(2)] for bi in range(2)]
        mo = mo + 512
        blk = RA(mo + 1024, 128)
        rst = RA(mo + 1152, HS)
        tqr = RA(mo + 1152 + HS, 512)
        assert mo + 1152 + HS + 512 <= ARENA_R
        T = [FA(i * HS, HS) for i in range(4)]
        fo = 4 * HS
        def ft(n):
            nonlocal fo
            r = FA(fo, n)
            fo += n
            return r
        Am, ATm = ft(256), ft(256)
        AkTm2 = [ft(256) for _ in range(2)]
        BrbTm2 = [ft(256) for _ in range(2)]
        BrkTm2 = [ft(256) for _ in range(2)]
        Pk = [ft(256) for _ in range(2)]
        PTk = [ft(256) for _ in range(2)]
        MTk = [ft(256) for _ in range(2)]
        MTfin = [ft(256) for _ in range(2)]
        tok2 = [[ft(128) for _ in range(3)] for _ in range(2)]
        Vc2 = [[ft(128) for _ in range(2)] for _ in range(2)]
        Vpad2 = [[ft(128) for _ in range(2)] for _ in range(2)]
        Upad = [[ft(128) for _ in range(2)] for _ in range(2)]
        Wsb = ft(128)
        WkV = ft(128)
        Pbd = ft(128)
        Ptmp = ft(128)
        blkm = ft(128)
        gam = ft(NCH)
        msk = ft(3 * 256).rearrange("p (a b) -> p a b", b=256)
        idn2 = ft(256)
        OTs = ft(512)
        tq = [ft(512) for _ in range(3)]
        assert fo <= ARENA_F, fo
        rp = self.rwp
        MU, W0, A0, KKs, KA, OMKA, RK, GNG, GNB = 0, 14, 18, 22, 26, 30, 34, 38, 42
        col = lambda base, i: rp[:, base + i:base + i + 1]
        hm = self.rw_hm
        P.dma(rp[:, 0:14], self.rw_muT[l], writes=["rwp"])
        for base, src in ((W0, self.rw_w0T), (A0, self.rw_a0T), (KKs, self.rw_kkT), (KA, self.rw_kaT), (RK, self.rw_rkT), (GNG, self.rw_gngT), (GNB, self.rw_gnbT)):
            P.dma(rp[:, base:base + 4], src[l], writes=["rwp"])
        P.op("dve", lambda e: e.tensor_scalar(rp[:, OMKA:OMKA + 4], rp[:, KA:KA + 4], -1.0, 1.0, ALU.mult, ALU.add), ["rwp"], ["rwp"])
        P.dma(W2p.bitcast(F32R), self.rw_w2p[l].bitcast(F32R), writes=["rww"], q="pool")
        P.dma(A2p.bitcast(F32R), self.rw_a2p[l].bitcast(F32R), writes=["rww"], q="pool")
        P.dma(G2w.bitcast(F32R), self.rw_g2[l].bitcast(F32R), writes=["rww"], q="pool")
        P.dma(blk.bitcast(F32R), self.blk64_d.bitcast(F32R), writes=["blk"], q="pool")
        P.dma(rst.bitcast(F32R), self.rst64_d[:, 0:HS].bitcast(F32R), writes=["rst"], q="pool")
        P.dma(msk, self.rwmask_d.rearrange("a p t -> p a t"), writes=["msk"])
        P.dma(idn2[:, 0:128], self.ident_d, writes=["idn2"])
        P.dma(idn2[:, 128:256], self.ident_d, writes=["idn2"])
        P.dma(blkm, self.blk64_d, writes=["blkm"])
        for t_ in Vc2[0] + Vc2[1] + Vpad2[0] + Vpad2[1] + Upad[0] + Upad[1] + [Wsb]:
            P.op("dve", lambda e, t_=t_: e.memset(t_, 0.0), [], ["small"])

        def shifted(dst, dstp, tagX, tagP, row0, t0):
            P.dma(dst.bitcast(F32R), self.projT[row0:row0 + 128, t0:t0 + HS].bitcast(F32R), writes=[tagX], q="pool")
            if t0 == 0:
                P.op("dve", lambda e: e.memset(dstp[:, 0:1], 0.0), [], [tagP])
                P.dma(dstp[:, 1:HS], self.projT[row0:row0 + 128, 0:HS - 1], writes=[tagP])
            else:
                P.dma(dstp, self.projT[row0:row0 + 128, t0 - 1:t0 + HS - 1], writes=[tagP])

        def tshift(X, Xp, mucol, tagX, tagP):
            P.op("pool", lambda e: e.tensor_tensor(Xp, Xp, X, ALU.subtract), [tagX, tagP], [tagP])
            P.op("dve", lambda e: e.scalar_tensor_tensor(X.bitcast(F32R), Xp, mucol, X, ALU.mult, ALU.add), [tagX, tagP, "rwp"], [tagX])

        RW0 = 4096
        import os as _os
        rstop = int(_os.environ.get("RW_STOP", 99))
        for hp in range(4 if rstop > 0 else 0):
            P.op("dve", lambda e: e.memset(Pbd, 0.0), [], ["Pbd"])
            for half in range(NH):
                t0 = half * HS
                shifted(XWA, T[0], "XWA", "T0", RW0 + 1536, t0)
                tshift(XWA, T[0], col(MU, 12), "XWA", "T0")
                P.op("act", lambda e: e.activation(out=XWA[0:64].bitcast(F32R), in_=XWA[0:64], func=AF.Tanh), ["XWA"], ["XWA"])
                shifted(XG, T[1], "XG", "T1", RW0 + 1664, t0)
                tshift(XG, T[1], col(MU, 13), "XG", "T1")
                P.op("act", lambda e: e.activation(out=XG.bitcast(F32R), in_=XG, func=AF.Sigmoid), ["XG"], ["XG"])
                csl = slice(hp * 128, (hp + 1) * 128)
                for g4 in range(HS // 512):
                    gs = slice(g4 * 512, (g4 + 1) * 512)
                    pw_, pa_, pg_ = self.psb[0], self.psb[1], self.psb[2]
                    P.op("pe", lambda e, gs=gs, csl=csl, pw_=pw_: e.matmul(pw_[:], W2p[:, csl].bitcast(F32R), XWA[:, gs].bitcast(F32R), start=True, stop=True), ["rww", "XWA"], ["ps0"])
                    P.op("pe", lambda e, gs=gs, csl=csl, pa_=pa_: e.matmul(pa_[:], A2p[:, csl].bitcast(F32R), XWA[:, gs].bitcast(F32R), start=True, stop=True), ["rww", "XWA"], ["ps1"])
                    P.op("pe", lambda e, gs=gs, csl=csl, pg_=pg_: e.matmul(pg_[:], G2w[:, csl].bitcast(F32R), XG[:, gs].bitcast(F32R), start=True, stop=True), ["rww", "XG"], ["ps2"])
                    P.op("act", lambda e, gs=gs, pw_=pw_, hp=hp: e.activation(out=T[2][:, gs], in_=pw_[:], func=AF.Sigmoid, bias=col(W0, hp), scale=1.0), ["ps0", "rwp"], ["T2"])
                    P.op("act", lambda e, gs=gs, pa_=pa_, hp=hp: e.activation(out=T[3][:, gs], in_=pa_[:], func=AF.Sigmoid, bias=col(A0, hp), scale=1.0), ["ps1", "rwp"], ["T3"])
                    P.op("act", lambda e, gs=gs, pg_=pg_: e.copy(GT[:, gs].bitcast(F32R), pg_[:]), ["ps2"], ["GT"])
                P.op("dve", lambda e: e.tensor_scalar(T[2], T[2], -E05, None, ALU.mult), ["T2"], ["T2"])
                shifted(KK, T[1], "KK", "T1", RW0 + 512 + hp * 128, t0)
                tshift(KK, T[1], col(MU, 4 + hp), "KK", "T1")
                P.op("dve", lambda e, hp=hp: e.tensor_scalar(T[0], KK, col(KKs, hp), None, ALU.mult), ["KK", "rwp"], ["T0"])
                for g4 in range(HS // 512):
                    gs = slice(g4 * 512, (g4 + 1) * 512)
                    pq = self.psb[g4 % 2]
                    P.op("pool", lambda e, gs=gs: e.tensor_tensor(tqr.bitcast(F32R), T[0][:, gs], T[0][:, gs], ALU.mult), ["T0"], ["tqr"])
                    P.op("pe", lambda e, pq=pq: e.matmul(pq[:], blk.bitcast(F32R), tqr.bitcast(F32R), start=True, stop=True), ["blk", "tqr"], ["ps%d" % (g4 % 2)])
                    P.op("dve", lambda e, pq=pq: e.tensor_scalar_max(tq[1], pq[:], 1e-24), ["ps%d" % (g4 % 2)], ["tq1"])
                    P.op("act", lambda e: e.activation(out=tq[1], in_=tq[1], func=AF.Sqrt), ["tq1"], ["tq1"])
                    P.op("dve", lambda e: e.reciprocal(tq[1], tq[1]), ["tq1"], ["tq1"])
                    P.op("dve", lambda e, gs=gs: e.tensor_tensor(T[0][:, gs], T[0][:, gs], tq[1], ALU.mult), ["T0", "tq1"], ["T0"])
                P.op("dve", lambda e: e.tensor_tensor_scan(T[1], rst, T[2], 0.0, ALU.mult, ALU.add), ["T2", "rst"], ["T1"])
                c3 = lambda t_: t_.rearrange("p (c t) -> p c t", t=64)
                P.op("act", lambda e: e.activation(out=gam, in_=c3(T[1])[:, :, 63], func=AF.Exp), ["T1"], ["gam"])
                P.op("pool", lambda e: e.tensor_tensor(T[2], T[1], T[2], ALU.subtract), ["T1", "T2"], ["T2"])
                P.op("act", lambda e: e.activation(out=T[2], in_=T[2], func=AF.Exp), ["T2"], ["T2"])
                P.op("dve", lambda e: e.tensor_tensor(KT.bitcast(F32R), T[0], T[2], ALU.mult), ["T0", "T2"], ["KT"])
                P.op("act", lambda e: e.activation(out=T[2], in_=T[1], func=AF.Exp, scale=-1.0), ["T1"], ["T2"])
                P.op("pool", lambda e: e.tensor_tensor(T[0], T[0], T[3], ALU.mult), ["T0", "T3"], ["T0"])
                P.op("dve", lambda e: e.scalar_tensor_tensor(BT.bitcast(F32R), T[0], -1.0, T[2], ALU.mult, ALU.mult), ["T0", "T2"], ["BT"])
                P.op("dve", lambda e, hp=hp: e.tensor_scalar(T[3], T[3], col(KA, hp), col(OMKA, hp), ALU.mult, ALU.add), ["T3", "rwp"], ["T3"])
                P.op("pool", lambda e: e.tensor_tensor(T[0], KK, T[3], ALU.mult), ["KK", "T3"], ["T0"])
                P.op("dve", lambda e: e.tensor_tensor(KK.bitcast(F32R), T[0], T[2], ALU.mult), ["T0", "T2"], ["KK"])
                shifted(RT, T[3], "RT", "T3", RW0 + hp * 128, t0)
                tshift(RT, T[3], col(MU, hp), "RT", "T3")
                P.op("pool", lambda e: e.tensor_tensor(T[0], T[0], RT, ALU.mult), ["T0", "RT"], ["T0"])
                P.op("dve", lambda e, hp=hp: e.tensor_scalar(T[0], T[0], col(RK, hp), None, ALU.mult), ["T0", "rwp"], ["T0"])
                P.op("act", lambda e: e.activation(out=T[2], in_=T[1], func=AF.Exp), ["T1"], ["T2"])
                P.op("dve", lambda e: e.tensor_tensor(RT.bitcast(F32R), RT, T[2], ALU.mult), ["RT", "T2"], ["RT"])
                shifted(VT, T[3], "VT", "T3", RW0 + 1024 + hp * 128, t0)
                tshift(VT, T[3], col(MU, 8 + hp), "VT", "T3")
                for g4 in range(HS // 512):
                    gs = slice(g4 * 512, (g4 + 1) * 512)
                    pq = self.psb[g4 % 2]
                    P.op("act", lambda e, gs=gs: e.copy(tqr.bitcast(F32R), T[0][:, gs]), ["T0"], ["tqr"])
                    P.op("pe", lambda e, pq=pq: e.matmul(pq[:], blk.bitcast(F32R), tqr.bitcast(F32R), start=True, stop=True), ["blk", "tqr"], ["ps%d" % (g4 % 2)])
                    P.op("dve", lambda e, pq=pq, gs=gs: e.tensor_tensor(BV[:, gs].bitcast(F32R), pq[:], VT[:, gs], ALU.mult), ["ps%d" % (g4 % 2), "VT"], ["BV"])
                def front(sb, bi, hp=hp):
                    ts_ = slice(sb * 128, (sb + 1) * 128)
                    pA, pB, pC, ptr = self.psb[0], self.psb[1], self.psb[2], self.psb[4]
                    BTm, KKm, KTm = BTm2[bi], KKm2[bi], KTm2[bi]
                    tok, Vc, Vpad = tok2[bi], Vc2[bi], Vpad2[bi]
                    AkTm, BrbTm, BrkTm = AkTm2[bi], BrbTm2[bi], BrkTm2[bi]
                    B = str(bi)
                    for h in range(2):
                        for dst_, src_, tg in ((BTm, BT, "BT"), (KKm, KK, "KK"), (KTm, KT, "KT")):
                            P.op("pool" if h else "dve", lambda e, dst_=dst_, src_=src_, h=h, ts_=ts_: e.tensor_scalar(dst_[h].bitcast(F32R), src_[:, ts_], hm[:, h:h + 1], None, ALU.mult),
                                 [tg, "hm"], ["m%s%d%s" % (tg, h, B)])
                    for i_, (src_, tg) in enumerate(((BT, "BT"), (KK, "KK"), (VT, "VT"))):
                        P.op("pe", lambda e, i_=i_, src_=src_, ts_=ts_, ptr=ptr: e.transpose(ptr[:, i_ * 128:(i_ + 1) * 128], src_[:, ts_], self.ident[:]), [tg, "ident"], ["ps4"])
                    yield
                    for i_ in range(3):
                        self.copy("act", tok[i_], ptr[:, i_ * 128:(i_ + 1) * 128], ["ps4"], ["tok%d%s" % (i_, B)])
                    yield
                    for c in range(2):
                        P.op("pool", lambda e, c=c, Vc=Vc, tok=tok: e.tensor_copy(Vc[c][64 * c:64 * c + 64, :], tok[2][64 * c:64 * c + 64, :]), ["tok2" + B, "small"], ["Vc%d%s" % (c, B)])
                        P.op("pool", lambda e, c=c, Vpad=Vpad, tok=tok: e.tensor_copy(Vpad[c][:, 64 * c:64 * c + 64], tok[2][:, 64 * c:64 * c + 64]), ["tok2" + B, "small"], ["Vpad%d%s" % (c, B)])
                    for h in range(2):
                        hs = slice(h * 128, (h + 1) * 128)
                        P.op("pe", lambda e, h=h, hs=hs, ts_=ts_, BTm=BTm: e.matmul(pA[:, hs], BTm[h].bitcast(F32R), KT[:, ts_].bitcast(F32R), start=True, stop=True, skip_group_check=True), ["mBT%d%s" % (h, B), "KT"], ["ps0"])
                        P.op("pe", lambda e, h=h, hs=hs, ts_=ts_, KTm=KTm: e.matmul(pA[:, 256 + h * 128:384 + h * 128], KTm[h].bitcast(F32R), BT[:, ts_].bitcast(F32R), start=True, stop=True, skip_group_check=True), ["mKT%d%s" % (h, B), "BT"], ["ps0"])
                        P.op("pe", lambda e, h=h, hs=hs, ts_=ts_, KKm=KKm: e.matmul(pB[:, hs], KKm[h].bitcast(F32R), KT[:, ts_].bitcast(F32R), start=True, stop=True, skip_group_check=True), ["mKK%d%s" % (h, B), "KT"], ["ps1"])
                        P.op("pe", lambda e, h=h, hs=hs, ts_=ts_, BTm=BTm: e.matmul(pC[:, hs], BTm[h].bitcast(F32R), RT[:, ts_].bitcast(F32R), start=True, stop=True, skip_group_check=True), ["mBT%d%s" % (h, B), "RT"], ["ps2"])
                        P.op("pe", lambda e, h=h, hs=hs, ts_=ts_, KKm=KKm: e.matmul(pC[:, 256 + h * 128:384 + h * 128], KKm[h].bitcast(F32R), RT[:, ts_].bitcast(F32R), start=True, stop=True, skip_group_check=True), ["mKK%d%s" % (h, B), "RT"], ["ps2"])
                    yield
                    P.op("dve", lambda e: e.tensor_tensor(ATm, pA[:, 0:256], msk[:, 0, :], ALU.mult), ["ps0", "msk"], ["ATm"])
                    P.op("dve", lambda e: e.tensor_tensor(Am, pA[:, 256:512], msk[:, 1, :], ALU.mult), ["ps0", "msk"], ["Am"])
                    P.op("dve", lambda e, AkTm=AkTm: e.tensor_tensor(AkTm, pB[:, 0:256], msk[:, 0, :], ALU.mult), ["ps1", "msk"], ["AkTm" + B])
                    P.op("dve", lambda e, BrbTm=BrbTm: e.tensor_tensor(BrbTm, pC[:, 0:256], msk[:, 2, :], ALU.mult), ["ps2", "msk"], ["BrbTm" + B])
                    P.op("dve", lambda e, BrkTm=BrkTm: e.tensor_tensor(BrkTm, pC[:, 256:512], msk[:, 2, :], ALU.mult), ["ps2", "msk"], ["BrkTm" + B])
                    P.op("pool", lambda e: e.tensor_tensor(MTk[0], ATm, idn2, ALU.add), ["ATm", "idn2"], ["MT0"])
                    yield
                    curP, curPT, curM = Am, ATm, 0
                    for lev in range(5):
                        last = lev == 4
                        nP, nPT = Pk[lev % 2], PTk[lev % 2]
                        for h in range(2):
                            hs = slice(h * 128, (h + 1) * 128)
                            P.op("pe", lambda e, hs=hs, curP=curP, curPT=curPT: e.matmul(pA[:, hs], curPT[:, hs], curP[:, hs], start=True, stop=True, skip_group_check=True),
                                 ["Pc", "ATm", "Am"], ["ps0"])
                            if not last:
                                P.op("pe", lambda e, h=h, curP=curP, curPT=curPT, hs=hs: e.matmul(pA[:, 256 + h * 128:384 + h * 128], curP[:, hs], curPT[:, hs], start=True, stop=True, skip_group_check=True),
                                     ["Pc", "ATm", "Am"], ["ps0"])
                        yield
                        self.copy("act", nP, pA[:, 0:256], ["ps0"], ["Pc"])
                        if not last:
                            self.copy("dve", nPT, pA[:, 256:512], ["ps0"], ["Pc"])
                        yield
                        for h in range(2):
                            hs = slice(h * 128, (h + 1) * 128)
                            P.op("pe", lambda e, hs=hs, nP=nP, curM=curM: e.matmul(pB[:, hs], nP[:, hs], MTk[curM][:, hs], start=True, stop=True, skip_group_check=True),
                                 ["Pc", "MT%d" % curM], ["ps1"])
                        yield
                        if last:
                            P.op("dve", lambda e, curM=curM, bi=bi: e.tensor_tensor(MTfin[bi], MTk[curM], pB[:, 0:256], ALU.add), ["ps1", "MT%d" % curM], ["MTfin" + B])
                        else:
                            P.op("dve", lambda e, curM=curM: e.tensor_tensor(MTk[1 - curM], MTk[curM], pB[:, 0:256], ALU.add), ["ps1", "MT%d" % curM], ["MT%d" % (1 - curM)])
                        yield
                        curP, curPT, curM = nP, nPT, 1 - curM

                def back(sb, bi, hp=hp, t0=t0):
                    ts_ = slice(sb * 128, (sb + 1) * 128)
                    psm, pO = self.psb[3], self.psb[5 + sb % 2]
                    tO = "ps%d" % (5 + sb % 2)
                    tok, Vc, Vpad = tok2[bi], Vc2[bi], Vpad2[bi]
                    AkTm, BrbTm, BrkTm, MT = AkTm2[bi], BrbTm2[bi], BrkTm2[bi], MTfin[bi]
                    B = str(bi)
                    tM = "MTfin" + B
                    for h in range(2):
                        P.op("pe", lambda e, h=h, AkTm=AkTm, tok=tok: e.matmul(psm[:, h * 64:(h + 1) * 64], AkTm[:, h * 128:(h + 1) * 128], tok[2][:, h * 64:(h + 1) * 64], start=True, stop=True, skip_group_check=True),
                             ["AkTm" + B, "tok2" + B], ["ps3"])
                    yield
                    self.copy("act", WkV, psm[:, 0:128], ["ps3"], ["WkV"])
                    for h in range(2):
                        P.op("pe", lambda e, pO=pO, h=h, Vpad=Vpad, BrkTm=BrkTm: e.matmul(pO[:, 0:128], Vpad[h], BrkTm[:, h * 128:(h + 1) * 128], start=(h == 0), stop=False, skip_group_check=True),
                             ["Vpad%d%s" % (h, B), "BrkTm" + B], [tO])
                    yield
                    for c in range(2):
                        cs = slice(64 * c, 64 * c + 64)
                        tcs = slice(sb * 128 + 64 * c, sb * 128 + 64 * c + 64)
                        P.op("pe", lambda e, pO=pO, cs=cs, tcs=tcs: e.matmul(pO[:, cs], Pbd, RT[:, tcs], start=False, stop=False, skip_group_check=True), ["Pbd", "RT"], [tO])
                        P.op("pe", lambda e, ts_=ts_: e.matmul(psm[:, 128:256], KT[:, ts_], Pbd, start=True, stop=True, skip_group_check=True), ["Pbd", "KT"], ["ps3"])
                        yield
                        P.op("dve", lambda e, cs=cs: e.tensor_tensor(Wsb[cs, :], psm[cs, 128:256], WkV[cs, :], ALU.add), ["ps3", "WkV", "small"], ["Wsb"])
                        yield
                        for h in range(2):
                            P.op("pe", lambda e, h=h, MT=MT: e.matmul(psm[:, 256 + h * 64:320 + h * 64], MT[:, h * 128:(h + 1) * 128], Wsb[:, h * 64:(h + 1) * 64],
                                                                  start=True, stop=True, skip_group_check=True),
                                 [tM, "Wsb"], ["ps3"])
                        yield
                        for h in range(2):
                            self.copy("dve", Upad[c][h][cs, 64 * h:64 * h + 64], psm[cs, 256 + h * 64:320 + h * 64], ["ps3", "small"], ["Up%d%d" % (c, h)])
                        yield
                        for h in range(2):
                            P.op("pe", lambda e, pO=pO, h=h, c=c, BrbTm=BrbTm: e.matmul(pO[:, 0:128], Upad[c][h], BrbTm[:, h * 128:(h + 1) * 128], start=False, stop=(c == 1 and h == 1), skip_group_check=True),
                                 ["Up%d%d" % (c, h), "BrbTm" + B], [tO])
                        P.op("pe", lambda e, c=c, tok=tok: e.matmul(psm[:, 384:512], tok[0], Upad[c][0], start=True, stop=False, skip_group_check=True), ["tok0" + B, "Up%d0" % c], ["ps3"])
                        P.op("pe", lambda e, c=c, tok=tok: e.matmul(psm[:, 384:512], tok[0], Upad[c][1], start=False, stop=False, skip_group_check=True), ["tok0" + B, "Up%d1" % c], ["ps3"])
                        P.op("pe", lambda e, c=c, tok=tok, Vc=Vc: e.matmul(psm[:, 384:512], tok[1], Vc[c], start=False, stop=True, skip_group_check=True), ["tok1" + B, "Vc%d%s" % (c, B)], ["ps3"])
                        yield
                        gi = sb * 2 + c
                        P.op("dve", lambda e, gi=gi: e.scalar_tensor_tensor(Ptmp, psm[:, 384:512], gam[:, gi:gi + 1], blkm, ALU.mult, ALU.mult), ["ps3", "gam", "blkm"], ["Ptmp"])
                        P.op("dve", lambda e, gi=gi: e.scalar_tensor_tensor(Pbd, Pbd, gam[:, gi:gi + 1], Ptmp, ALU.mult, ALU.add), ["Pbd", "gam", "Ptmp"], ["Pbd"])
                        yield
                    self.copy("act", OTs[:, (sb % 4) * 128:(sb % 4 + 1) * 128], pO[:, 0:128], [tO], ["OTs"])
                    yield
                    if sb % 4 == 3:
                        gs = slice((sb // 4) * 512, (sb // 4 + 1) * 512)
                        pm, pq2 = self.psb[6 - sb % 2], self.psb[7]
                        tpm, tpq = "ps%d" % (6 - sb % 2), "ps7"
                        P.op("pool", lambda e: e.tensor_tensor(tq[0], OTs, OTs, ALU.mult), ["OTs", "tq0"], ["tq0"])
                        P.op("pe", lambda e, pm=pm: e.matmul(pm[:], blk, OTs, start=True, stop=True), ["blk", "OTs"], [tpm])
                        P.op("pe", lambda e, pq2=pq2: e.matmul(pq2[:], blk, tq[0], start=True, stop=True), ["blk", "tq0"], [tpq])
                        yield
                        P.op("act", lambda e, pm=pm: e.activation(out=tq[1], in_=pm[:], func=AF.Copy, scale=1.0 / 64.0), [tpm], ["tq1"])
                        P.op("dve", lambda e: e.tensor_tensor(tq[2], tq[1], tq[1], ALU.mult), ["tq1", "tq2"], ["tq2"])
                        P.op("dve", lambda e, pq2=pq2: e.scalar_tensor_tensor(tq[2], pq2[:], 1.0 / 64.0, tq[2], ALU.mult, ALU.subtract), [tpq, "tq2"], ["tq2"])
                        P.op("act", lambda e: e.activation(out=tq[2], in_=tq[2], func=AF.Sqrt, bias=self.gnepsT[:], scale=1.0), ["tq2", "gneps"], ["tq2"])
                        P.op("dve", lambda e: e.reciprocal(tq[2], tq[2]), ["tq2"], ["tq2"])
                        yield
                        P.op("pool", lambda e: e.tensor_tensor(tq[1], OTs, tq[1], ALU.subtract), ["OTs", "tq1"], ["tq1"])
                        P.op("dve", lambda e: e.tensor_tensor(tq[1], tq[1], tq[2], ALU.mult), ["tq1", "tq2"], ["tq1"])
                        P.op("dve", lambda e, hp=hp: e.tensor_scalar(tq[1], tq[1], col(GNG, hp), col(GNB, hp), ALU.mult, ALU.add), ["tq1", "rwp"], ["tq1"])
                        P.op("pool", lambda e, gs=gs: e.tensor_tensor(tq[1], tq[1], BV[:, gs], ALU.add), ["tq1", "BV"], ["tq1"])
                        P.op("dve", lambda e, gs=gs: e.tensor_tensor(tq[1], tq[1], GT[:, gs], ALU.mult), ["tq1", "GT"], ["tq1"])
                        self.out_rows(tq[1], 1536 + hp * 128, t0 + (sb // 4) * 512, 512, "tq1")

                def drive(gens):
                    gens = [g for g in gens if g is not None]
                    while gens:
                        for g in list(gens):
                            try:
                                next(g)
                            except StopIteration:
                                gens.remove(g)

                drive([front(0, 0)])
                for sb in range(NSB):
                    drive([back(sb, sb % 2), front(sb + 1, (sb + 1) % 2) if sb + 1 < NSB else None])

    def ln_store(self, yt, g_bc, b_bc, dst_rows, tagy):
        P = self.P
        st = self.AF(12288, 24).rearrange("p (a b) -> p a b", b=6)
        mv = self.AF(12288 + 24, 2)
        rs = self.AF(12288 + 26, 1)
        for c in range(4):
            P.op("dve", lambda e, c=c: e.bn_stats(st[:, c, :], yt[:, c * 512:(c + 1) * 512]), [tagy], ["lnst%d" % c])
        P.op("dve", lambda e: e.bn_aggr(mv, st), ["lnst%d" % c for c in range(4)], ["lnmv"])
        P.op("act", lambda e: e.activation(out=rs, in_=mv[:, 1:2], func=AF.Sqrt, bias=self.epsT[:], scale=1.0), ["lnmv", "epsT"], ["lnrs"])
        P.op("dve", lambda e: e.reciprocal(rs, rs), ["lnrs"], ["lnrs"])
        P.op("dve", lambda e: e.tensor_scalar(yt, yt, mv[:, 0:1], rs, ALU.subtract, ALU.mult), [tagy, "lnmv", "lnrs"], [tagy])
        P.op("dve", lambda e: e.tensor_tensor(yt, yt, g_bc, ALU.mult), [tagy, "lngb"], [tagy])
        P.op("dve", lambda e: e.tensor_tensor(yt, yt, b_bc, ALU.add), [tagy, "lngb"], [tagy])
        P.dma(dst_rows, yt, reads=[tagy])

    def back_to_tokens(self, yT, tagT, xsrc, t0, g_bc, b_bc, dst, tagdiv=1):
        P = self.P
        for i in range(4):
            xt = self.AF((i % 2) * 2048, 2048)
            yt = self.AF((2 + i % 2) * 2048, 2048)
            P.dma(xt, xsrc[t0 + i * 128:t0 + (i + 1) * 128, :], writes=["F%d" % (i % 2)], q="act")
            for q4 in range(4):
                ps = self.psb[6 + q4 % 2]
                for n in range(4):
                    k = q4 * 4 + n
                    P.op("pe", lambda e, ps=ps, n=n, k=k, i=i: e.transpose(ps[:, n * 128:(n + 1) * 128], yT[:, k, i * 128:(i + 1) * 128], self.ident[:]),
                         [tagT + str(k // tagdiv), "ident"], ["psy%d" % (q4 % 2)])
                P.op("dve", lambda e, ps=ps, q4=q4, xt=xt, yt=yt: e.scalar_tensor_tensor(yt[:, q4 * 512:(q4 + 1) * 512], xt[:, q4 * 512:(q4 + 1) * 512], ALPHA, ps[:],
                                                                                  ALU.mult, ALU.add),
                     ["psy%d" % (q4 % 2), "F%d" % (i % 2)], ["F%d" % (2 + i % 2)])
            self.ln_store(yt, g_bc, b_bc, dst[t0 + i * 128:t0 + (i + 1) * 128, :], "F%d" % (2 + i % 2))

    def load_gb(self, g, b, l):
        P = self.P
        g_bc = self.AF(8192, 2048)
        b_bc = self.AF(8192 + 2048, 2048)
        P.dma(g_bc, g[l:l + 1, :].to_broadcast([128, D]), writes=["lngb"])
        P.dma(b_bc, b[l:l + 1, :].to_broadcast([128, D]), writes=["lngb"])
        return g_bc, b_bc

    def stage_wout(self, l, xin):
        P, S = self.P, self.S
        off_c = 0
        off_m = off_c + 16384
        off_w = off_m + 8192
        g_bc, b_bc = self.load_gb(self.ln1_g, self.ln1_b, l)
        for tb in range(S // TB):
            t0 = tb * TB
            cT = self.A(off_c + (tb % 2) * 8192, 8192).rearrange("p (a b) -> p a b", b=512)
            ctag = "cT%d" % (tb % 2)
            P.dma(cT.bitcast(F32R), self.catT[:, t0:t0 + TB].rearrange("(a p) t -> p a t", p=128).bitcast(F32R), writes=[ctag], q="pool")
            mT = self.A(off_m, 8192).rearrange("p (a b) -> p a b", b=512)
            for j in range(16):
                wi = (tb * 16 + j) % 3
                wt = self.A(off_w + wi * 2048, 2048)
                P.dma(wt.bitcast(F32R), self.w_out[l, j].bitcast(F32R), writes=["w%d" % wi], q="pool")
                ps = self.psb[2 + j % 4]
                for a in range(16):
                    P.op("pe", lambda e, ps=ps, wt=wt, a=a, cT=cT: e.matmul(ps[:], wt[:, a * 128:(a + 1) * 128].bitcast(F32R), cT[:, a, :].bitcast(F32R),
                                                                        start=(a == 0), stop=(a == 15)),
                         ["w%d" % wi, ctag], ["psm%d" % (j % 4)])
                P.op("act", lambda e, ps=ps, j=j: e.activation(out=mT[:, j, :].bitcast(F32R), in_=ps[:], func=AF.Copy, scale=self.modT[:, 32 + j:33 + j]),
                     ["psm%d" % (j % 4), "modT"], ["mT%d" % j])
            self.back_to_tokens(mT, "mT", xin, t0, g_bc, b_bc, self.xB)

    def stage_ffn(self, l, xout):
        P, S = self.P, self.S
        off_yacc = 0
        off_h = 8192
        off_g = off_h + 8192
        off_w = off_g + 11264
        off_w2 = off_w + 6144
        assert off_w2 + 2 * 2816 <= ARENA_R
        g_bc, b_bc = self.load_gb(self.ln2_g, self.ln2_b, l)
        wc = 0
        for tb in range(S // TB):
            t0 = tb * TB
            hT = self.A(off_h, 8192).rearrange("p (a b) -> p a b", b=512)
            self.load_xT(self.xB, t0, hT, 3, 4, "fh")
            yacc = self.A(off_yacc, 8192).rearrange("p (a b) -> p a b", b=512)
            gT = self.A(off_g, 11264).rearrange("p (a b) -> p a b", b=512)
            for half in range(2):
                for fl in range(22):
                    f = half * 22 + fl
                    pss = []
                    for wi_, wsrc in enumerate((self.w1, self.w3)):
                        wi = wc % 3
                        wc += 1
                        wt = self.A(off_w + wi * 2048, 2048)
                        P.dma(wt.bitcast(F32R), wsrc[l, f].bitcast(F32R), writes=["w%d" % wi], q="pool")
                        ps = self.psb[2 + (2 * fl + wi_) % 4]
                        ptag = "psm%d" % ((2 * fl + wi_) % 4)
                        for a in range(16):
                            P.op("pe", lambda e, ps=ps, wt=wt, a=a: e.matmul(ps[:], wt[:, a * 128:(a + 1) * 128].bitcast(F32R), hT[:, a, :].bitcast(F32R),
                                                                         start=(a == 0), stop=(a == 15)),
                                 ["w%d" % wi, "fh%d" % a], [ptag])
                        pss.append((ps, ptag))
                    sl = self.A(off_w2 + 2 * 2816 - 512, 512) if False else None
                    gt = gT[:, fl, :]
                    P.op("act", lambda e, gt=gt, ps=pss[0][0]: e.activation(out=gt.bitcast(F32R), in_=ps[:], func=AF.Silu), [pss[0][1]], ["g%d" % fl])
                    P.op("dve", lambda e, gt=gt, ps=pss[1][0]: e.tensor_tensor(gt.bitcast(F32R), gt, ps[:], ALU.mult), [pss[1][1], "g%d" % fl], ["g%d" % fl])
                for j in range(16):
                    w2i = (half * 16 + j) % 2
                    w2t = self.A(off_w2 + w2i * 2816, 2816)
                    P.dma(w2t.bitcast(F32R), self.w2[l, j, :, half * 2816:(half + 1) * 2816].bitcast(F32R), writes=["w2_%d" % w2i], q="pool")
                    ps = self.psb[j % 2]
                    ptag = "pst%d" % (j % 2)
                    for fl in range(22):
                        P.op("pe", lambda e, ps=ps, w2t=w2t, fl=fl: e.matmul(ps[:], w2t[:, fl * 128:(fl + 1) * 128].bitcast(F32R), gT[:, fl, :].bitcast(F32R),
                                                                         start=(fl == 0), stop=(fl == 21)),
                             ["w2_%d" % w2i, "g%d" % fl], [ptag])
                    if half == 0:
                        P.op("act", lambda e, ps=ps, j=j: e.activation(out=yacc[:, j, :].bitcast(F32R), in_=ps[:], func=AF.Copy, scale=self.modT[:, 80 + j:81 + j]),
                             [ptag, "modT"], ["R%d" % (j // 4)])
                    else:
                        P.op("dve", lambda e, ps=ps, j=j: e.scalar_tensor_tensor(yacc[:, j, :].bitcast(F32R), ps[:], self.modT[:, 80 + j:81 + j], yacc[:, j, :], ALU.mult, ALU.add),
                             [ptag, "R%d" % (j // 4), "modT"], ["R%d" % (j // 4)])
            self.back_to_tokens(yacc, "R", self.xB, t0, g_bc, b_bc, xout, tagdiv=4)


def prep_common(inp, L):
    f = lambda a: np.ascontiguousarray(np.asarray(a, dtype=np.float32))
    m = {}
    m["ada_w"] = np.stack([wtile(f(inp["ada_w"][l])).reshape(96, 128, 2048) for l in range(L)])
    m["ada_bT"] = vecT(f(inp["ada_b"][:L]))
    m["w_in"] = np.stack([wtile(f(inp["w_in"][l])).reshape(46, 128, 2048) for l in range(L)])
    m["w_out"] = np.stack([wtile(f(inp["w_out"][l])).reshape(16, 128, 2048) for l in range(L)])
    m["w1"] = np.stack([wtile(f(inp["ffn_w1"][l])).reshape(44, 128, 2048) for l in range(L)])
    m["w3"] = np.stack([wtile(f(inp["ffn_w3"][l])).reshape(44, 128, 2048) for l in range(L)])
    m["w2"] = np.stack([wtile(f(inp["ffn_w2"][l])).reshape(16, 128, 44 * 128) for l in range(L)])
    for k in ("ln1_g", "ln1_b", "ln2_g", "ln2_b"):
        m[k] = f(inp[k][:L])
    m["ident"] = np.eye(128, dtype=np.float32)
    sp_, tp_ = np.arange(128)[:, None], np.arange(512)[None, :]
    m["sbmask"] = np.stack([(tp_ > mm * 128 + sp_) for mm in range(4)]).astype(np.float32)
    m["tri"] = (np.arange(128)[:, None] >= np.arange(128)[None, :]).astype(np.float32)
    m["ones"] = np.ones((128, 128), np.float32)
    a_ = np.arange(128)
    m["m32"] = ((a_[:, None] // 32 == a_[None, :] // 32) & (a_[:, None] <= a_[None, :])).astype(np.float32)
    m["rst"] = np.broadcast_to((np.arange(4096) % 32 != 0).astype(np.float32), (128, 4096)).copy()
    m["hg_lgT"] = np.ascontiguousarray(f(inp["hg_lb_logits"]).reshape(4, 4, 128).transpose(2, 0, 1).reshape(128, 16))
    m["s5_dT"] = np.ascontiguousarray(f(inp["s5_d"]).reshape(4, 4, 128).transpose(2, 0, 1).reshape(128, 16))
    m["s5_gbT"] = np.ascontiguousarray(f(inp["s5_glu_b"]).reshape(4, 4, 128).transpose(2, 0, 1).reshape(128, 16))
    m["s5_arT"] = vecT(f(inp["s5_a_re"][:L]).reshape(L, 2048))
    m["s5_aiT"] = vecT(f(inp["s5_a_im"][:L]).reshape(L, 2048))
    m["s5_ldtT"] = vecT(np.repeat(f(inp["s5_log_dt"][:L]), 64, axis=1))
    wb = np.zeros((L, 128, 2, 16, 128), np.float32)
    wcre = np.zeros((L, 128, 16, 128), np.float32)
    wcim = np.zeros((L, 128, 16, 128), np.float32)
    for g in range(32):
        a, gb = g // 2, g % 2
        r0 = 32 * (a % 4) + 16 * gb
        for ci, key in enumerate(("s5_b_re", "s5_b_im")):
            wb[:, r0:r0 + 16, ci, a, gb * 64:(gb + 1) * 64] = f(inp[key][:L, g]).transpose(0, 2, 1)
        c0 = 32 * (a % 4) + 16 * gb
        wcre[:, gb * 64:(gb + 1) * 64, a, c0:c0 + 16] = f(inp["s5_c_re"][:L, g]).transpose(0, 2, 1)
        wcim[:, gb * 64:(gb + 1) * 64, a, c0:c0 + 16] = f(inp["s5_c_im"][:L, g]).transpose(0, 2, 1)
    m["s5_wb"] = wb.reshape(L, 128, 4096)
    m["s5_wcre"] = wcre.reshape(L, 128, 2048)
    m["s5_wcim"] = wcim.reshape(L, 128, 2048)
    m["s5_glu"] = np.stack([wtile(f(inp["s5_glu_w"][l])).reshape(4, 128, 512) for l in range(L)])
    m["hm"] = np.stack([(a_ < 64), (a_ >= 64)], axis=1).astype(np.float32)
    m["blk64"] = (a_[:, None] // 64 == a_[None, :] // 64).astype(np.float32)
    m["rst64"] = np.broadcast_to((np.arange(2048) % 64 != 0).astype(np.float32), (128, 2048)).copy()
    same = a_[:, None] // 64 == a_[None, :] // 64
    m0 = (same & (a_[:, None] < a_[None, :])).astype(np.float32)
    m1 = (same & (a_[:, None] > a_[None, :])).astype(np.float32)
    m2 = (same & (a_[:, None] <= a_[None, :])).astype(np.float32)
    m["rwmask"] = np.stack([np.concatenate([mm, mm], axis=1) for mm in (m0, m1, m2)])
    m["rw_muT"] = vecT(f(inp["rw_mu"][:L]))
    for nm, key in (("rw_w0T", "rw_w0"), ("rw_a0T", "rw_a0"), ("rw_kkT", "rw_k_k"), ("rw_kaT", "rw_k_a"), ("rw_gngT", "rw_gn_g"), ("rw_gnbT", "rw_gn_b")):
        m[nm] = vecT(f(inp[key][:L]))
    m["rw_rkT"] = vecT(f(inp["rw_r_k"][:L]).reshape(L, 512))
    w2p = np.zeros((L, 128, 512), np.float32); w2p[:, 0:64] = f(inp["rw_w2"][:L])
    a2p = np.zeros((L, 128, 512), np.float32); a2p[:, 64:128] = f(inp["rw_a2"][:L])
    m["rw_w2p"], m["rw_a2p"], m["rw_g2"] = w2p, a2p, f(inp["rw_g2"][:L])
    m["hg_ngT"] = np.ascontiguousarray(f(inp["hg_norm_g"]).reshape(4, 4, 128).transpose(2, 0, 1).reshape(128, 16))
    return m


def run(inp, S=4096, L=DEPTH, ncores=8, **bk):
    b = Builder(S, L, **bk)
    nc = b.build()
    common = prep_common(inp, L)
    in_maps = []
    for c in range(ncores):
        m = dict(common)
        m["x"] = np.ascontiguousarray(np.asarray(inp["x"][c, :S], dtype=np.float32))
        m["cT"] = vecT(np.asarray(inp["c"][c], dtype=np.float32))
        in_maps.append(m)
    res = run_bass_kernel_spmd(nc, in_maps, core_ids=list(range(ncores)))
    b.results = res.results
    return np.stack([np.asarray(r["out"]) for r in res.results]).astype(np.float32), b


def kernel(**inputs):
    out, _ = run(inputs)
    return out
```
